# Optimizing a Trainium2 kernel written in Bass

```python
import math
import jax, jax.numpy as jnp
from jax import lax
import numpy as np

D_MODEL = 1024
BATCH = 8
SEQ = 4096
DEPTH = 4

SSM_WIDTH = 256
POOL_WIDTH = 256
ATTN_WIDTH = 512
MIX_WIDTH = SSM_WIDTH + POOL_WIDTH + ATTN_WIDTH
SSM_GROUP = 16
SSM_GROUPS = SSM_WIDTH // SSM_GROUP
SSM_STATE = 64
DT_MIN = 1e-3
DT_MAX = 1e-1
POOL_WINDOWS = (2, 4, 8, 16)
POOL_GROUP = POOL_WIDTH // len(POOL_WINDOWS)
HEAD_DIM = 64
N_HEADS = ATTN_WIDTH // HEAD_DIM
N_KV_HEADS = 2
KV_WIDTH = N_KV_HEADS * HEAD_DIM
Q_PER_KV = N_HEADS // N_KV_HEADS
WINDOW = 128
BLOCK = 128
RMS_EPS = 1e-6
IN_WIDTHS = (SSM_WIDTH, SSM_WIDTH, POOL_WIDTH, POOL_WIDTH, ATTN_WIDTH, KV_WIDTH, KV_WIDTH, ATTN_WIDTH)
IN_WIDTH = sum(IN_WIDTHS)
IN_SPLITS = [int(c) for c in np.cumsum(IN_WIDTHS)[:-1]]

kernel_name = "hybrid_s5_pool_swa_encoder"

F32 = jnp.float32


def rms_norm(x, g):
    xf = x.astype(F32)
    y = xf * lax.rsqrt(jnp.mean(xf * xf, axis=-1, keepdims=True) + RMS_EPS)
    return (y * g.astype(F32)).astype(x.dtype)


def _complex_linear_combine(left, right):
    a_re1, a_im1, b_re1, b_im1 = left
    a_re2, a_im2, b_re2, b_im2 = right
    a_re = a_re1 * a_re2 - a_im1 * a_im2
    a_im = a_re1 * a_im2 + a_im1 * a_re2
    b_re = a_re2 * b_re1 - a_im2 * b_im1 + b_re2
    b_im = a_re2 * b_im1 + a_im2 * b_re1 + b_im2
    return a_re, a_im, b_re, b_im


def ssm_direction(u, a_re, a_im, log_dt, b_re, b_im, c_re, c_im, reverse):
    a_re = a_re.astype(F32)
    a_im = a_im.astype(F32)
    dt = jnp.exp(log_dt.astype(F32))[:, None]
    mag = jnp.exp(a_re * dt)
    lb_re = mag * jnp.cos(a_im * dt)
    lb_im = mag * jnp.sin(a_im * dt)
    den = a_re * a_re + a_im * a_im
    num_re = lb_re - 1.0
    coef_re = (num_re * a_re + lb_im * a_im) / den
    coef_im = (lb_im * a_re - num_re * a_im) / den
    b_re = b_re.astype(F32)
    b_im = b_im.astype(F32)
    bb_re = coef_re[..., None] * b_re - coef_im[..., None] * b_im
    bb_im = coef_re[..., None] * b_im + coef_im[..., None] * b_re
    bu_re = jnp.einsum('blgp,gnp->blgn', u, bb_re)
    bu_im = jnp.einsum('blgp,gnp->blgn', u, bb_im)
    lam_re = jnp.broadcast_to(lb_re, bu_re.shape)
    lam_im = jnp.broadcast_to(lb_im, bu_im.shape)
    _, _, h_re, h_im = lax.associative_scan(
        _complex_linear_combine, (lam_re, lam_im, bu_re, bu_im), reverse=reverse, axis=1)
    return (jnp.einsum('blgn,gpn->blgp', h_re, c_re.astype(F32))
            - jnp.einsum('blgn,gpn->blgp', h_im, c_im.astype(F32)))


def ssm_branch(u, a_re, a_im, log_dt, b_re, b_im, c_re, c_im, d, glu_w, glu_b):
    bsz, seq, _ = u.shape
    uf = u.astype(F32)
    ug = uf.reshape(bsz, seq, SSM_GROUPS, SSM_GROUP)
    y = (ssm_direction(ug, a_re[0], a_im[0], log_dt[0], b_re[0], b_im[0], c_re[0], c_im[0], False)
         + ssm_direction(ug, a_re[1], a_im[1], log_dt[1], b_re[1], b_im[1], c_re[1], c_im[1], True))
    y = y.reshape(bsz, seq, SSM_WIDTH) + d.astype(F32) * uf
    y = jax.nn.gelu(y)
    z = y @ glu_w.astype(F32) + glu_b.astype(F32)
    za, zb = jnp.split(z, 2, axis=-1)
    return (za * jax.nn.sigmoid(zb)).astype(u.dtype)


def pool_branch(p, pool_w, pool_scale):
    bsz, seq, _ = p.shape
    pf = p.astype(F32)
    cs = jnp.pad(jnp.cumsum(pf, axis=1), ((0, 0), (1, 0), (0, 0)))
    pos = jnp.arange(seq)
    outs = []
    for gi, w in enumerate(POOL_WINDOWS):
        lo = jnp.clip(pos - w // 2, 0, seq)
        hi = jnp.clip(pos + w // 2, 0, seq)
        cs_g = cs[:, :, gi * POOL_GROUP:(gi + 1) * POOL_GROUP]
        mean = (cs_g[:, hi] - cs_g[:, lo]) / (hi - lo).astype(F32)[None, :, None]
        outs.append(mean - pf[:, :, gi * POOL_GROUP:(gi + 1) * POOL_GROUP])
    mixed = jnp.stack(outs, axis=2)
    y = jnp.einsum('blgc,gcd->blgd', mixed, pool_w.astype(F32)).reshape(bsz, seq, POOL_WIDTH)
    return (y * pool_scale.astype(F32)).astype(p.dtype)


def alibi_slopes():
    return jnp.exp2(-8.0 * jnp.arange(1, N_HEADS + 1, dtype=F32) / N_HEADS)


def window_attention(q, k, v, sink):
    bsz, seq, _ = q.shape
    nb = seq // BLOCK
    qf = q.astype(F32).reshape(bsz, nb, BLOCK, N_KV_HEADS, Q_PER_KV, HEAD_DIM) * (HEAD_DIM ** -0.5)

    def band(t):
        t = t.astype(F32).reshape(bsz, nb, BLOCK, N_KV_HEADS, HEAD_DIM)
        tp = jnp.pad(t, ((0, 0), (1, 1), (0, 0), (0, 0), (0, 0)))
        return jnp.concatenate([tp[:, :-2], tp[:, 1:-1], tp[:, 2:]], axis=2)

    kb = band(k)
    vb = band(v)
    scores = jnp.einsum('bnqkgd,bnskd->bnkgqs', qf, kb)
    blk = jnp.arange(nb)[:, None]
    qpos = blk * BLOCK + jnp.arange(BLOCK)[None, :]
    kpos = (blk - 1) * BLOCK + jnp.arange(3 * BLOCK)[None, :]
    dist = jnp.abs(qpos[:, :, None] - kpos[:, None, :])
    valid = (dist <= WINDOW) & (kpos[:, None, :] >= 0) & (kpos[:, None, :] < seq)
    slopes = alibi_slopes().reshape(N_KV_HEADS, Q_PER_KV)
    bias = -slopes[None, :, :, None, None] * dist.astype(F32)[:, None, None]
    scores = jnp.where(valid[None, :, None, None], scores + bias[None], -jnp.inf)
    sink_b = sink.astype(F32).reshape(1, 1, N_KV_HEADS, Q_PER_KV, 1, 1)
    m = jnp.maximum(jnp.max(scores, axis=-1, keepdims=True), sink_b)
    pr = jnp.exp(scores - m)
    pr = pr / (jnp.sum(pr, axis=-1, keepdims=True) + jnp.exp(sink_b - m))
    out = jnp.einsum('bnkgqs,bnskd->bnqkgd', pr, vb)
    return out.reshape(bsz, seq, ATTN_WIDTH).astype(q.dtype)


def hybrid_layer(x, pre_g, w_in, a_re, a_im, log_dt, b_re, b_im, c_re, c_im, d, glu_w, glu_b,
                 pool_w, pool_scale, sink, w_out, post_g):
    h = rms_norm(x, pre_g)
    proj = h @ w_in
    su, sg, pu, pg, q, k, v, ag = jnp.split(proj, IN_SPLITS, axis=-1)
    y_ssm = ssm_branch(su, a_re, a_im, log_dt, b_re, b_im, c_re, c_im, d, glu_w, glu_b) * jax.nn.silu(sg)
    y_pool = pool_branch(pu, pool_w, pool_scale) * jax.nn.silu(pg)
    y_attn = window_attention(q, k, v, sink) * jax.nn.silu(ag)
    y = jnp.concatenate([y_ssm, y_pool, y_attn], axis=-1) @ w_out
    return x + rms_norm(y, post_g)


def setup_inputs(seed: int = 0) -> dict:
    key = jax.random.key(seed)
    ks = jax.random.split(key, 20)
    nrm = lambda k, shape, s: jax.random.normal(k, shape, F32) * s
    n_idx = jnp.arange(SSM_STATE, dtype=F32)
    a_re = -0.5 + nrm(ks[3], (DEPTH, 2, SSM_GROUPS, SSM_STATE), 0.01)
    a_im = math.pi * n_idx + nrm(ks[4], (DEPTH, 2, SSM_GROUPS, SSM_STATE), 0.01)
    log_dt = jax.random.uniform(ks[5], (DEPTH, 2, SSM_GROUPS), F32,
                                math.log(DT_MIN), math.log(DT_MAX))
    return {
        "x": nrm(ks[0], (BATCH, SEQ, D_MODEL), 1.0),
        "pre_norm_g": 1.0 + nrm(ks[1], (DEPTH, D_MODEL), 0.05),
        "w_in": nrm(ks[2], (DEPTH, D_MODEL, IN_WIDTH), D_MODEL ** -0.5),
        "ssm_a_re": a_re,
        "ssm_a_im": a_im,
        "ssm_log_dt": log_dt,
        "ssm_b_re": nrm(ks[6], (DEPTH, 2, SSM_GROUPS, SSM_STATE, SSM_GROUP), (2 * SSM_GROUP) ** -0.5),
        "ssm_b_im": nrm(ks[7], (DEPTH, 2, SSM_GROUPS, SSM_STATE, SSM_GROUP), (2 * SSM_GROUP) ** -0.5),
        "ssm_c_re": nrm(ks[8], (DEPTH, 2, SSM_GROUPS, SSM_GROUP, SSM_STATE), SSM_STATE ** -0.5),
        "ssm_c_im": nrm(ks[9], (DEPTH, 2, SSM_GROUPS, SSM_GROUP, SSM_STATE), SSM_STATE ** -0.5),
        "ssm_d": nrm(ks[10], (DEPTH, SSM_WIDTH), 0.5),
        "ssm_glu_w": nrm(ks[11], (DEPTH, SSM_WIDTH, 2 * SSM_WIDTH), SSM_WIDTH ** -0.5),
        "ssm_glu_b": nrm(ks[12], (DEPTH, 2 * SSM_WIDTH), 0.01),
        "pool_w": nrm(ks[13], (DEPTH, len(POOL_WINDOWS), POOL_GROUP, POOL_GROUP), POOL_GROUP ** -0.5),
        "pool_scale": 0.5 + nrm(ks[14], (DEPTH, POOL_WIDTH), 0.1),
        "attn_sink": nrm(ks[15], (DEPTH, N_HEADS), 0.5),
        "w_out": nrm(ks[16], (DEPTH, MIX_WIDTH, D_MODEL), MIX_WIDTH ** -0.5),
        "post_norm_g": 1.0 + nrm(ks[17], (DEPTH, D_MODEL), 0.05),
    }


def reference(x, pre_norm_g, w_in, ssm_a_re, ssm_a_im, ssm_log_dt, ssm_b_re, ssm_b_im, ssm_c_re,
              ssm_c_im, ssm_d, ssm_glu_w, ssm_glu_b, pool_w, pool_scale, attn_sink, w_out, post_norm_g):
    for l in range(DEPTH):
        x = hybrid_layer(x, pre_norm_g[l], w_in[l], ssm_a_re[l], ssm_a_im[l], ssm_log_dt[l],
                         ssm_b_re[l], ssm_b_im[l], ssm_c_re[l], ssm_c_im[l], ssm_d[l],
                         ssm_glu_w[l], ssm_glu_b[l], pool_w[l], pool_scale[l], attn_sink[l],
                         w_out[l], post_norm_g[l])
    return x
```

```python
import math
from contextlib import ExitStack

import numpy as np
import concourse.bass as bass
import concourse.mybir as mybir
from concourse.bass_utils import run_bass_kernel_spmd

F32 = mybir.dt.float32
BF16 = mybir.dt.bfloat16
I32 = mybir.dt.int32
ALU = mybir.AluOpType
AF = mybir.ActivationFunctionType

D = 1024
SEQ = 4096
DEPTH = 4
W = 256
NS = SEQ // W
NSM = 93
TCH = 4
NCH = W // TCH
EPS = 1e-6
TWO_PI = 2.0 * math.pi
SC2 = TWO_PI * (1.0 - 2e-6)
SC1 = math.pi * (1.0 - 2e-6)

C_GPRE, C_GPOST, C_SD, C_GB, C_PSC, C_SINK, C_LDTR, C_LDTS, C_ARS, C_AIS, C_MASK, C_INVW = (
    0, 8, 16, 18, 22, 24, 28, 32, 48, 64, 80, 88)
C_MASK2, C_MASK3 = 90, 92


class Prog:
    ENGS = ["sp", "pe", "act", "dve", "pool"]
    SAME_SKIP = {"pe"}

    def __init__(self):
        self.ops = []
        self.lastw = {}
        self.readers = {}
        self.dmacnt = {}

    def add(self, eng, fn, r=(), w=(), dma=None, force=False):
        i = len(self.ops)
        deps = set()
        for k in r:
            if k in self.lastw:
                deps.add(self.lastw[k])
        for k in w:
            if k in self.lastw:
                deps.add(self.lastw[k])
            deps.update(self.readers.get(k, ()))
        for k in r:
            self.readers.setdefault(k, []).append(i)
        for k in w:
            self.lastw[k] = i
            self.readers[k] = []
        op = dict(eng=eng, fn=fn, deps=deps, dma=dma, needed=False, force=force)
        if dma is not None:
            self.dmacnt[dma] = self.dmacnt.get(dma, 0) + 16
            op["dcount"] = self.dmacnt[dma]
        self.ops.append(op)
        return i

    def pe(self, fn, r=(), w=()):
        return self.add("pe", fn, r, w)

    def act(self, fn, r=(), w=()):
        return self.add("act", fn, r, w)

    def dve(self, fn, r=(), w=()):
        return self.add("dve", fn, r, w)

    def pool(self, fn, r=(), w=()):
        return self.add("pool", fn, r, w)

    def dma(self, fn, key, r=(), w=()):
        return self.add("sp", fn, r, w, dma=key)


    def mm(self, out, lhsT, rhs, start, stop, r=(), w=(), force=False):
        return self.add("pe", lambda e: e.matmul(out, lhsT=lhsT, rhs=rhs, start=start, stop=stop), r, w, force=force)

    def actf(self, out, in_, func, r=(), w=(), scale=None, bias=None):
        kw = {}
        if scale is not None:
            kw["scale"] = scale
        if bias is not None:
            kw["bias"] = bias
        return self.add("act", lambda e: e.activation(out=out, in_=in_, func=func, **kw), r, w)

    def amul(self, out, in_, mul, r=(), w=()):
        return self.add("act", lambda e: e.mul(out, in_, mul), r, w)

    def acopy(self, out, in_, r=(), w=()):
        return self.add("act", lambda e: e.copy(out, in_), r, w)

    def tt(self, eng, out, in0, in1, op, r=(), w=()):
        return self.add(eng, lambda e: e.tensor_tensor(out=out, in0=in0, in1=in1, op=op), r, w)

    def ts(self, eng, out, in0, s1, s2, op0, op1=None, r=(), w=()):
        if op1 is None:
            return self.add(eng, lambda e: e.tensor_scalar(out=out, in0=in0, scalar1=s1, scalar2=None, op0=op0), r, w)
        return self.add(eng, lambda e: e.tensor_scalar(out=out, in0=in0, scalar1=s1, scalar2=s2, op0=op0, op1=op1), r, w)

    def stt(self, eng, out, in0, scalar, in1, op0, op1, r=(), w=()):
        return self.add(eng, lambda e: e.scalar_tensor_tensor(out=out, in0=in0, scalar=scalar, in1=in1, op0=op0, op1=op1), r, w)

    def scan(self, out, d0, d1, init, r=(), w=()):
        return self.add("dve", lambda e: e.tensor_tensor_scan(out=out, data0=d0, data1=d1, initial=init,
                                                              op0=ALU.mult, op1=ALU.add), r, w)

    def recip(self, out, in_, r=(), w=()):
        return self.add("dve", lambda e: e.reciprocal(out=out, in_=in_), r, w)

    def copy(self, eng, out, in_, r=(), w=()):
        return self.add(eng, lambda e: e.tensor_copy(out=out, in_=in_), r, w)

    def memset(self, eng, ap, val, w=()):
        return self.add(eng, lambda e: e.memset(ap, val), (), w)

    def dmas(self, out, in_, key, r=(), w=(), q="sp"):
        return self.add(q, lambda e: e.dma_start(out=out, in_=in_), r, w, dma=key)

    def _skip(self, op, dop):
        if op["force"]:
            return False
        return dop["dma"] is None and dop["eng"] == op["eng"] and op["eng"] in self.SAME_SKIP

    def emit(self, nc, es):
        ops = self.ops
        for op in ops:
            for d in op["deps"]:
                dop = ops[d]
                if dop["dma"] is None and not self._skip(op, dop):
                    dop["needed"] = True
        cnt = {e: 0 for e in self.ENGS}
        for op in ops:
            if op["dma"] is None and op["needed"]:
                cnt[op["eng"]] += 1
                op["sig"] = cnt[op["eng"]]
        csem = {e: es.enter_context(nc.semaphore("c_" + e)) for e in ["pe", "act", "dve", "pool"]}
        dsem = {k: es.enter_context(nc.semaphore("d_%d" % i)) for i, k in enumerate(self.dmacnt)}
        block = es.enter_context(nc.Block())
        regs = {"sp": block.sync, "pe": block.tensor, "act": block.scalar, "dve": block.vector,
                "pool": block.gpsimd}
        for eng in self.ENGS:
            def body(e, eng=eng):
                waited = {}
                for op in ops:
                    if op["eng"] != eng:
                        continue
                    for d in sorted(op["deps"]):
                        dop = ops[d]
                        if dop["dma"] is not None:
                            key, val, sem = ("d", dop["dma"]), dop["dcount"], dsem[dop["dma"]]
                        else:
                            if self._skip(op, dop):
                                continue
                            key, val, sem = ("c", dop["eng"]), dop["sig"], csem[dop["eng"]]
                        if waited.get(key, 0) >= val:
                            continue
                        e.wait_ge(sem, val)
                        waited[key] = val
                    ins = op["fn"](e)
                    if op["dma"] is not None:
                        ins.then_inc(dsem[op["dma"]], 16)
                    elif op["needed"]:
                        ins.then_inc(csem[eng], 1)
                if eng == "sp":
                    for k, v in self.dmacnt.items():
                        e.wait_ge(dsem[k], v)
            regs[eng](body)


class Rec:
    def __init__(self):
        self.calls = []

    def __getattr__(self, name):
        def f(*a, **k):
            self.calls.append((name, a, k))
        return f

    def thunks(self, prog):
        return [lambda n=n, a=a, k=k: getattr(prog, n)(*a, **k) for (n, a, k) in self.calls]


def build(L, from_out_first=False):
    nc = bass.Bass("TRN2", target_bir_lowering=False)
    dr = lambda name, shape, dt=F32, kind="ExternalInput": nc.dram_tensor(name, shape, dt, kind=kind).ap()
    xT = dr("xT", [D, SEQ])
    w1 = dr("w1", [L, D, 896])
    w2 = dr("w2", [L, D, 1536])
    wo = dr("wo", [L, D, D])
    gluw = dr("gluw", [L, 256, 512])
    poolw = dr("poolw", [L, 128, 256])
    psmall = dr("psmall", [L, 128, NSM])
    pchan = dr("pchan", [L, 128, 1024])
    cpad = dr("cpad", [L, 2, 128, 2048])
    cdist = dr("cdist", [128, 384])
    cvalid = dr("cvalid", [128, 384])
    ciota = dr("ciota", [128, W])
    cratio = dr("cratio", [128, 32])
    cident = dr("cident", [128, 128])
    out = dr("out", [D, SEQ], kind="ExternalOutput")
    skind = "ExternalOutput" if DBG.get("dump") else "Internal"
    susc = nc.dram_tensor("susc", [128, 2, SEQ], BF16, kind=skind).ap()
    ktsc = nc.dram_tensor("ktsc", [128, 2, SEQ], BF16, kind=skind).ap()
    pusc = nc.dram_tensor("pusc", [128, 2, SEQ + 16], F32, kind=skind).ap()
    vsc = nc.dram_tensor("vsc", [128, 32, 128], BF16, kind=skind).ap()
    yfsc = nc.dram_tensor("yfsc", [128, 2, SEQ], F32, kind=skind).ap()

    if DBG.get("dump"):
        dbg32 = nc.dram_tensor("dbg32", [128, 96], F32, kind="ExternalOutput").ap()
        dbgb = nc.dram_tensor("dbgb", [128, 5, 2048], BF16, kind="ExternalOutput").ap()
    xT_v = xT.rearrange("(kc p) t -> p kc t", p=128)
    out_v = out.rearrange("(kc p) t -> p kc t", p=128)

    es = ExitStack()
    with es:
        sb = lambda name, shape, dt=F32: es.enter_context(nc.sbuf_tensor(name, shape, dt))
        w1b = sb("w1b", [128, 8, 896], BF16)
        w2b = sb("w2b", [128, 8, 1536], BF16)
        wob = sb("wob", [128, 8, 1024], BF16)
        glub = sb("glub", [128, 2, 512], BF16)
        plwb = sb("plwb", [128, 2, 128], BF16)
        WS = sb("WS", [128, 2, 4, 2, 128], BF16)
        WS3 = sb("WS3", [128, 2, 4, 2, 128], BF16)
        WC = sb("WC", [128, 8, 4, 2, 128], BF16)
        Kt = sb("Kt", [128, 2, 4, 128], BF16)
        identb = sb("identb", [128, 128], BF16)
        stage = [sb("stage%d" % i, [128, 1024]) for i in range(2)]
        psm = sb("psm", [128, NSM])
        psm_g = sb("psm_g", [128, 16])
        Eb = sb("Eb", [128, 8, 384], BF16)
        onesb = sb("onesb", [128, 128], BF16)
        onesLR = sb("onesLR", [128, 2, 128], BF16)
        iota = sb("iota", [128, W])
        ratio = sb("ratio", [128, 2, 2, 8])
        zer = sb("zer", [128, 16])
        rr = sb("rr", [128, 16])
        ff = sb("ff", [128, 16])
        dts = sb("dts", [128, 16])
        k16 = sb("k16", [128, 16], I32)
        u16 = sb("u16", [128, 16])
        off = sb("off", [128, 16])
        bo1 = sb("bo1", [128, 16])
        bo2 = sb("bo2", [128, 16])
        carry = sb("carry", [128, 16, 2])
        esink = sb("esink", [128, 4])
        dtc = sb("dtc", [128, 4])
        xs = [sb("xs%d" % i, [128, 8, W]) for i in range(2)]
        sqb = sb("sqb", [128, 8, W], BF16)
        hT = sb("hT", [128, 8, W], BF16)
        sd = sb("sd", [128, W])
        rstd = sb("rstd", [128, W])
        cA = sb("cA", [128, 8, NCH]); cK = sb("cK", [128, 8, NCH], I32)
        cSn = sb("cSn", [128, 8, NCH]); cCs = sb("cCs", [128, 8, NCH])
        cVr = sb("cVr", [128, 8, NCH]); cVi = sb("cVi", [128, 8, NCH])
        cGr = sb("cGr", [128, 8, NCH]); cGi = sb("cGi", [128, 8, NCH])
        Hre = sb("Hre", [128, 8, NCH + 2], BF16); Him = sb("Him", [128, 8, NCH + 2], BF16)
        ffT = sb("ffT", [128, 16]); rrT = sb("rrT", [128, 16]); offT = sb("offT", [128, 16]); ffp = sb("ffp", [128, 16])
        zsm = sb("zsm", [128, 5, 2, 8])
        carc = sb("carc", [128, 2, 8])
        su_s = [sb("su_s%d" % i, [128, 2, W], BF16) for i in range(2)]
        kt_s = sb("kt_s", [128, 2, W], BF16)
        pu_s = sb("pu_s", [128, 2, W])
        v_s = sb("v_s", [128, 2, 128], BF16)
        yf_s = [sb("yf_s%d" % i, [128, W]) for i in range(2)]
        su_l = [sb("su_l%d" % i, [128, 2, W], BF16) for i in range(2)]
        kt_l = sb("kt_l", [128, 2, 4 * 128], BF16)
        pu_l = sb("pu_l", [128, 2, W + 16])
        v_l = sb("v_l", [128, 4, 128], BF16)
        VLR = sb("VLR", [128, 4, 2, 2, 128], BF16)
        yf_l = [sb("yf_l%d" % i, [128, 2, W]) for i in range(2)]
        gate_s = sb("gate_s", [128, 2, W])
        gate_p = sb("gate_p", [128, 2, W])
        gate_a = sb("gate_a", [128, 4, W])
        qT = sb("qT", [128, 4, W], BF16)
        yg = [sb("yg%d" % i, [128, 2, W], BF16) for i in range(2)]
        yo2 = sb("yo2", [128, 8, W])
        pch = yo2[:, 0:4, :].rearrange("p a (b c) -> p a b c", b=4)
        ytmp = sb("ytmp", [128, W])
        mix = sb("mix", [128, 8, W], BF16)
        sig = sb("sig", [128, W])
        glt = sb("glt", [128, W])
        pA = sb("pA", [128, W + 16])
        pB = sb("pB", [128, W + 16])
        pC = sb("pC", [128, W + 16])
        pD = sb("pD", [128, W + 16])
        pmean = sb("pmean", [128, W])
        mixed = sb("mixed", [128, 2, W], BF16)
        pexp = [sb("pexp%d" % i, [128, 384], BF16) for i in range(2)]
        pT = [sb("pT%d" % i, [128, 384], BF16) for i in range(4)]
        dn = sb("dn", [128, 256])
        o1 = sb("o1", [128, 256])
        banks = [es.enter_context(nc.psum_tensor("bank%d" % i, [128, 512], F32)) for i in range(8)]

        def bview(b, n):
            return banks[b][:, 0:n * 256].rearrange("p (a c) -> p a c", a=n)

        if DBG.get('mem'):
            print('SBUF bytes remaining', nc.sbuf_bytes_remaining)
        p = Prog()
        p_real = p
        WKALL = ["wk%d" % i for i in range(8)]
        CARRYALL = ["carry%d" % i for i in range(16)]

        def pslot(n):
            st = {"i": 0}

            def nxt():
                st["i"] += 1
                return (st["i"] - 1) % n
            return nxt

        fl = lambda ap, pat: ap.rearrange(pat)

        p.memset("dve", onesb[:], 1.0, w=["onesb"])
        p.memset("dve", onesLR[:].rearrange("p a b -> p (a b)"), 0.0, w=["onesLR"])
        p.memset("dve", onesLR[:, 0, 0:64], 1.0, w=["onesLR"])
        p.memset("dve", onesLR[:, 1, 64:128], 1.0, w=["onesLR"])
        p.memset("dve", VLR[:].rearrange("p a b c d -> p (a b c d)"), 0.0, w=["VLR"])
        p.memset("dve", zer[:], 0.0, w=["zer"])
        p.memset("dve", v_l[:].rearrange("p a b -> p (a b)"), 0.0, w=["v_l"])
        p.memset("dve", kt_l[:].rearrange("p a b -> p (a b)"), 0.0, w=["kt_l"])
        p.dmas(iota[:], ciota, "c0", w=["iota"])
        p.dmas(stage[0][:, 0:128], cident, "stg0", w=["stage0", "stage0b"])
        p.copy("dve", identb[:], stage[0][:, 0:128], r=["stage0"], w=["identb"])
        p.dmas(ratio[:].rearrange("p a b c -> p (a b c)"), cratio, "c1", w=["ratio"])
        for ct in range(2):
            p.dmas(pusc[:, ct, 0:8], zer[:, 0:8], "c2", r=["zer"], w=["pusc_pad"])
            p.dmas(pusc[:, ct, SEQ + 8:SEQ + 16], zer[:, 0:8], "c2", r=["zer"], w=["pusc_pad"])
        p.dmas(stage[0][:, 0:384], cdist, "stg0", w=["stage0", "stage0b"])
        p.dmas(stage[1][:, 0:384], cvalid, "stg1", w=["stage1", "stage1b"])
        for h in range(8):
            slope = 2.0 ** (-(h + 1))
            p.actf(stage[0][:, 384:768], stage[0][:, 0:384], AF.Exp, r=["stage0"], w=["stage0b"], scale=-slope)
            p.tt("dve", Eb[:, h, :], stage[0][:, 384:768], stage[1][:, 0:384], ALU.mult,
                 r=["stage0b", "stage1"], w=["Eb"])

        stg_i = [0]

        def stage_load(src_ap, ncols):
            s = stg_i[0] % 2
            stg_i[0] += 1
            key = "stage%d" % s
            p.dmas(stage[s][:, 0:ncols], src_ap, "stg%d" % s, w=[key, key + "b"])
            return s, key

        def prep_small(l):
            p.dmas(psm[:], psmall[l], "psm", w=["psm"])
            p.actf(esink[:], psm[:, C_SINK:C_SINK + 4], AF.Exp, r=["psm"], w=["esink"])

        def prep_weights_p1(l):
            ch = []
            for kc in range(8):
                def f(kc=kc):
                    s, key = stage_load(w1[l, kc * 128:(kc + 1) * 128, :], 896)
                    p.amul(w1b[:, kc, :], stage[s][:, 0:896], psm_g[:, C_GPRE + kc:C_GPRE + kc + 1], r=[key, "psm_g"], w=["w1b"])
                ch.append(f)
            return ch

        def prep_weights_main(l):
            ch = []
            for kc in range(8):
                for hcol in range(2):
                    def f(kc=kc, hcol=hcol):
                        c0 = hcol * 768
                        s, key = stage_load(w2[l, kc * 128:(kc + 1) * 128, c0:c0 + 768], 768)
                        p.amul(w2b[:, kc, c0:c0 + 768], stage[s][:, 0:768], psm_g[:, C_GPRE + kc:C_GPRE + kc + 1],
                               r=[key, "psm_g"], w=["w2b"])
                    ch.append(f)
            for kc in range(8):
                def f(kc=kc):
                    s, key = stage_load(wo[l, kc * 128:(kc + 1) * 128, :], 1024)
                    p.acopy(wob[:, kc, :], stage[s][:, 0:1024], r=[key], w=["wob"])
                ch.append(f)
            for kc in range(2):
                def f(kc=kc):
                    s, key = stage_load(gluw[l, kc * 128:(kc + 1) * 128, :], 512)
                    p.acopy(glub[:, kc, :], stage[s][:, 0:512], r=[key], w=["glub"])
                ch.append(f)

            def f():
                s, key = stage_load(poolw[l], 256)
                p.acopy(plwb[:].rearrange("p a b -> p (a b)"), stage[s][:, 0:256], r=[key], w=["plwb"])
            ch.append(f)
            return ch

        def prep_gains(l):
            p.dmas(psm_g[:], psmall[l, :, 0:16], "psmg", w=["psm_g"])

        CB = [cA, cSn, cCs, cVr, cVi, cGr, cGi]
        CBK = ["cA", "cSn", "cCs", "cVr", "cVi", "cGr", "cGi"]
        GRALL = ["cGr"] + ["cGr_%d" % t_ for t_ in range(8)]
        GIALL = ["cGi"] + ["cGi_%d" % t_ for t_ in range(8)]
        ALLC = CBK + ["cK", "Hre", "Him"] + GRALL[1:] + GIALL[1:]

        def prep_tables(l, d, rec=None, which="AB"):
            p = rec if (rec is not None) else p_real
            fl2 = lambda t: t[:].rearrange("p a b -> p (a b)")
            sl = slice(8 * d, 8 * d + 8)
            if "A" in which:
                fl2 = lambda t: t[:].rearrange("p a b -> p (a b)")
                T = lambda i: fl2(CB[i // 2])[:, (i % 2) * 256:(i % 2) * 256 + 256]
                T3 = lambda i: T(i).rearrange("p (a b) -> p a b", a=4)
                H3 = lambda i: T3(i)[:, 2 * d:2 * d + 2, :]
                R = ALLC + ["yo2", "psm", "dtc"]
                Wk = ALLC
                p.dmas(yo2[:, 0:4, :].rearrange("p a b -> p (a b)"), pchan[l], "pch", w=["yo2"])
                BTre, BTim, AR, AI = (pch[:, 0, :, :], pch[:, 1, :, :], pch[:, 2, :, :], pch[:, 3, :, :])
                p.actf(dtc[:], psm[:, C_LDTR:C_LDTR + 4], AF.Exp, r=["psm"], w=["dtc"])
                dtb = dtc[:].unsqueeze(2).to_broadcast([128, 4, 64])
                kint = fl2(cK)[:, 0:256]
                p.tt("dve", T3(0), AR, dtb, ALU.mult, r=R, w=Wk)
                p.tt("dve", T3(1), AI, dtb, ALU.mult, r=R, w=Wk)
                p.actf(T(2), T(0), AF.Exp, r=R, w=Wk)
                p.ts("dve", kint, T(1), 1.0 / TWO_PI, None, ALU.mult, r=R, w=Wk)
                p.stt("dve", T(3), T(1), 1.0 / TWO_PI, kint, ALU.mult, ALU.subtract, r=R, w=Wk)
                p.actf(T(4), T(3), AF.Sin, r=R, w=Wk, scale=SC2)
                p.actf(T(5), T(3), AF.Sin, r=R, w=Wk, scale=SC1)
                p.actf(T(5), T(5), AF.Square, r=R, w=Wk)
                p.ts("dve", T(5), T(5), -2.0, 1.0, ALU.mult, ALU.add, r=R, w=Wk)
                p.tt("dve", T(6), T(2), T(5), ALU.mult, r=R, w=Wk)
                p.tt("dve", T(7), T(2), T(4), ALU.mult, r=R, w=Wk)
                p.ts("dve", T(8), T(6), -1.0, None, ALU.add, r=R, w=Wk)
                p.tt("dve", T3(0), AR, AR, ALU.mult, r=R, w=Wk)
                p.tt("dve", T3(1), AI, AI, ALU.mult, r=R, w=Wk)
                p.tt("dve", T(0), T(0), T(1), ALU.add, r=R, w=Wk)
                p.recip(T(0), T(0), r=R, w=Wk)
                p.tt("dve", T3(1), T3(8), AR, ALU.mult, r=R, w=Wk)
                p.tt("dve", T3(2), T3(7), AI, ALU.mult, r=R, w=Wk)
                p.tt("dve", T(1), T(1), T(2), ALU.add, r=R, w=Wk)
                p.tt("dve", T(1), T(1), T(0), ALU.mult, r=R, w=Wk)
                p.tt("dve", T3(2), T3(7), AR, ALU.mult, r=R, w=Wk)
                p.tt("dve", T3(3), T3(8), AI, ALU.mult, r=R, w=Wk)
                p.tt("dve", T(2), T(2), T(3), ALU.subtract, r=R, w=Wk)
                p.tt("dve", T(2), T(2), T(0), ALU.mult, r=R, w=Wk)
                ZR, ZI, LR, LI = 1, 2, 6, 7
                mk2 = psm[:, C_MASK2:C_MASK2 + 2].unsqueeze(1).unsqueeze(3).to_broadcast([128, 2, 2, 64])
                for k in range(4):
                    j = (3 - k) if d == 0 else k
                    p.tt("dve", H3(3), H3(ZR), BTre[:, 2 * d:2 * d + 2, :], ALU.mult, r=R, w=Wk)
                    p.tt("dve", H3(4), H3(ZI), BTim[:, 2 * d:2 * d + 2, :], ALU.mult, r=R, w=Wk)
                    p.tt("dve", H3(3), H3(3), H3(4), ALU.subtract, r=R, w=Wk)
                    p.tt("dve", H3(4), H3(ZR), BTim[:, 2 * d:2 * d + 2, :], ALU.mult, r=R, w=Wk)
                    p.tt("dve", H3(5), H3(ZI), BTre[:, 2 * d:2 * d + 2, :], ALU.mult, r=R, w=Wk)
                    p.tt("dve", H3(4), H3(4), H3(5), ALU.add, r=R, w=Wk)
                    for ri, src in ((0, 3), (1, 4)):
                        p.tt("dve", WS[:, :, j, ri, :].rearrange("p a (b c) -> p a b c", b=2),
                             H3(src).unsqueeze(2).to_broadcast([128, 2, 2, 64]), mk2, ALU.mult, r=R, w=["WS"])
                    if k < 3:
                        p.tt("dve", H3(3), H3(ZR), H3(LR), ALU.mult, r=R, w=Wk)
                        p.tt("dve", H3(4), H3(ZI), H3(LI), ALU.mult, r=R, w=Wk)
                        p.tt("dve", H3(5), H3(ZR), H3(LI), ALU.mult, r=R, w=Wk)
                        p.tt("dve", H3(9), H3(ZI), H3(LR), ALU.mult, r=R, w=Wk)
                        p.tt("dve", H3(ZR), H3(3), H3(4), ALU.subtract, r=R, w=Wk)
                        p.tt("dve", H3(ZI), H3(5), H3(9), ALU.add, r=R, w=Wk)
                p.ts("dve", WS3[64:128].rearrange("p a b c d -> p (a b c d)"), WS[64:128].rearrange("p a b c d -> p (a b c d)"),
                     psm[64:128, C_MASK3:C_MASK3 + 1], None, ALU.mult, r=["WS", "psm"], w=["WS3"])

                sl = slice(8 * d, 8 * d + 8)
                p.actf(dts[:], psm[:, C_LDTS:C_LDTS + 16], AF.Exp, r=["psm"], w=["dts"])
                p.tt("dve", u16[:], psm[:, C_ARS:C_ARS + 16], dts[:], ALU.mult, r=["psm", "dts"], w=["u16"])
                p.actf(rr[:], u16[:], AF.Exp, r=["u16"], w=["rr"])
                p.tt("dve", u16[:], psm[:, C_AIS:C_AIS + 16], dts[:], ALU.mult, r=["psm", "dts", "rr"], w=["u16"])
                p.ts("dve", k16[:], u16[:], 1.0 / TWO_PI, None, ALU.mult, r=["u16"], w=["k16"])
                p.stt("dve", ffp[:], u16[:], 1.0 / TWO_PI, k16[:], ALU.mult, ALU.subtract, r=["u16", "k16"], w=["ffp"])
                p.actf(off[:], ffp[:], AF.Sin, r=["ffp"], w=["off"], scale=SC2)
                p.actf(bo1[:], ffp[:], AF.Sin, r=["ffp"], w=["bo1"], scale=SC1)
                p.actf(bo1[:], bo1[:], AF.Square, r=["bo1"], w=["bo1"])
                p.ts("dve", bo1[:], bo1[:], -2.0, 1.0, ALU.mult, ALU.add, r=["bo1"], w=["bo1"])
                p.tt("dve", zsm[:, 1, 0, :], rr[:, sl], bo1[:, sl], ALU.mult, r=["rr", "bo1"], w=["zsm"])
                p.tt("dve", zsm[:, 1, 1, :], rr[:, sl], off[:, sl], ALU.mult, r=["rr", "off"], w=["zsm"])
                for k in range(1, 4):
                    a_r, a_i = zsm[:, k, 0, :], zsm[:, k, 1, :]
                    l_r, l_i = zsm[:, 1, 0, :], zsm[:, 1, 1, :]
                    p.tt("dve", bo2[:, 0:8], a_r, l_r, ALU.mult, r=["zsm"], w=["bo2"])
                    p.tt("dve", bo2[:, 8:16], a_i, l_i, ALU.mult, r=["zsm"], w=["bo2"])
                    p.tt("dve", zsm[:, k + 1, 0, :], bo2[:, 0:8], bo2[:, 8:16], ALU.subtract, r=["bo2"], w=["zsm"])
                    p.tt("dve", bo2[:, 0:8], a_r, l_i, ALU.mult, r=["zsm"], w=["bo2"])
                    p.tt("dve", bo2[:, 8:16], a_i, l_r, ALU.mult, r=["zsm"], w=["bo2"])
                    p.tt("dve", zsm[:, k + 1, 1, :], bo2[:, 0:8], bo2[:, 8:16], ALU.add, r=["bo2"], w=["zsm"])
                sgn = 1.0 if d == 0 else -1.0
                p.ts("dve", k16[:, sl], ffp[:, sl], sgn * TCH, None, ALU.mult, r=["ffp"], w=["k16"])
                p.stt("dve", ffT[:, sl], ffp[:, sl], sgn * TCH, k16[:, sl], ALU.mult, ALU.subtract, r=["ffp", "k16"], w=["ffT"])
                p.tt("dve", rrT[:, sl], rr[:, sl], rr[:, sl], ALU.mult, r=["rr"], w=["rrT"])
                p.tt("dve", rrT[:, sl], rrT[:, sl], rrT[:, sl], ALU.mult, r=["rrT"], w=["rrT"])


            p = p_real
            if "B" not in which:
                return
            p.dmas(stage[0][:, 0:1024], cpad[l, 0, :, d * 1024:(d + 1) * 1024], "stg0", w=["stage0", "stage0b"])
            p.dmas(stage[1][:, 0:1024], cpad[l, 1, :, d * 1024:(d + 1) * 1024], "stg1", w=["stage1", "stage1b"])
            c3 = lambda t: t[:, 0:1024].rearrange("p (a b) -> p a b", a=8)
            cre, cim = c3(stage[0]), c3(stage[1])
            t1 = yo2[:, 0:4, :].rearrange("p a b -> p (a b)").rearrange("p (a b) -> p a b", a=8)
            t2 = yo2[:, 4:8, :].rearrange("p a b -> p (a b)").rearrange("p (a b) -> p a b", a=8)
            RW = ["yo2", "stage0", "stage1", "zsm"]
            Ta0 = cGr[:].bitcast(BF16)
            Tb0 = cGi[:].bitcast(BF16)
            assert tuple(Ta0.shape) == (128, 8, 128), Ta0.shape
            p.acopy(Ta0, cre, r=["stage0"] + ALLC, w=GRALL)
            p.amul(Tb0, cim, -1.0, r=["stage1"] + ALLC, w=GIALL)

            x2 = lambda t, o: fl2(t)[:, o:o + 256].rearrange("p (a b) -> p a b", a=2)
            Xs = [x2(Hre, 0), x2(Hre, 256), x2(Him, 0), x2(Him, 256)]
            XK = ["Xs0", "Xs1", "Xs2", "Xs3"]
            p.memset("dve", fl2(Hre), 0.0, w=["Hre", "Xs0", "Xs1"])
            p.memset("dve", fl2(Him), 0.0, w=["Him", "Xs2", "Xs3"])
            def kg_transpose(ct, tau, gp):
                j = (3 - tau) if d == 0 else tau
                tb = proj_slot()
                tkey = "b%d" % tb
                if gp < 3:
                    rows, ncol, cbase, src = slice(32 * gp, 32 * gp + 32), 32, 32 * gp, WS
                else:
                    rows, ncol, cbase, src = slice(64, 128), 64, 64, WS3
                for ri in range(2):
                    p.mm(banks[tb][:, ri * 64:ri * 64 + ncol], src[rows, ct, j, ri, :], identb[rows, cbase:cbase + ncol],
                         True, True, r=["WS", "WS3", "identb"], w=[tkey])
                p.acopy(Xs[gp][:, :, cbase:cbase + ncol],
                        banks[tb][:, 0:128].rearrange("p (a b) -> p a b", a=2)[:, :, 0:ncol], r=[tkey], w=[XK[gp]])

            steps = [(ct, tau, gp) for ct in range(2) for tau in range(4) for gp in range(4)]
            kg_transpose(*steps[0])
            for si, (ct, tau, gp) in enumerate(steps):
                if si + 1 < len(steps):
                    kg_transpose(*steps[si + 1])
                kb_ = 2
                tile = ct * 4 + gp
                p.mm(banks[kb_][:, 0:128], Xs[gp][:, 0, :], Ta0[:, tile, :], gp == 0, False, r=[XK[gp], "cGr"], w=["b2"])
                p.mm(banks[kb_][:, 0:128], Xs[gp][:, 1, :], Tb0[:, tile, :], False, gp == 3, r=[XK[gp], "cGi"], w=["b2"])
                if gp == 3:
                    p.acopy(Kt[:, ct, tau, :], banks[kb_][:, 0:128], r=["b2"], w=["Kt"])
            for i in range(4):
                k = (i + 1) if d == 0 else (TCH - i)
                zr = zsm[:, k, 0, :].unsqueeze(2).to_broadcast([128, 8, 128])
                zi = zsm[:, k, 1, :].unsqueeze(2).to_broadcast([128, 8, 128])
                p.tt("dve", t1, cre, zr, ALU.mult, r=RW, w=["yo2"])
                p.tt("dve", t2, cim, zi, ALU.mult, r=RW, w=["yo2"])
                p.tt("dve", WC[:, :, i, 0, :], t1, t2, ALU.subtract, r=RW, w=["WC"])
                p.tt("dve", t1, cre, zi, ALU.mult, r=RW, w=["yo2"])
                p.tt("dve", t2, cim, zr, ALU.mult, r=RW, w=["yo2"])
                p.stt("dve", WC[:, :, i, 1, :], t1, -1.0, t2, ALU.mult, ALU.subtract, r=RW, w=["WC"])
            p.memset("dve", fl2(Hre), 0.0, w=["Hre", "Xs0", "Xs1"])
            p.memset("dve", fl2(Him), 0.0, w=["Him", "Xs2", "Xs3"])
            p.memset("dve", carc[:].rearrange("p a b -> p (a b)"), 0.0, w=["carc"])

        proj_slot = pslot(2)

        def norm_sq(xs_t, xkey):
            p.actf(sqb[:].rearrange("p a b -> p (a b)"), xs_t[:].rearrange("p a b -> p (a b)"), AF.Square,
                   r=[xkey], w=["sqb"])

        def norm_a(xs_t, xkey, sq=True):
            if sq:
                norm_sq(xs_t, xkey)
            b = proj_slot()
            bk = "b%d" % b
            for kc in range(8):
                p.mm(banks[b][:, 0:W], onesb[:], sqb[:, kc, :], kc == 0, kc == 7, r=["sqb", "onesb"], w=[bk])
            p.actf(sd[:], banks[b][:, 0:W], AF.Ln, r=[bk], w=["sd"], scale=1.0 / D, bias=EPS)
            p.actf(rstd[:], sd[:], AF.Exp, r=["sd"], w=["rstd"], scale=-0.5)

        def norm_b(xs_t, xkey, eng="dve"):
            p.tt(eng, hT[:], xs_t[:], rstd[:].unsqueeze(1).to_broadcast([128, 8, W]), ALU.mult,
                 r=[xkey, "rstd"], w=["hT"])

        def norm_slab(xs_t, xkey):
            norm_a(xs_t, xkey)
            norm_b(xs_t, xkey)

        def proj_fm(wb, wkey, col0, nt=2):
            b = proj_slot()
            key = "b%d" % b
            for t in range(nt):
                for kc in range(8):
                    p.mm(banks[b][:, t * W:(t + 1) * W], wb[:, kc, col0 + t * 128:col0 + (t + 1) * 128], hT[:, kc, :],
                         kc == 0, kc == 7, r=["hT", wkey], w=[key])
            return bview(b, nt), key

        tab_slot = pslot(2)
        pp_slot = pslot(2)
        brbi_slot = pslot(2)

        def ssm_slab3(direction, s, su_t, sukey, after_ct, fillers=(), after_s=None):
            d = direction
            fillers = list(fillers)
            sl = slice(8 * d, 8 * d + 8)
            rev = (d == 1)
            nops = 30
            per_op = max(1, -(-len(fillers) // nops)) if fillers else 0
            fl2 = lambda t: t[:].rearrange("p a b -> p (a b)")

            def fill(n=1):
                for _ in range(n * per_op):
                    if fillers:
                        fillers.pop(0)()

            for tile in range(8):
                ct, gp = tile // 4, tile % 4
                if gp < 3:
                    rows, src = slice(32 * gp, 32 * gp + 32), WS
                else:
                    rows, src = slice(64, 128), WS3
                if gp not in DBG.get("gps", (0, 1, 2, 3)):
                    continue
                for ri in range(2):
                    for j in range(TCH):
                        p.mm(banks[2 + ri][:, tile * NCH:(tile + 1) * NCH], src[rows, ct, j, ri, :], su_t[rows, ct, j:W:TCH],
                             j == 0, j == TCH - 1, r=[sukey, "WS", "WS3"], w=["b%d" % (2 + ri), "rgser"],
                             force=(ri == 0 and j == 0))
            Sre = banks[2][:, 0:8 * NCH].rearrange("p (a b) -> p a b", a=8)
            Sim = banks[3][:, 0:8 * NCH].rearrange("p (a b) -> p a b", a=8)
            c0 = float(s * NCH)
            p.ts("dve", k16[:, sl], ffT[:, sl], c0, None, ALU.mult, r=["ffT"], w=["k16"])
            p.stt("dve", offT[:, sl], ffT[:, sl], c0, k16[:, sl], ALU.mult, ALU.subtract, r=["ffT", "k16"], w=["offT"])
            bc_t = lambda t: t[:, sl].unsqueeze(2).to_broadcast([128, 8, NCH])
            io_b = iota[:, 0:NCH].unsqueeze(1).to_broadcast([128, 8, NCH])
            p.tt("dve", cA[:], io_b, bc_t(ffT), ALU.mult, r=["iota", "ffT", "cA"], w=["cA"])
            p.tt("dve", cK[:], cA[:], bc_t(offT), ALU.add, r=["cA", "offT"], w=["cK"])
            p.tt("dve", cA[:], cA[:], bc_t(offT), ALU.add, r=["cA", "offT", "cK"], w=["cA"])
            p.tt("dve", cA[:], cA[:], cK[:], ALU.subtract, r=["cA", "cK"], w=["cA"])
            p.actf(fl2(cSn), fl2(cA), AF.Sin, r=["cA"], w=["cSn"], scale=SC2)
            p.actf(fl2(cCs), fl2(cA), AF.Sin, r=["cA"], w=["cCs"], scale=SC1)
            p.actf(fl2(cCs), fl2(cCs), AF.Square, r=["cCs"], w=["cCs"])
            p.actf(fl2(cCs), fl2(cCs), AF.Identity, r=["cCs"], w=["cCs"], scale=-2.0, bias=1.0)
            if after_s is not None:
                after_s()
            fill(4)
            p.tt("dve", cVr[:], Sre, cCs[:], ALU.mult, r=["b2", "cCs"], w=["cVr"])
            p.tt("dve", cA[:], Sim, cSn[:], ALU.mult, r=["b3", "cSn", "cA"], w=["cA"])
            fill()
            p.tt("dve", cVi[:], Sim, cCs[:], ALU.mult, r=["b3", "cCs"], w=["cVi"])
            p.tt("dve", cGr[:], Sre, cSn[:], ALU.mult, r=["b2", "cSn"], w=GRALL)
            fill()
            p.tt("dve", cVr[:], cVr[:], cA[:], ALU.add, r=["cVr", "cA"], w=["cVr"])
            p.tt("dve", cVi[:], cVi[:], cGr[:], ALU.subtract, r=["cVi"] + GRALL, w=["cVi"])
            fill()
            fw = (lambda ap: ap[:, ::-1]) if rev else (lambda ap: ap)
            for tile in range(8):
                st = 8 * d + tile
                rb = rrT[:, st:st + 1].to_broadcast([128, NCH])
                p.scan(fw(cGr[:, tile, :]), rb, fw(cVr[:, tile, :]), carc[:, 0, tile:tile + 1], r=["cVr", "rrT", "carc"],
                       w=["cGr_%d" % tile])
                p.scan(fw(cGi[:, tile, :]), rb, fw(cVi[:, tile, :]), carc[:, 1, tile:tile + 1], r=["cVi", "rrT", "carc"],
                       w=["cGi_%d" % tile])
                if tile % 2 == 1:
                    fill()
            lastc = 0 if rev else NCH - 1
            GRK = ["cGr_%d" % t_ for t_ in range(8)]
            GIK = ["cGi_%d" % t_ for t_ in range(8)]
            p.copy("dve", carc[:, 0, :], cGr[:, :, lastc], r=GRK, w=["carc"])
            p.copy("dve", carc[:, 1, :], cGi[:, :, lastc], r=GIK, w=["carc"])
            hcols = slice(1, NCH + 1)
            if not rev:
                hdst, hsrc, hrhs = 0, NCH, slice(0, NCH)
            else:
                hdst, hsrc, hrhs = NCH + 1, 1, slice(2, NCH + 2)
            p.copy("dve", Hre[:, :, hdst], Hre[:, :, hsrc], r=["Hre"], w=["Hre"])
            p.copy("dve", Him[:, :, hdst], Him[:, :, hsrc], r=["Him"], w=["Him"])
            fill()
            p.tt("dve", cVr[:], cGr[:], cCs[:], ALU.mult, r=GRK + ["cCs", "cVr"], w=["cVr"])
            p.tt("dve", cVi[:], cGi[:], cSn[:], ALU.mult, r=GIK + ["cSn", "cVi"], w=["cVi"])
            fill()
            p.tt("dve", Hre[:, :, hcols], cVr[:], cVi[:], ALU.subtract, r=["cVr", "cVi", "Hre"], w=["Hre"])
            p.tt("dve", cVr[:], cGr[:], cSn[:], ALU.mult, r=GRK + ["cSn", "cVr", "Hre"], w=["cVr"])
            fill()
            p.tt("dve", cVi[:], cGi[:], cCs[:], ALU.mult, r=GIK + ["cCs", "cVi", "Hre"], w=["cVi"])
            fill()
            p.tt("dve", Him[:, :, hcols], cVr[:], cVi[:], ALU.add, r=["cVr", "cVi", "Him"], w=["Him"])
            while fillers:
                fillers.pop(0)()

            def part_b():
              for ct in range(2):
                  ykey = "b%d" % (4 + ct)
                  for i in range(TCH):
                      yo_ = banks[4 + ct][:, i:W:TCH]
                      js = list(range(0, i + 1)) if not rev else list(range(i, TCH))
                      nmm = len(js) + 8
                      n = 0
                      for j in js:
                          tau = abs(i - j)
                          p.mm(yo_, Kt[:, ct, tau, :], su_t[:, ct, j:W:TCH], n == 0, n == nmm - 1, r=[sukey, "Kt"], w=[ykey])
                          n += 1
                      for gp in range(4):
                          tile = ct * 4 + gp
                          p.mm(yo_, WC[:, tile, i, 0, :], Hre[:, tile, hrhs], n == 0, n == nmm - 1, r=["WC", "Hre"], w=[ykey])
                          n += 1
                          p.mm(yo_, WC[:, tile, i, 1, :], Him[:, tile, hrhs], n == 0, n == nmm - 1, r=["WC", "Him"], w=[ykey])
                          n += 1
                  after_ct(ct, banks[4 + ct][:, 0:W], ykey)
            return part_b

        def p1_pieces(s, src_v, srckeyf):
            c0, c1 = s * W, (s + 1) * W
            xb = s % 2
            xkey = "xs%d" % xb
            sus, suk = su_s[s % 2], "su_s%d" % (s % 2)
            pcs = []

            def f_norm():
                norm_b(xs[xb], xkey, eng=DBG.get("p1_norm_eng", "pool"))
            pcs.append(f_norm)

            def f_sq_next():
                if s + 1 < NS:
                    norm_sq(xs[(s + 1) % 2], "xs%d" % ((s + 1) % 2))
            pcs.append(f_sq_next)

            def f_su():
                ps, key = proj_fm(w1b, "w1b", 0)
                p.acopy(sus[:], ps, r=[key], w=[suk])
                p.dmas(susc[:, :, c0:c1], sus[:], "st_su%d" % (s % 2), r=[suk], w=["susc"])
            pcs.append(f_su)

            def f_pu():
                ps, key = proj_fm(w1b, "w1b", 256)
                p.acopy(pu_s[:], ps, r=[key], w=["pu_s"])
                p.dmas(pusc[:, :, 8 + c0:8 + c1], pu_s[:], "st_pu", r=["pu_s"], w=["pusc"])
            pcs.append(f_pu)

            def f_kt():
                ps, key = proj_fm(w1b, "w1b", 512)
                p.acopy(kt_s[:], ps, r=[key], w=["kt_s"])
                p.dmas(ktsc[:, :, c0:c1], kt_s[:], "st_kt", r=["kt_s"], w=["ktsc"])
            pcs.append(f_kt)

            def f_v():
                b = proj_slot()
                key = "b%d" % b
                for blk in range(2):
                    for kc in range(8):
                        p.mm(banks[b][:, blk * 128:(blk + 1) * 128], hT[:, kc, blk * 128:(blk + 1) * 128], w1b[:, kc, 768:896],
                             kc == 0, kc == 7, r=["hT", "w1b"], w=[key])
                p.acopy(v_s[:], banks[b][:, 0:256].rearrange("p (a c) -> p a c", a=2), r=[key], w=["v_s"])
                p.dmas(vsc[:, 2 * s:2 * s + 2, :], v_s[:], "st_v", r=["v_s"], w=["vsc"])
            pcs.append(f_v)

            def f_next_norm():
                if s + 1 < NS:
                    nb = (s + 1) % 2
                    norm_a(xs[nb], "xs%d" % nb, sq=False)
                if s + 2 < NS:
                    p.dmas(xs[xb][:], src_v[:, :, c1 + W:c1 + 2 * W], "xs%d" % xb, r=[srckeyf(s + 2)], w=["xs%d" % xb])
            pcs.append(f_next_norm)
            return pcs

        def interleave(pieces, thunks):
            thunks = list(thunks)
            per = -(-len(thunks) // max(len(pieces), 1))
            for pc in pieces:
                pc()
                for _ in range(per):
                    if thunks:
                        thunks.pop(0)()
            while thunks:
                thunks.pop(0)()

        def p1_prologue(l, src_v, srckeyf, thunks=()):
            def f0():
                p.dmas(xs[0][:], src_v[:, :, 0:W], "xs0", r=[srckeyf(0)], w=["xs0"])
                p.dmas(xs[1][:], src_v[:, :, W:2 * W], "xs1", r=[srckeyf(1)], w=["xs1"])
                norm_a(xs[0], "xs0")
            interleave([f0] + p1_pieces(0, src_v, srckeyf), thunks)

        def phase1(l, src_v, srckeyf, deferred=()):
            deferred = list(deferred)
            pend = None
            for s in range(DBG.get("p1", NS)):
                c0, c1 = s * W, (s + 1) * W
                nxt_p = p1_pieces(s + 1, src_v, srckeyf) if s + 1 < NS else []
                if nxt_p:
                    nxt_p.pop(0)()
                fill = []
                for _ in range(2):
                    if deferred:
                        fill.append(deferred.pop(0))
                fill = nxt_p + fill

                def after_f(ct, ps, key, c0=c0, c1=c1):
                    p.acopy(yf_s[ct][:], ps, r=[key], w=["yf_s%d" % ct])
                    p.dmas(yfsc[:, ct, c0:c1], yf_s[ct][:], "st_yf%d" % ct, r=["yf_s%d" % ct], w=["yfsc"])
                pend = ssm_slab3(0, s, su_s[s % 2], "su_s%d" % (s % 2), after_f, fillers=fill, after_s=pend)
            if pend is not None:
                pend()
            while deferred:
                deferred.pop(0)()

        s_slot = pslot(2)
        od_slot = pslot(2)
        od_bank = lambda: 0 + proj_slot()
        pt_slot = pslot(4)
        pe_slot = pslot(2)

        def attention_pairs(s, nb_lo):
            v3 = lambda ap: ap.rearrange("p (a b) -> p a b", a=2)
            pairs = [(nloc, jp, jj) for nloc in range(2) for jp in range(2) for jj in range(2)]
            state = {}

            def stage1(nloc, jp, jj):
                n = 2 * s + nloc
                kbs = [kb for kb in (n - 1, n, n + 1) if 0 <= kb < 32]
                rel0 = kbs[0] - (n - 1)
                nk = len(kbs)
                qc0 = nloc * 128
                j = jp * 2 + jj
                kv = j // 2
                pts = []
                for hh in range(2):
                    h = 2 * j + hh
                    ss = s_slot()
                    skey = "b%d" % (6 + ss)
                    s_ps = banks[6 + ss]
                    rows = slice(64 * hh, 64 * hh + 64)
                    for ki_, kb in enumerate(kbs):
                        kl = kb - nb_lo
                        p.mm(s_ps[:, ki_ * 128:(ki_ + 1) * 128], kt_l[rows, kv, kl * 128:(kl + 1) * 128],
                             qT[rows, j, qc0:qc0 + 128], True, True, r=["kt_l", "qT"], w=[skey])
                    pe_i = pe_slot()
                    pe_, pk = pexp[pe_i], "pexp%d" % pe_i
                    p.actf(pe_[:, 0:nk * 128], s_ps[:, 0:nk * 128], AF.Exp, r=[skey], w=[pk])
                    pi_ = pt_slot()
                    pt_, tk = pT[pi_], "pT%d" % pi_
                    p.tt("dve", pt_[:, 0:nk * 128], pe_[:, 0:nk * 128], Eb[:, h, rel0 * 128:(rel0 + nk) * 128], ALU.mult,
                         r=[pk, "Eb"], w=[tk])
                    pts.append((pt_, tk))
                state[(nloc, jp, jj)] = (pts, kbs)

            def stage2(nloc, jp, jj):
                pts, kbs = state.pop((nloc, jp, jj))
                nk = len(kbs)
                qc0 = nloc * 128
                j = jp * 2 + jj
                kv = j // 2
                if jj == 0:
                    state[("od", nloc, jp)] = proj_slot()
                ob = state[("od", nloc, jp)]
                okey = "b%d" % ob
                odv = banks[ob][:].rearrange("p (j t c) -> p j t c", j=2, t=2)
                for t in range(2):
                    for hh in range(2):
                        pt_, tk = pts[hh]
                        for ki_, kb in enumerate(kbs):
                            kl = kb - nb_lo
                            first = (hh == 0 and ki_ == 0)
                            last = (hh == 1 and ki_ == nk - 1)
                            lhs = VLR[:, kl, kv, hh, :] if t == 0 else onesLR[:, hh, :]
                            p.mm(odv[:, jj, t, :], lhs, pt_[:, ki_ * 128:(ki_ + 1) * 128], first, last,
                                 r=[tk, "VLR", "onesLR"], w=[okey])
                if jj == 1:
                    state.pop(("od", nloc, jp))
                    p.tt("dve", v3(dn[:]), odv[:, :, 1, :], esink[:, 2 * jp:2 * jp + 2].unsqueeze(2).to_broadcast([128, 2, 128]), ALU.add,
                         r=[okey, "esink"], w=["dn"])
                    p.actf(dn[:], dn[:], AF.Ln, r=["dn"], w=["dn"])
                    p.actf(dn[:], dn[:], AF.Exp, r=["dn"], w=["dn"], scale=-1.0)
                    p.tt("dve", v3(o1[:]), odv[:, :, 0, :], v3(dn[:]), ALU.mult, r=[okey, "dn"], w=["o1"])
                    p.tt("dve", mix[:, 4 + 2 * jp:6 + 2 * jp, qc0:qc0 + 128], v3(o1[:]), gate_a[:, 2 * jp:2 * jp + 2, qc0:qc0 + 128],
                         ALU.mult, r=["o1", "gate_a"], w=["mix"])

            pcs = [lambda: stage1(*pairs[0])]
            for k in range(1, len(pairs)):
                pcs.append(lambda k=k: (stage1(*pairs[k]), stage2(*pairs[k - 1])))
            pcs.append(lambda: stage2(*pairs[-1]))
            return pcs

        def pool_slab(s):
            first, last = (s == 0), (s == NS - 1)
            n = W + 16
            lo, hi = slice(0, 64), slice(64, 128)
            for ct in range(2):
                Xc = pu_l[:, ct, :]
                p.tt("pool", pA[:, 1:n], Xc[:, 0:n - 1], Xc[:, 1:n], ALU.add, r=["pu_l"], w=["pA"])
                if ct == 0:
                    p.tt("pool", pB[hi, 2:n - 1], pA[hi, 1:n - 2], pA[hi, 3:n], ALU.add, r=["pA"], w=["pB"])
                    srcs = ((lo, pA, "pA"), (hi, pB, "pB"))
                else:
                    p.tt("pool", pB[:, 2:n - 1], pA[:, 1:n - 2], pA[:, 3:n], ALU.add, r=["pA"], w=["pB"])
                    p.tt("pool", pC[:, 4:n - 3], pB[:, 2:n - 5], pB[:, 6:n - 1], ALU.add, r=["pB"], w=["pC"])
                    p.tt("pool", pD[hi, 8:n - 7], pC[hi, 4:n - 11], pC[hi, 12:n - 3], ALU.add, r=["pC"], w=["pD"])
                    srcs = ((lo, pC, "pC"), (hi, pD, "pD"))
                for (rows, buf, bk) in srcs:
                    p.ts("pool", pmean[rows, :], buf[rows, 8:8 + W], psm[rows, C_INVW + ct:C_INVW + ct + 1], None, ALU.mult,
                         r=[bk, "psm"], w=["pmean"])
                if first:
                    p.tt("pool", pmean[:, 0:8], pmean[:, 0:8], ratio[:, ct, 0, :], ALU.mult, r=["pmean", "ratio"], w=["pmean"])
                if last:
                    p.tt("pool", pmean[:, W - 8:W], pmean[:, W - 8:W], ratio[:, ct, 1, :], ALU.mult, r=["pmean", "ratio"], w=["pmean"])
                p.tt("pool", mixed[:, ct, :], pmean[:], Xc[:, 8:8 + W], ALU.subtract, r=["pmean", "pu_l"], w=["mixed"])
                b = proj_slot()
                key = "b%d" % b
                p.mm(banks[b][:, 0:W], plwb[:, ct, :], mixed[:, ct, :], True, True, r=["mixed", "plwb"], w=[key])
                p.stt("dve", mix[:, 2 + ct, :], banks[b][:, 0:W], psm[:, C_PSC + ct:C_PSC + ct + 1], gate_p[:, ct, :], ALU.mult, ALU.mult,
                      r=[key, "psm", "gate_p"], w=["mix"])

        def ssm_b_loads(s):
            c0, c1 = s * W, (s + 1) * W
            pb = s % 2
            p.dmas(su_l[pb][:], susc[:, :, c0:c1], "ld_su%d" % pb, r=["susc"], w=["su_l%d" % pb])
            p.dmas(yf_l[pb][:], yfsc[:, :, c0:c1], "ld_yf%d" % pb, r=["yfsc"], w=["yf_l%d" % pb])

        def ssm_b(s, fillers=(), after_s=None):
            pb = s % 2

            def after_b(ct, ps, key):
                p.tt("dve", ytmp[:], ps, yf_l[pb][:, ct, :], ALU.add, r=[key, "yf_l%d" % pb], w=["ytmp"])
                p.stt("dve", ytmp[:], su_l[pb][:, ct, :], psm[:, C_SD + ct:C_SD + ct + 1], ytmp[:], ALU.mult, ALU.add,
                      r=["su_l%d" % pb, "psm", "ytmp"], w=["ytmp"])
                p.actf(yg[pb][:, ct, :], ytmp[:], AF.Gelu_apprx_tanh, r=["ytmp"], w=["yg%d" % pb])
            return ssm_slab3(1, s, su_l[pb], "su_l%d" % pb, after_b, fillers=fillers, after_s=after_s)

        out_stores = []

        def flush_stores():
            while out_stores:
                out_stores.pop(0)()

        def main_loads(s):
            c0 = s * W
            nb_lo = 2 * s - 1
            p.dmas(pu_l[:], pusc[:, :, c0:c0 + W + 16], "ld_pu", r=["pusc", "pusc_pad"], w=["pu_l"])
            blo, bhi = max(nb_lo, 0), min(nb_lo + 4, 32)
            p.dmas(kt_l[:, :, (blo - nb_lo) * 128:(bhi - nb_lo) * 128], ktsc[:, :, blo * 128:bhi * 128], "ld_kt",
                   r=["ktsc"], w=["kt_l"])
            p.dmas(v_l[:, blo - nb_lo:bhi - nb_lo, :], vsc[:, blo:bhi, :], "ld_v", r=["vsc"], w=["v_l"])

        def main_pieces(s, src_v, srckeyf):
            c0, c1 = s * W, (s + 1) * W
            xb = s % 2
            xkey = "xs%d" % xb
            nb_lo = 2 * s - 1
            pb = s % 2
            pcs = []

            def f_loads():
                if s - 1 >= 0:
                    nb = (s - 1) % 2
                    p.dmas(xs[nb][:], src_v[:, :, c0 - W:c0], "xs%d" % nb, r=[srckeyf(s - 1)], w=["xs%d" % nb])
                for kv in range(2):
                    for hh in range(2):
                        p.copy("pool", VLR[:, :, kv, hh, 64 * hh:64 * hh + 64], v_l[:, :, 64 * kv:64 * kv + 64],
                               r=["v_l"], w=["VLR"])
                norm_b(xs[xb], xkey, eng=DBG.get("main_norm_eng", "pool"))
            pcs.append(f_loads)


            def f_gs():
                ps, key = proj_fm(w2b, "w2b", 0)
                p.actf(gate_s[:], ps, AF.Silu, r=[key], w=["gate_s"])
            pcs.append(f_gs)

            def f_gp():
                ps, key = proj_fm(w2b, "w2b", 256)
                p.actf(gate_p[:], ps, AF.Silu, r=[key], w=["gate_p"])
            pcs.append(f_gp)
            for jp in range(2):
                def f_q(jp=jp):
                    ps, key = proj_fm(w2b, "w2b", 512 + jp * 256)
                    p.amul(qT[:, 2 * jp:2 * jp + 2, :], ps, 0.125, r=[key], w=["qT"])
                pcs.append(f_q)
            for jp in range(2):
                def f_ga(jp=jp):
                    ps, key = proj_fm(w2b, "w2b", 1024 + jp * 256)
                    p.actf(gate_a[:, 2 * jp:2 * jp + 2, :], ps, AF.Silu, r=[key], w=["gate_a"])
                pcs.append(f_ga)
            pcs.append(lambda: pool_slab(s))
            pcs.extend(attention_pairs(s, nb_lo))
            pass
            for mt in range(2):
                def f_glu(mt=mt):
                    b = proj_slot()
                    bk = "b%d" % b
                    for (hx, col) in ((0, mt * 128), (1, 256 + mt * 128)):
                        for kc in range(2):
                            p.mm(banks[b][:, hx * W:(hx + 1) * W], glub[:, kc, col:col + 128], yg[pb][:, kc, :], kc == 0, kc == 1,
                                 r=["yg%d" % pb, "glub"], w=[bk])
                    p.actf(sig[:], banks[b][:, W:2 * W], AF.Sigmoid, r=[bk, "psm"], w=["sig"], bias=psm[:, C_GB + 2 + mt:C_GB + 3 + mt])
                    p.stt("dve", glt[:], banks[b][:, 0:W], psm[:, C_GB + mt:C_GB + mt + 1], sig[:], ALU.add, ALU.mult,
                          r=[bk, "psm", "sig"], w=["glt"])
                    p.tt("dve", mix[:, mt, :], glt[:], gate_s[:, mt, :], ALU.mult, r=["glt", "gate_s"], w=["mix"])
                pcs.append(f_glu)
            for mp in range(4):
                def f_op(mp=mp):
                    b = proj_slot()
                    key = "b%d" % b
                    for t in range(2):
                        mt = 2 * mp + t
                        for kt in range(8):
                            p.mm(banks[b][:, t * W:(t + 1) * W], wob[:, kt, mt * 128:(mt + 1) * 128], mix[:, kt, :], kt == 0, kt == 7,
                                 r=["mix", "wob"], w=[key])
                    for t in range(2):
                        mt = 2 * mp + t
                        p.amul(yo2[:, mt, :], banks[b][:, t * W:(t + 1) * W], psm[:, C_GPOST + mt:C_GPOST + mt + 1],
                               r=[key, "psm"], w=["yo2"])
                    p.actf(sqb[:, 2 * mp:2 * mp + 2, :], bview(b, 2), AF.Square, r=[key], w=["sqb"])
                pcs.append(f_op)

            def f_post():
                b = proj_slot()
                bk = "b%d" % b
                for mt in range(8):
                    p.mm(banks[b][:, 0:W], onesb[:], sqb[:, mt, :], mt == 0, mt == 7, r=["sqb", "onesb"], w=[bk])
                p.actf(sd[:], banks[b][:, 0:W], AF.Ln, r=[bk], w=["sd"], scale=1.0 / D, bias=EPS)
                p.actf(rstd[:], sd[:], AF.Exp, r=["sd"], w=["rstd"], scale=-0.5)
                p.tt("dve", yo2[:], yo2[:], rstd[:].unsqueeze(1).to_broadcast([128, 8, W]), ALU.mult, r=["yo2", "rstd"], w=["yo2"])
                p.tt("dve", xs[xb][:], xs[xb][:], yo2[:], ALU.add, r=["yo2", xkey], w=[xkey])
                out_stores.append(lambda: p.dmas(out_v[:, :, c0:c1], xs[xb][:], "st_o%d" % xb, r=[xkey], w=["out%d" % s]))
            pcs.append(f_post)

            def f_tail():
                if s - 1 >= 0:
                    main_loads(s - 1)
                    norm_a(xs[(s - 1) % 2], "xs%d" % ((s - 1) % 2), sq=True)
            pcs.append(f_tail)
            return pcs

        EARLY = list(range(17))

        def main_early(l, src_v, srckeyf, thunks=()):
            lastb = (NS - 1) % 2
            p.dmas(xs[lastb][:], src_v[:, :, (NS - 1) * W:NS * W], "xs%d" % lastb, r=[srckeyf(NS - 1)], w=["xs%d" % lastb])
            main_loads(NS - 1)
            ssm_b_loads(NS - 1)
            norm_a(xs[lastb], "xs%d" % lastb)
            pcs = main_pieces(NS - 1, src_v, srckeyf)
            interleave([pcs[i_] for i_ in EARLY], thunks)
            return [pc for i_, pc in enumerate(pcs) if i_ not in EARLY]

        def phase_main(l, src_v, srckeyf, rest_first, deferred=()):
            deferred = list(deferred)
            nmain = DBG.get("main", NS)
            if nmain == 0:
                return
            pend = ssm_b(NS - 1)
            for s in range(NS - 1, NS - 1 - nmain, -1):
                fill = rest_first if s == NS - 1 else main_pieces(s, src_v, srckeyf)
                if s - 1 >= 0:
                    ssm_b_loads(s - 1)
                flush_stores()
                if s != NS - 1:
                    fill.pop(0)()
                if deferred:
                    fill.insert(1, deferred.pop(0))
                if s - 1 >= 0:
                    pend = ssm_b(s - 1, fillers=fill, after_s=pend)
                else:
                    if pend is not None:
                        pend()
                        pend = None
                    for f_ in fill:
                        f_()
            flush_stores()
            while deferred:
                deferred.pop(0)()

        prep_gains(0)
        for f_ in prep_weights_p1(0):
            f_()
        for l in range(L):
            first_src = (l == 0 and not from_out_first)
            src_v = xT_v if first_src else out_v
            srckeyf = (lambda s: "xin") if first_src else (lambda s: "out%d" % s)
            if DBG.get("stop") == "const":
                break
            prep_small(l)
            rec = Rec()
            prep_tables(l, 0, rec=rec, which="A")
            p1_prologue(l, src_v, srckeyf, thunks=rec.thunks(p))
            prep_tables(l, 0, which="B")
            if DBG.get("stop") == "prep":
                break
            phase1(l, src_v, srckeyf, deferred=prep_weights_main(l))
            rec = Rec()
            prep_tables(l, 1, rec=rec, which="A")
            rest_first = main_early(l, src_v, srckeyf, thunks=rec.thunks(p))
            prep_tables(l, 1, which="B")
            nxt = []
            if l + 1 < L:
                nxt = [lambda l=l: prep_gains(l + 1)] + prep_weights_p1(l + 1)
            phase_main(l, src_v, srckeyf, rest_first, deferred=nxt)

        p.emit(nc, es)
    return nc


def _consts():
    sp = np.arange(128)[:, None, None]
    rel = np.arange(3)[None, :, None]
    qi = np.arange(128)[None, None, :]
    dist = np.abs(qi - (rel - 1) * 128 - sp).astype(np.float32)
    valid = (dist <= 128).astype(np.float32)
    iota = np.broadcast_to(np.arange(W, dtype=np.float32)[None, :], (128, W)).copy()
    ratio = np.zeros((128, 2, 2, 8), np.float32)
    for ct in range(2):
        for row in range(128):
            w = (2, 4, 8, 16)[2 * ct + row // 64]
            for side in range(2):
                for c in range(8):
                    t = c if side == 0 else SEQ - 8 + c
                    cnt = min(t + w // 2, SEQ) - max(t - w // 2, 0)
                    ratio[row, ct, side, c] = w / cnt
    return (dist.reshape(128, 384), valid.reshape(128, 384), iota, ratio.reshape(128, 32), np.eye(128, dtype=np.float32))


def _prep_layer_inputs(inp, ls):
    L = len(ls)
    f = lambda a: np.ascontiguousarray(a, dtype=np.float32)
    w_in = inp["w_in"][ls]
    su, sg, pu, pg, q, k, v, ag = np.split(w_in, [256, 512, 768, 1024, 1536, 1664, 1792], axis=-1)
    k0, k1 = k[..., :64], k[..., 64:]
    w1 = f(np.concatenate([su, pu, k0, k0, k1, k1, v], axis=-1))
    w2 = f(np.concatenate([sg, pg, q, ag], axis=-1))
    wo = f(inp["w_out"][ls])
    gluw = f(inp["ssm_glu_w"][ls])
    pw = inp["pool_w"][ls]
    poolw = np.zeros((L, 128, 2, 128), np.float32)
    for ct in range(2):
        for hg in range(2):
            poolw[:, hg * 64:(hg + 1) * 64, ct, hg * 64:(hg + 1) * 64] = pw[:, 2 * ct + hg]
    poolw = poolw.reshape(L, 128, 256)

    a_re, a_im, ldt = inp["ssm_a_re"][ls], inp["ssm_a_im"][ls], inp["ssm_log_dt"][ls]
    b_re, b_im = inp["ssm_b_re"][ls], inp["ssm_b_im"][ls]
    c_re, c_im = inp["ssm_c_re"][ls], inp["ssm_c_im"][ls]

    psm = np.zeros((L, 128, NSM), np.float32)
    psm[:, :, C_GPRE:C_GPRE + 8] = inp["pre_norm_g"][ls].reshape(L, 8, 128).transpose(0, 2, 1)
    psm[:, :, C_GPOST:C_GPOST + 8] = inp["post_norm_g"][ls].reshape(L, 8, 128).transpose(0, 2, 1)
    psm[:, :, C_SD:C_SD + 2] = inp["ssm_d"][ls].reshape(L, 2, 128).transpose(0, 2, 1)
    psm[:, :, C_GB:C_GB + 4] = inp["ssm_glu_b"][ls].reshape(L, 4, 128).transpose(0, 2, 1)
    psm[:, :, C_PSC:C_PSC + 2] = inp["pool_scale"][ls].reshape(L, 2, 128).transpose(0, 2, 1)
    sink = inp["attn_sink"][ls]
    for j in range(4):
        psm[:, 0:64, C_SINK + j] = sink[:, 2 * j][:, None]
        psm[:, 64:128, C_SINK + j] = sink[:, 2 * j + 1][:, None]
    ldt_r = ldt.reshape(L, 2, 2, 8)
    psm[:, :, C_LDTR:C_LDTR + 4] = np.repeat(ldt_r.transpose(0, 3, 1, 2).reshape(L, 8, 4), 16, axis=1)
    ldt_s = ldt.reshape(L, 2, 2, 4, 2)
    psm[:, :, C_LDTS:C_LDTS + 16] = np.repeat(ldt_s.transpose(0, 4, 1, 2, 3).reshape(L, 2, 16), 64, axis=1)
    ars = a_re.reshape(L, 2, 2, 4, 2, 64)
    psm[:, :, C_ARS:C_ARS + 16] = ars.transpose(0, 4, 5, 1, 2, 3).reshape(L, 128, 16)
    ais = a_im.reshape(L, 2, 2, 4, 2, 64)
    psm[:, :, C_AIS:C_AIS + 16] = ais.transpose(0, 4, 5, 1, 2, 3).reshape(L, 128, 16)
    mask = np.zeros((128, 4, 2), np.float32)
    for g8 in range(8):
        mask[g8 * 16:(g8 + 1) * 16, g8 // 2, g8 % 2] = 1.0
    psm[:, :, C_MASK:C_MASK + 8] = mask.reshape(128, 8)[None]
    for pp_ in range(128):
        psm[:, pp_, C_MASK2 + (pp_ // 16) % 2] = 1.0
    psm[:, 96:128, C_MASK3] = 1.0
    for ct in range(2):
        psm[:, 0:64, C_INVW + ct] = 1.0 / (2, 4, 8, 16)[2 * ct]
        psm[:, 64:128, C_INVW + ct] = 1.0 / (2, 4, 8, 16)[2 * ct + 1]

    pch = np.zeros((L, 128, 4, 4, 64), np.float32)
    br = b_re.reshape(L, 2, 2, 8, 64, 16)
    bi = b_im.reshape(L, 2, 2, 8, 64, 16)
    pch[:, :, 0] = br.transpose(0, 3, 5, 1, 2, 4).reshape(L, 128, 4, 64)
    pch[:, :, 1] = bi.transpose(0, 3, 5, 1, 2, 4).reshape(L, 128, 4, 64)
    ar = a_re.reshape(L, 2, 2, 8, 64)
    ai = a_im.reshape(L, 2, 2, 8, 64)
    pch[:, :, 2] = np.repeat(ar.transpose(0, 3, 1, 2, 4).reshape(L, 8, 4, 64), 16, axis=1)
    pch[:, :, 3] = np.repeat(ai.transpose(0, 3, 1, 2, 4).reshape(L, 8, 4, 64), 16, axis=1)
    pch = pch.reshape(L, 128, 1024)

    cpad = np.zeros((L, 2, 2, 64, 2, 2, 4, 8, 16), np.float32)
    cr = c_re.reshape(L, 2, 2, 4, 2, 16, 64)
    ci = c_im.reshape(L, 2, 2, 4, 2, 16, 64)
    for gp in range(4):
        for gg in range(2):
            cpad[:, 0, gg, :, :, :, gp, 2 * gp + gg, :] = cr[:, :, :, gp, gg].transpose(0, 4, 1, 2, 3)
            cpad[:, 1, gg, :, :, :, gp, 2 * gp + gg, :] = ci[:, :, :, gp, gg].transpose(0, 4, 1, 2, 3)
    cpad = cpad.reshape(L, 2, 128, 2048)
    return dict(w1=w1, w2=w2, wo=wo, gluw=gluw, poolw=f(poolw), psmall=psm, pchan=f(pch), cpad=f(cpad))


_NC_CACHE = {}


def _get_nc(L, from_out_first=False):
    key = (L, from_out_first)
    if key not in _NC_CACHE:
        _NC_CACHE[key] = build(L, from_out_first)
    return _NC_CACHE[key]


FUSED = True
DBG = {}


def kernel(**inputs):
    x = np.asarray(inputs["x"], dtype=np.float32)
    B = x.shape[0]
    inp = {k: np.asarray(v, dtype=np.float32) for k, v in inputs.items()}
    cd, cv, cio, crat, cid = _consts()
    xT = [np.ascontiguousarray(x[b].T) for b in range(B)]
    if FUSED:
        groups = [list(range(DEPTH))]
    else:
        groups = [[l] for l in range(DEPTH)]
    for ls in groups:
        nc = _get_nc(len(ls))
        lw = _prep_layer_inputs(inp, ls)
        in_maps = []
        for b in range(B):
            m = dict(lw)
            m.update(xT=xT[b], cdist=cd, cvalid=cv, ciota=cio, cratio=crat, cident=cid)
            in_maps.append(m)
        res = run_bass_kernel_spmd(nc, in_maps, core_ids=list(range(B)))
        xT = [np.asarray(res.results[b]["out"]) for b in range(B)]
    return np.stack([xT[b].T for b in range(B)], axis=0).astype(np.float32)
```

```python
import math
from contextlib import ExitStack

import numpy as np
import concourse.bass as bass
import concourse.mybir as mybir
from concourse.bass_utils import run_bass_kernel_spmd

F32 = mybir.dt.float32
BF16 = mybir.dt.bfloat16
I32 = mybir.dt.int32
ALU = mybir.AluOpType
AF = mybir.ActivationFunctionType

D = 1024
SEQ = 4096
DEPTH = 4
W = 256
NS = SEQ // W
NSM = 93
TCH = 4
NCH = W // TCH
EPS = 1e-6
TWO_PI = 2.0 * math.pi
SC2 = TWO_PI * (1.0 - 2e-6)
SC1 = math.pi * (1.0 - 2e-6)

C_GPRE, C_GPOST, C_SD, C_GB, C_PSC, C_SINK, C_LDTR, C_LDTS, C_ARS, C_AIS, C_MASK, C_INVW = (
    0, 8, 16, 18, 22, 24, 28, 32, 48, 64, 80, 88)
C_MASK2, C_MASK3 = 90, 92


class Prog:
    ENGS = ["sp", "pe", "act", "dve", "pool"]
    SAME_SKIP = {"pe"}

    def __init__(self):
        self.ops = []
        self.lastw = {}
        self.readers = {}
        self.dmacnt = {}

    def add(self, eng, fn, r=(), w=(), dma=None, force=False):
        i = len(self.ops)
        deps = set()
        for k in r:
            if k in self.lastw:
                deps.add(self.lastw[k])
        for k in w:
            if k in self.lastw:
                deps.add(self.lastw[k])
            deps.update(self.readers.get(k, ()))
        for k in r:
            self.readers.setdefault(k, []).append(i)
        for k in w:
            self.lastw[k] = i
            self.readers[k] = []
        op = dict(eng=eng, fn=fn, deps=deps, dma=dma, needed=False, force=force)
        if dma is not None:
            self.dmacnt[dma] = self.dmacnt.get(dma, 0) + 16
            op["dcount"] = self.dmacnt[dma]
        self.ops.append(op)
        return i

    def pe(self, fn, r=(), w=()):
        return self.add("pe", fn, r, w)

    def act(self, fn, r=(), w=()):
        return self.add("act", fn, r, w)

    def dve(self, fn, r=(), w=()):
        return self.add("dve", fn, r, w)

    def pool(self, fn, r=(), w=()):
        return self.add("pool", fn, r, w)

    def dma(self, fn, key, r=(), w=()):
        return self.add("sp", fn, r, w, dma=key)


    def mm(self, out, lhsT, rhs, start, stop, r=(), w=(), force=False):
        return self.add("pe", lambda e: e.matmul(out, lhsT=lhsT, rhs=rhs, start=start, stop=stop), r, w, force=force)

    def actf(self, out, in_, func, r=(), w=(), scale=None, bias=None):
        kw = {}
        if scale is not None:
            kw["scale"] = scale
        if bias is not None:
            kw["bias"] = bias
        return self.add("act", lambda e: e.activation(out=out, in_=in_, func=func, **kw), r, w)

    def amul(self, out, in_, mul, r=(), w=()):
        return self.add("act", lambda e: e.mul(out, in_, mul), r, w)

    def acopy(self, out, in_, r=(), w=()):
        return self.add("act", lambda e: e.copy(out, in_), r, w)

    def tt(self, eng, out, in0, in1, op, r=(), w=()):
        return self.add(eng, lambda e: e.tensor_tensor(out=out, in0=in0, in1=in1, op=op), r, w)

    def ts(self, eng, out, in0, s1, s2, op0, op1=None, r=(), w=()):
        if op1 is None:
            return self.add(eng, lambda e: e.tensor_scalar(out=out, in0=in0, scalar1=s1, scalar2=None, op0=op0), r, w)
        return self.add(eng, lambda e: e.tensor_scalar(out=out, in0=in0, scalar1=s1, scalar2=s2, op0=op0, op1=op1), r, w)

    def stt(self, eng, out, in0, scalar, in1, op0, op1, r=(), w=()):
        return self.add(eng, lambda e: e.scalar_tensor_tensor(out=out, in0=in0, scalar=scalar, in1=in1, op0=op0, op1=op1), r, w)

    def scan(self, out, d0, d1, init, r=(), w=()):
        return self.add("dve", lambda e: e.tensor_tensor_scan(out=out, data0=d0, data1=d1, initial=init,
                                                              op0=ALU.mult, op1=ALU.add), r, w)

    def recip(self, out, in_, r=(), w=()):
        return self.add("dve", lambda e: e.reciprocal(out=out, in_=in_), r, w)

    def copy(self, eng, out, in_, r=(), w=()):
        return self.add(eng, lambda e: e.tensor_copy(out=out, in_=in_), r, w)

    def memset(self, eng, ap, val, w=()):
        return self.add(eng, lambda e: e.memset(ap, val), (), w)

    def dmas(self, out, in_, key, r=(), w=(), q="sp"):
        return self.add(q, lambda e: e.dma_start(out=out, in_=in_), r, w, dma=key)

    def _skip(self, op, dop):
        if op["force"]:
            return False
        return dop["dma"] is None and dop["eng"] == op["eng"] and op["eng"] in self.SAME_SKIP

    def emit(self, nc, es):
        ops = self.ops
        for op in ops:
            for d in op["deps"]:
                dop = ops[d]
                if dop["dma"] is None and not self._skip(op, dop):
                    dop["needed"] = True
        cnt = {e: 0 for e in self.ENGS}
        for op in ops:
            if op["dma"] is None and op["needed"]:
                cnt[op["eng"]] += 1
                op["sig"] = cnt[op["eng"]]
        csem = {e: es.enter_context(nc.semaphore("c_" + e)) for e in ["pe", "act", "dve", "pool"]}
        dsem = {k: es.enter_context(nc.semaphore("d_%d" % i)) for i, k in enumerate(self.dmacnt)}
        block = es.enter_context(nc.Block())
        regs = {"sp": block.sync, "pe": block.tensor, "act": block.scalar, "dve": block.vector,
                "pool": block.gpsimd}
        for eng in self.ENGS:
            def body(e, eng=eng):
                waited = {}
                for op in ops:
                    if op["eng"] != eng:
                        continue
                    for d in sorted(op["deps"]):
                        dop = ops[d]
                        if dop["dma"] is not None:
                            key, val, sem = ("d", dop["dma"]), dop["dcount"], dsem[dop["dma"]]
                        else:
                            if self._skip(op, dop):
                                continue
                            key, val, sem = ("c", dop["eng"]), dop["sig"], csem[dop["eng"]]
                        if waited.get(key, 0) >= val:
                            continue
                        e.wait_ge(sem, val)
                        waited[key] = val
                    ins = op["fn"](e)
                    if op["dma"] is not None:
                        ins.then_inc(dsem[op["dma"]], 16)
                    elif op["needed"]:
                        ins.then_inc(csem[eng], 1)
                if eng == "sp":
                    for k, v in self.dmacnt.items():
                        e.wait_ge(dsem[k], v)
            regs[eng](body)


class Rec:
    def __init__(self):
        self.calls = []

    def __getattr__(self, name):
        def f(*a, **k):
            self.calls.append((name, a, k))
        return f

    def thunks(self, prog):
        return [lambda n=n, a=a, k=k: getattr(prog, n)(*a, **k) for (n, a, k) in self.calls]


def build(L, from_out_first=False):
    nc = bass.Bass("TRN2", target_bir_lowering=False)
    dr = lambda name, shape, dt=F32, kind="ExternalInput": nc.dram_tensor(name, shape, dt, kind=kind).ap()
    xT = dr("xT", [D, SEQ])
    w1 = dr("w1", [L, D, 896])
    w2 = dr("w2", [L, D, 1536])
    wo = dr("wo", [L, D, D])
    gluw = dr("gluw", [L, 256, 512])
    poolw = dr("poolw", [L, 128, 256])
    psmall = dr("psmall", [L, 128, NSM])
    pchan = dr("pchan", [L, 128, 1024])
    cpad = dr("cpad", [L, 2, 128, 2048])
    cdist = dr("cdist", [128, 384])
    cvalid = dr("cvalid", [128, 384])
    ciota = dr("ciota", [128, W])
    cratio = dr("cratio", [128, 32])
    cident = dr("cident", [128, 128])
    out = dr("out", [D, SEQ], kind="ExternalOutput")
    skind = "ExternalOutput" if DBG.get("dump") else "Internal"
    susc = nc.dram_tensor("susc", [128, 2, SEQ], BF16, kind=skind).ap()
    ktsc = nc.dram_tensor("ktsc", [128, 2, SEQ], BF16, kind=skind).ap()
    pusc = nc.dram_tensor("pusc", [128, 2, SEQ + 16], F32, kind=skind).ap()
    vsc = nc.dram_tensor("vsc", [128, 32, 128], BF16, kind=skind).ap()
    yfsc = nc.dram_tensor("yfsc", [128, 2, SEQ], F32, kind=skind).ap()

    if DBG.get("dump"):
        dbg32 = nc.dram_tensor("dbg32", [128, 96], F32, kind="ExternalOutput").ap()
        dbgb = nc.dram_tensor("dbgb", [128, 5, 2048], BF16, kind="ExternalOutput").ap()
    xT_v = xT.rearrange("(kc p) t -> p kc t", p=128)
    out_v = out.rearrange("(kc p) t -> p kc t", p=128)

    es = ExitStack()
    with es:
        sb = lambda name, shape, dt=F32: es.enter_context(nc.sbuf_tensor(name, shape, dt))
        w1b = sb("w1b", [128, 8, 896], BF16)
        w2b = sb("w2b", [128, 8, 1536], BF16)
        wob = sb("wob", [128, 8, 1024], BF16)
        glub = sb("glub", [128, 2, 512], BF16)
        plwb = sb("plwb", [128, 2, 128], BF16)
        WS = sb("WS", [128, 2, 4, 2, 128], BF16)
        WS3 = sb("WS3", [128, 2, 4, 2, 128], BF16)
        WC = sb("WC", [128, 8, 4, 2, 128], BF16)
        Kt = sb("Kt", [128, 2, 4, 128], BF16)
        identb = sb("identb", [128, 128], BF16)
        stage = [sb("stage%d" % i, [128, 1024]) for i in range(2)]
        psm = sb("psm", [128, NSM])
        psm_g = sb("psm_g", [128, 16])
        Eb = sb("Eb", [128, 8, 384], BF16)
        onesb = sb("onesb", [128, 128], BF16)
        onesLR = sb("onesLR", [128, 2, 128], BF16)
        iota = sb("iota", [128, W])
        ratio = sb("ratio", [128, 2, 2, 8])
        zer = sb("zer", [128, 16])
        dmy = sb("dmy", [128, 2])
        rr = sb("rr", [128, 16])
        ff = sb("ff", [128, 16])
        dts = sb("dts", [128, 16])
        k16 = sb("k16", [128, 16], I32)
        u16 = sb("u16", [128, 16])
        off = sb("off", [128, 16])
        bo1 = sb("bo1", [128, 16])
        bo2 = sb("bo2", [128, 16])
        carry = sb("carry", [128, 16, 2])
        esink = sb("esink", [128, 4])
        dtc = sb("dtc", [128, 4])
        xs = [sb("xs%d" % i, [128, 8, W]) for i in range(2)]
        sqb = sb("sqb", [128, 8, W], BF16)
        hT = sb("hT", [128, 8, W], BF16)
        sd = sb("sd", [128, W])
        rstd = sb("rstd", [128, W])
        cA = sb("cA", [128, 8, NCH]); cK = sb("cK", [128, 8, NCH], I32)
        cSn = sb("cSn", [128, 8, NCH]); cCs = sb("cCs", [128, 8, NCH])
        cVr = sb("cVr", [128, 8, NCH]); cVi = sb("cVi", [128, 8, NCH])
        cGr = sb("cGr", [128, 8, NCH]); cGi = sb("cGi", [128, 8, NCH])
        Hre = sb("Hre", [128, 8, NCH + 2], BF16); Him = sb("Him", [128, 8, NCH + 2], BF16)
        ffT = sb("ffT", [128, 16]); rrT = sb("rrT", [128, 16]); offT = sb("offT", [128, 16]); ffp = sb("ffp", [128, 16])
        zsm = sb("zsm", [128, 5, 2, 8])
        carc = sb("carc", [128, 2, 8])
        su_s = [sb("su_s%d" % i, [128, 2, W], BF16) for i in range(2)]
        kt_s = sb("kt_s", [128, 2, W], BF16)
        pu_s = sb("pu_s", [128, 2, W])
        v_s = sb("v_s", [128, 2, 128], BF16)
        yf_s = [sb("yf_s%d" % i, [128, W]) for i in range(2)]
        su_l = [sb("su_l%d" % i, [128, 2, W], BF16) for i in range(2)]
        kt_l = sb("kt_l", [128, 2, 4 * 128], BF16)
        pu_l = sb("pu_l", [128, 2, W + 16])
        v_l = sb("v_l", [128, 4, 128], BF16)
        VLR = sb("VLR", [128, 4, 2, 2, 128], BF16)
        yf_l = [sb("yf_l%d" % i, [128, 2, W]) for i in range(2)]
        gate_s = sb("gate_s", [128, 2, W])
        gate_p = sb("gate_p", [128, 2, W])
        gate_a = sb("gate_a", [128, 4, W])
        qT = sb("qT", [128, 4, W], BF16)
        yg = [sb("yg%d" % i, [128, 2, W], BF16) for i in range(2)]
        yo2 = sb("yo2", [128, 8, W])
        pch = yo2[:, 0:4, :].rearrange("p a (b c) -> p a b c", b=4)
        ytmp = sb("ytmp", [128, W])
        mix = sb("mix", [128, 8, W], BF16)
        sig = sb("sig", [128, W])
        glt = sb("glt", [128, W])
        pA = sb("pA", [128, W + 16])
        pB = sb("pB", [128, W + 16])
        pC = sb("pC", [128, W + 16])
        pD = sb("pD", [128, W + 16])
        pmean = sb("pmean", [128, W])
        mixed = sb("mixed", [128, 2, W], BF16)
        pexp = [sb("pexp%d" % i, [128, 384], BF16) for i in range(2)]
        pT = [sb("pT%d" % i, [128, 384], BF16) for i in range(4)]
        dn = sb("dn", [128, 256])
        o1 = sb("o1", [128, 256])
        banks = [es.enter_context(nc.psum_tensor("bank%d" % i, [128, 512], F32)) for i in range(8)]

        def bview(b, n):
            return banks[b][:, 0:n * 256].rearrange("p (a c) -> p a c", a=n)

        if DBG.get('mem'):
            print('SBUF bytes remaining', nc.sbuf_bytes_remaining)
        p = Prog()
        p_real = p
        WKALL = ["wk%d" % i for i in range(8)]
        CARRYALL = ["carry%d" % i for i in range(16)]

        def pslot(n):
            st = {"i": 0}

            def nxt():
                st["i"] += 1
                return (st["i"] - 1) % n
            return nxt

        fl = lambda ap, pat: ap.rearrange(pat)

        p.memset("dve", onesb[:], 1.0, w=["onesb"])
        p.memset("dve", onesLR[:].rearrange("p a b -> p (a b)"), 0.0, w=["onesLR"])
        p.memset("dve", onesLR[:, 0, 0:64], 1.0, w=["onesLR"])
        p.memset("dve", onesLR[:, 1, 64:128], 1.0, w=["onesLR"])
        p.memset("dve", VLR[:].rearrange("p a b c d -> p (a b c d)"), 0.0, w=["VLR"])
        p.memset("dve", zer[:], 0.0, w=["zer"])
        p.memset("dve", v_l[:].rearrange("p a b -> p (a b)"), 0.0, w=["v_l"])
        p.memset("dve", kt_l[:].rearrange("p a b -> p (a b)"), 0.0, w=["kt_l"])
        p.dmas(iota[:], ciota, "c0", w=["iota"])
        p.dmas(stage[0][:, 0:128], cident, "stg0", w=["stage0", "stage0b"])
        p.copy("dve", identb[:], stage[0][:, 0:128], r=["stage0"], w=["identb"])
        p.dmas(ratio[:].rearrange("p a b c -> p (a b c)"), cratio, "c1", w=["ratio"])
        for ct in range(2):
            p.dmas(pusc[:, ct, 0:8], zer[:, 0:8], "c2", r=["zer"], w=["pusc_pad"])
            p.dmas(pusc[:, ct, SEQ + 8:SEQ + 16], zer[:, 0:8], "c2", r=["zer"], w=["pusc_pad"])
        p.dmas(stage[0][:, 0:384], cdist, "stg0", w=["stage0", "stage0b"])
        p.dmas(stage[1][:, 0:384], cvalid, "stg1", w=["stage1", "stage1b"])
        for h in range(8):
            slope = 2.0 ** (-(h + 1))
            p.actf(stage[0][:, 384:768], stage[0][:, 0:384], AF.Exp, r=["stage0"], w=["stage0b"], scale=-slope)
            p.tt("dve", Eb[:, h, :], stage[0][:, 384:768], stage[1][:, 0:384], ALU.mult,
                 r=["stage0b", "stage1"], w=["Eb"])

        stg_i = [0]

        def stage_load(src_ap, ncols):
            s = stg_i[0] % 2
            stg_i[0] += 1
            key = "stage%d" % s
            p.dmas(stage[s][:, 0:ncols], src_ap, "stg%d" % s, w=[key, key + "b"])
            return s, key

        def prep_small(l):
            p.dmas(psm[:], psmall[l], "psm", w=["psm"])
            p.actf(esink[:], psm[:, C_SINK:C_SINK + 4], AF.Exp, r=["psm"], w=["esink"])

        def prep_weights_p1(l):
            ch = []
            for kc in range(8):
                def f(kc=kc):
                    s, key = stage_load(w1[l, kc * 128:(kc + 1) * 128, :], 896)
                    p.amul(w1b[:, kc, :], stage[s][:, 0:896], psm_g[:, C_GPRE + kc:C_GPRE + kc + 1], r=[key, "psm_g"], w=["w1b"])
                ch.append(f)
            return ch

        def prep_weights_main(l):
            ch = []
            for kc in range(8):
                for hcol in range(2):
                    def f(kc=kc, hcol=hcol):
                        c0 = hcol * 768
                        s, key = stage_load(w2[l, kc * 128:(kc + 1) * 128, c0:c0 + 768], 768)
                        p.amul(w2b[:, kc, c0:c0 + 768], stage[s][:, 0:768], psm_g[:, C_GPRE + kc:C_GPRE + kc + 1],
                               r=[key, "psm_g"], w=["w2b"])
                    ch.append(f)
            for kc in range(8):
                def f(kc=kc):
                    s, key = stage_load(wo[l, kc * 128:(kc + 1) * 128, :], 1024)
                    p.acopy(wob[:, kc, :], stage[s][:, 0:1024], r=[key], w=["wob"])
                ch.append(f)
            for kc in range(2):
                def f(kc=kc):
                    s, key = stage_load(gluw[l, kc * 128:(kc + 1) * 128, :], 512)
                    p.acopy(glub[:, kc, :], stage[s][:, 0:512], r=[key], w=["glub"])
                ch.append(f)

            def f():
                s, key = stage_load(poolw[l], 256)
                p.acopy(plwb[:].rearrange("p a b -> p (a b)"), stage[s][:, 0:256], r=[key], w=["plwb"])
            ch.append(f)
            return ch

        def prep_gains(l):
            p.dmas(psm_g[:], psmall[l, :, 0:16], "psmg", w=["psm_g"])

        CB = [cA, cSn, cCs, cVr, cVi, cGr, cGi]
        CBK = ["cA", "cSn", "cCs", "cVr", "cVi", "cGr", "cGi"]
        GRALL = ["cGr"] + ["cGr_%d" % t_ for t_ in range(8)]
        GIALL = ["cGi"] + ["cGi_%d" % t_ for t_ in range(8)]
        ALLC = CBK + ["cK", "Hre", "Him"] + GRALL[1:] + GIALL[1:]

        def prep_tables(l, d, rec=None, which="AB"):
            p = rec if (rec is not None) else p_real
            fl2 = lambda t: t[:].rearrange("p a b -> p (a b)")
            sl = slice(8 * d, 8 * d + 8)
            if "A" in which:
                fl2 = lambda t: t[:].rearrange("p a b -> p (a b)")
                T = lambda i: fl2(CB[i // 2])[:, (i % 2) * 256:(i % 2) * 256 + 256]
                T3 = lambda i: T(i).rearrange("p (a b) -> p a b", a=4)
                H3 = lambda i: T3(i)[:, 2 * d:2 * d + 2, :]
                R = ALLC + ["yo2", "psm", "dtc"]
                Wk = ALLC
                p.dmas(yo2[:, 0:4, :].rearrange("p a b -> p (a b)"), pchan[l], "pch", w=["yo2"])
                BTre, BTim, AR, AI = (pch[:, 0, :, :], pch[:, 1, :, :], pch[:, 2, :, :], pch[:, 3, :, :])
                p.actf(dtc[:], psm[:, C_LDTR:C_LDTR + 4], AF.Exp, r=["psm"], w=["dtc"])
                dtb = dtc[:].unsqueeze(2).to_broadcast([128, 4, 64])
                kint = fl2(cK)[:, 0:256]
                p.tt("dve", T3(0), AR, dtb, ALU.mult, r=R, w=Wk)
                p.tt("dve", T3(1), AI, dtb, ALU.mult, r=R, w=Wk)
                p.actf(T(2), T(0), AF.Exp, r=R, w=Wk)
                p.ts("dve", kint, T(1), 1.0 / TWO_PI, None, ALU.mult, r=R, w=Wk)
                p.stt("dve", T(3), T(1), 1.0 / TWO_PI, kint, ALU.mult, ALU.subtract, r=R, w=Wk)
                p.actf(T(4), T(3), AF.Sin, r=R, w=Wk, scale=SC2)
                p.actf(T(5), T(3), AF.Sin, r=R, w=Wk, scale=SC1)
                p.actf(T(5), T(5), AF.Square, r=R, w=Wk)
                p.ts("dve", T(5), T(5), -2.0, 1.0, ALU.mult, ALU.add, r=R, w=Wk)
                p.tt("dve", T(6), T(2), T(5), ALU.mult, r=R, w=Wk)
                p.tt("dve", T(7), T(2), T(4), ALU.mult, r=R, w=Wk)
                p.ts("dve", T(8), T(6), -1.0, None, ALU.add, r=R, w=Wk)
                p.tt("dve", T3(0), AR, AR, ALU.mult, r=R, w=Wk)
                p.tt("dve", T3(1), AI, AI, ALU.mult, r=R, w=Wk)
                p.tt("dve", T(0), T(0), T(1), ALU.add, r=R, w=Wk)
                p.recip(T(0), T(0), r=R, w=Wk)
                p.tt("dve", T3(1), T3(8), AR, ALU.mult, r=R, w=Wk)
                p.tt("dve", T3(2), T3(7), AI, ALU.mult, r=R, w=Wk)
                p.tt("dve", T(1), T(1), T(2), ALU.add, r=R, w=Wk)
                p.tt("dve", T(1), T(1), T(0), ALU.mult, r=R, w=Wk)
                p.tt("dve", T3(2), T3(7), AR, ALU.mult, r=R, w=Wk)
                p.tt("dve", T3(3), T3(8), AI, ALU.mult, r=R, w=Wk)
                p.tt("dve", T(2), T(2), T(3), ALU.subtract, r=R, w=Wk)
                p.tt("dve", T(2), T(2), T(0), ALU.mult, r=R, w=Wk)
                ZR, ZI, LR, LI = 1, 2, 6, 7
                mk2 = psm[:, C_MASK2:C_MASK2 + 2].unsqueeze(1).unsqueeze(3).to_broadcast([128, 2, 2, 64])
                for k in range(4):
                    j = (3 - k) if d == 0 else k
                    p.tt("dve", H3(3), H3(ZR), BTre[:, 2 * d:2 * d + 2, :], ALU.mult, r=R, w=Wk)
                    p.tt("dve", H3(4), H3(ZI), BTim[:, 2 * d:2 * d + 2, :], ALU.mult, r=R, w=Wk)
                    p.tt("dve", H3(3), H3(3), H3(4), ALU.subtract, r=R, w=Wk)
                    p.tt("dve", H3(4), H3(ZR), BTim[:, 2 * d:2 * d + 2, :], ALU.mult, r=R, w=Wk)
                    p.tt("dve", H3(5), H3(ZI), BTre[:, 2 * d:2 * d + 2, :], ALU.mult, r=R, w=Wk)
                    p.tt("dve", H3(4), H3(4), H3(5), ALU.add, r=R, w=Wk)
                    for ri, src in ((0, 3), (1, 4)):
                        p.tt("dve", WS[:, :, j, ri, :].rearrange("p a (b c) -> p a b c", b=2),
                             H3(src).unsqueeze(2).to_broadcast([128, 2, 2, 64]), mk2, ALU.mult, r=R, w=["WS"])
                    if k < 3:
                        p.tt("dve", H3(3), H3(ZR), H3(LR), ALU.mult, r=R, w=Wk)
                        p.tt("dve", H3(4), H3(ZI), H3(LI), ALU.mult, r=R, w=Wk)
                        p.tt("dve", H3(5), H3(ZR), H3(LI), ALU.mult, r=R, w=Wk)
                        p.tt("dve", H3(9), H3(ZI), H3(LR), ALU.mult, r=R, w=Wk)
                        p.tt("dve", H3(ZR), H3(3), H3(4), ALU.subtract, r=R, w=Wk)
                        p.tt("dve", H3(ZI), H3(5), H3(9), ALU.add, r=R, w=Wk)
                p.ts("dve", WS3[64:128].rearrange("p a b c d -> p (a b c d)"), WS[64:128].rearrange("p a b c d -> p (a b c d)"),
                     psm[64:128, C_MASK3:C_MASK3 + 1], None, ALU.mult, r=["WS", "psm"], w=["WS3"])

                sl = slice(8 * d, 8 * d + 8)
                p.actf(dts[:], psm[:, C_LDTS:C_LDTS + 16], AF.Exp, r=["psm"], w=["dts"])
                p.tt("dve", u16[:], psm[:, C_ARS:C_ARS + 16], dts[:], ALU.mult, r=["psm", "dts"], w=["u16"])
                p.actf(rr[:], u16[:], AF.Exp, r=["u16"], w=["rr"])
                p.tt("dve", u16[:], psm[:, C_AIS:C_AIS + 16], dts[:], ALU.mult, r=["psm", "dts", "rr"], w=["u16"])
                p.ts("dve", k16[:], u16[:], 1.0 / TWO_PI, None, ALU.mult, r=["u16"], w=["k16"])
                p.stt("dve", ffp[:], u16[:], 1.0 / TWO_PI, k16[:], ALU.mult, ALU.subtract, r=["u16", "k16"], w=["ffp"])
                p.actf(off[:], ffp[:], AF.Sin, r=["ffp"], w=["off"], scale=SC2)
                p.actf(bo1[:], ffp[:], AF.Sin, r=["ffp"], w=["bo1"], scale=SC1)
                p.actf(bo1[:], bo1[:], AF.Square, r=["bo1"], w=["bo1"])
                p.ts("dve", bo1[:], bo1[:], -2.0, 1.0, ALU.mult, ALU.add, r=["bo1"], w=["bo1"])
                p.tt("dve", zsm[:, 1, 0, :], rr[:, sl], bo1[:, sl], ALU.mult, r=["rr", "bo1"], w=["zsm"])
                p.tt("dve", zsm[:, 1, 1, :], rr[:, sl], off[:, sl], ALU.mult, r=["rr", "off"], w=["zsm"])
                for k in range(1, 4):
                    a_r, a_i = zsm[:, k, 0, :], zsm[:, k, 1, :]
                    l_r, l_i = zsm[:, 1, 0, :], zsm[:, 1, 1, :]
                    p.tt("dve", bo2[:, 0:8], a_r, l_r, ALU.mult, r=["zsm"], w=["bo2"])
                    p.tt("dve", bo2[:, 8:16], a_i, l_i, ALU.mult, r=["zsm"], w=["bo2"])
                    p.tt("dve", zsm[:, k + 1, 0, :], bo2[:, 0:8], bo2[:, 8:16], ALU.subtract, r=["bo2"], w=["zsm"])
                    p.tt("dve", bo2[:, 0:8], a_r, l_i, ALU.mult, r=["zsm"], w=["bo2"])
                    p.tt("dve", bo2[:, 8:16], a_i, l_r, ALU.mult, r=["zsm"], w=["bo2"])
                    p.tt("dve", zsm[:, k + 1, 1, :], bo2[:, 0:8], bo2[:, 8:16], ALU.add, r=["bo2"], w=["zsm"])
                sgn = 1.0 if d == 0 else -1.0
                p.ts("dve", k16[:, sl], ffp[:, sl], sgn * TCH, None, ALU.mult, r=["ffp"], w=["k16"])
                p.stt("dve", ffT[:, sl], ffp[:, sl], sgn * TCH, k16[:, sl], ALU.mult, ALU.subtract, r=["ffp", "k16"], w=["ffT"])
                p.tt("dve", rrT[:, sl], rr[:, sl], rr[:, sl], ALU.mult, r=["rr"], w=["rrT"])
                p.tt("dve", rrT[:, sl], rrT[:, sl], rrT[:, sl], ALU.mult, r=["rrT"], w=["rrT"])


            p = p_real
            if "B" not in which:
                return
            p.dmas(stage[0][:, 0:1024], cpad[l, 0, :, d * 1024:(d + 1) * 1024], "stg0", w=["stage0", "stage0b"])
            p.dmas(stage[1][:, 0:1024], cpad[l, 1, :, d * 1024:(d + 1) * 1024], "stg1", w=["stage1", "stage1b"])
            c3 = lambda t: t[:, 0:1024].rearrange("p (a b) -> p a b", a=8)
            cre, cim = c3(stage[0]), c3(stage[1])
            t1 = yo2[:, 0:4, :].rearrange("p a b -> p (a b)").rearrange("p (a b) -> p a b", a=8)
            t2 = yo2[:, 4:8, :].rearrange("p a b -> p (a b)").rearrange("p (a b) -> p a b", a=8)
            RW = ["yo2", "stage0", "stage1", "zsm"]
            Ta0 = cGr[:].bitcast(BF16)
            Tb0 = cGi[:].bitcast(BF16)
            assert tuple(Ta0.shape) == (128, 8, 128), Ta0.shape
            p.acopy(Ta0, cre, r=["stage0"] + ALLC, w=GRALL)
            p.amul(Tb0, cim, -1.0, r=["stage1"] + ALLC, w=GIALL)

            x2 = lambda t, o: fl2(t)[:, o:o + 256].rearrange("p (a b) -> p a b", a=2)
            Xs = [x2(Hre, 0), x2(Hre, 256), x2(Him, 0), x2(Him, 256)]
            XK = ["Xs0", "Xs1", "Xs2", "Xs3"]
            p.memset("dve", fl2(Hre), 0.0, w=["Hre", "Xs0", "Xs1"])
            p.memset("dve", fl2(Him), 0.0, w=["Him", "Xs2", "Xs3"])
            def kg_transpose(ct, tau, gp):
                j = (3 - tau) if d == 0 else tau
                tb = proj_slot()
                tkey = "b%d" % tb
                if gp < 3:
                    rows, ncol, cbase, src = slice(32 * gp, 32 * gp + 32), 32, 32 * gp, WS
                else:
                    rows, ncol, cbase, src = slice(64, 128), 64, 64, WS3
                for ri in range(2):
                    p.mm(banks[tb][:, ri * 64:ri * 64 + ncol], src[rows, ct, j, ri, :], identb[rows, cbase:cbase + ncol],
                         True, True, r=["WS", "WS3", "identb"], w=[tkey])
                p.acopy(Xs[gp][:, :, cbase:cbase + ncol],
                        banks[tb][:, 0:128].rearrange("p (a b) -> p a b", a=2)[:, :, 0:ncol], r=[tkey], w=[XK[gp]])

            steps = [(ct, tau, gp) for ct in range(2) for tau in range(4) for gp in range(4)]
            kg_transpose(*steps[0])
            for si, (ct, tau, gp) in enumerate(steps):
                if si + 1 < len(steps):
                    kg_transpose(*steps[si + 1])
                kb_ = 2
                tile = ct * 4 + gp
                p.mm(banks[kb_][:, 0:128], Xs[gp][:, 0, :], Ta0[:, tile, :], gp == 0, False, r=[XK[gp], "cGr"], w=["b2"])
                p.mm(banks[kb_][:, 0:128], Xs[gp][:, 1, :], Tb0[:, tile, :], False, gp == 3, r=[XK[gp], "cGi"], w=["b2"])
                if gp == 3:
                    p.acopy(Kt[:, ct, tau, :], banks[kb_][:, 0:128], r=["b2"], w=["Kt"])
            for i in range(4):
                k = (i + 1) if d == 0 else (TCH - i)
                zr = zsm[:, k, 0, :].unsqueeze(2).to_broadcast([128, 8, 128])
                zi = zsm[:, k, 1, :].unsqueeze(2).to_broadcast([128, 8, 128])
                p.tt("dve", t1, cre, zr, ALU.mult, r=RW, w=["yo2"])
                p.tt("dve", t2, cim, zi, ALU.mult, r=RW, w=["yo2"])
                p.tt("dve", WC[:, :, i, 0, :], t1, t2, ALU.subtract, r=RW, w=["WC"])
                p.tt("dve", t1, cre, zi, ALU.mult, r=RW, w=["yo2"])
                p.tt("dve", t2, cim, zr, ALU.mult, r=RW, w=["yo2"])
                p.stt("dve", WC[:, :, i, 1, :], t1, -1.0, t2, ALU.mult, ALU.subtract, r=RW, w=["WC"])
            p.memset("dve", fl2(Hre), 0.0, w=["Hre", "Xs0", "Xs1"])
            p.memset("dve", fl2(Him), 0.0, w=["Him", "Xs2", "Xs3"])
            p.memset("dve", carc[:].rearrange("p a b -> p (a b)"), 0.0, w=["carc"])

        proj_slot = pslot(2)

        def norm_sq(xs_t, xkey):
            p.actf(sqb[:].rearrange("p a b -> p (a b)"), xs_t[:].rearrange("p a b -> p (a b)"), AF.Square,
                   r=[xkey], w=["sqb"])

        def norm_a(xs_t, xkey, sq=True):
            if sq:
                norm_sq(xs_t, xkey)
            b = proj_slot()
            bk = "b%d" % b
            for kc in range(8):
                p.mm(banks[b][:, 0:W], onesb[:], sqb[:, kc, :], kc == 0, kc == 7, r=["sqb", "onesb"], w=[bk])
            p.actf(sd[:], banks[b][:, 0:W], AF.Ln, r=[bk], w=["sd"], scale=1.0 / D, bias=EPS)
            p.actf(rstd[:], sd[:], AF.Exp, r=["sd"], w=["rstd"], scale=-0.5)

        def norm_b(xs_t, xkey, eng="dve"):
            p.tt(eng, hT[:], xs_t[:], rstd[:].unsqueeze(1).to_broadcast([128, 8, W]), ALU.mult,
                 r=[xkey, "rstd"], w=["hT"])

        def norm_slab(xs_t, xkey):
            norm_a(xs_t, xkey)
            norm_b(xs_t, xkey)

        def proj_fm(wb, wkey, col0, nt=2):
            b = proj_slot()
            key = "b%d" % b
            for t in range(nt):
                for kc in range(8):
                    p.mm(banks[b][:, t * W:(t + 1) * W], wb[:, kc, col0 + t * 128:col0 + (t + 1) * 128], hT[:, kc, :],
                         kc == 0, kc == 7, r=["hT", wkey], w=[key])
            return bview(b, nt), key

        tab_slot = pslot(2)
        pp_slot = pslot(2)
        brbi_slot = pslot(2)

        def ssm_slab3(direction, s, su_t, sukey, after_ct, fillers=(), after_s=None):
            d = direction
            fillers = list(fillers)
            sl = slice(8 * d, 8 * d + 8)
            rev = (d == 1)
            nops = 30
            per_op = max(1, -(-len(fillers) // nops)) if fillers else 0
            fl2 = lambda t: t[:].rearrange("p a b -> p (a b)")

            def fill(n=1):
                for _ in range(n * per_op):
                    if fillers:
                        fillers.pop(0)()

            for tile in range(8):
                ct, gp = tile // 4, tile % 4
                if gp < 3:
                    rows, src = slice(32 * gp, 32 * gp + 32), WS
                else:
                    rows, src = slice(64, 128), WS3
                if gp not in DBG.get("gps", (0, 1, 2, 3)):
                    continue
                for ri in range(2):
                    for j in range(TCH):
                        p.mm(banks[2 + ri][:, tile * NCH:(tile + 1) * NCH], src[rows, ct, j, ri, :], su_t[rows, ct, j:W:TCH],
                             j == 0, j == TCH - 1, r=[sukey, "WS", "WS3"], w=["b%d" % (2 + ri), "rgser"],
                             force=(ri == 0 and j == 0))
            Sre = banks[2][:, 0:8 * NCH].rearrange("p (a b) -> p a b", a=8)
            Sim = banks[3][:, 0:8 * NCH].rearrange("p (a b) -> p a b", a=8)
            c0 = float(s * NCH)
            p.ts("dve", k16[:, sl], ffT[:, sl], c0, None, ALU.mult, r=["ffT"], w=["k16"])
            p.stt("dve", offT[:, sl], ffT[:, sl], c0, k16[:, sl], ALU.mult, ALU.subtract, r=["ffT", "k16"], w=["offT"])
            bc_t = lambda t: t[:, sl].unsqueeze(2).to_broadcast([128, 8, NCH])
            io_b = iota[:, 0:NCH].unsqueeze(1).to_broadcast([128, 8, NCH])
            p.tt("dve", cA[:], io_b, bc_t(ffT), ALU.mult, r=["iota", "ffT", "cA"], w=["cA"])
            p.tt("dve", cK[:], cA[:], bc_t(offT), ALU.add, r=["cA", "offT"], w=["cK"])
            p.tt("dve", cA[:], cA[:], bc_t(offT), ALU.add, r=["cA", "offT", "cK"], w=["cA"])
            p.tt("dve", cA[:], cA[:], cK[:], ALU.subtract, r=["cA", "cK"], w=["cA"])
            p.actf(fl2(cSn), fl2(cA), AF.Sin, r=["cA"], w=["cSn"], scale=SC2)
            p.actf(fl2(cCs), fl2(cA), AF.Sin, r=["cA"], w=["cCs"], scale=SC1)
            p.actf(fl2(cCs), fl2(cCs), AF.Square, r=["cCs"], w=["cCs"])
            p.actf(fl2(cCs), fl2(cCs), AF.Identity, r=["cCs"], w=["cCs"], scale=-2.0, bias=1.0)
            if after_s is not None:
                after_s()
            fill(4)
            p.tt("dve", cVr[:], Sre, cCs[:], ALU.mult, r=["b2", "cCs"], w=["cVr"])
            p.tt("dve", cA[:], Sim, cSn[:], ALU.mult, r=["b3", "cSn", "cA"], w=["cA"])
            fill()
            p.tt("dve", cVi[:], Sim, cCs[:], ALU.mult, r=["b3", "cCs"], w=["cVi"])
            p.tt("dve", cGr[:], Sre, cSn[:], ALU.mult, r=["b2", "cSn"], w=GRALL)
            fill()
            p.tt("dve", cVr[:], cVr[:], cA[:], ALU.add, r=["cVr", "cA"], w=["cVr"])
            p.tt("dve", cVi[:], cVi[:], cGr[:], ALU.subtract, r=["cVi"] + GRALL, w=["cVi"])
            fill()
            fw = (lambda ap: ap[:, ::-1]) if rev else (lambda ap: ap)
            for tile in range(8):
                st = 8 * d + tile
                rb = rrT[:, st:st + 1].to_broadcast([128, NCH])
                p.scan(fw(cGr[:, tile, :]), rb, fw(cVr[:, tile, :]), carc[:, 0, tile:tile + 1], r=["cVr", "rrT", "carc"],
                       w=["cGr_%d" % tile])
                p.scan(fw(cGi[:, tile, :]), rb, fw(cVi[:, tile, :]), carc[:, 1, tile:tile + 1], r=["cVi", "rrT", "carc"],
                       w=["cGi_%d" % tile])
                if tile % 2 == 1:
                    fill()
            lastc = 0 if rev else NCH - 1
            GRK = ["cGr_%d" % t_ for t_ in range(8)]
            GIK = ["cGi_%d" % t_ for t_ in range(8)]
            p.copy("dve", carc[:, 0, :], cGr[:, :, lastc], r=GRK, w=["carc"])
            p.copy("dve", carc[:, 1, :], cGi[:, :, lastc], r=GIK, w=["carc"])
            hcols = slice(1, NCH + 1)
            if not rev:
                hdst, hsrc, hrhs = 0, NCH, slice(0, NCH)
            else:
                hdst, hsrc, hrhs = NCH + 1, 1, slice(2, NCH + 2)
            p.copy("dve", Hre[:, :, hdst], Hre[:, :, hsrc], r=["Hre"], w=["Hre"])
            p.copy("dve", Him[:, :, hdst], Him[:, :, hsrc], r=["Him"], w=["Him"])
            fill()
            p.tt("dve", cVr[:], cGr[:], cCs[:], ALU.mult, r=GRK + ["cCs", "cVr"], w=["cVr"])
            p.tt("dve", cVi[:], cGi[:], cSn[:], ALU.mult, r=GIK + ["cSn", "cVi"], w=["cVi"])
            fill()
            p.tt("dve", Hre[:, :, hcols], cVr[:], cVi[:], ALU.subtract, r=["cVr", "cVi", "Hre"], w=["Hre"])
            p.tt("dve", cVr[:], cGr[:], cSn[:], ALU.mult, r=GRK + ["cSn", "cVr", "Hre"], w=["cVr"])
            fill()
            p.tt("dve", cVi[:], cGi[:], cCs[:], ALU.mult, r=GIK + ["cCs", "cVi", "Hre"], w=["cVi"])
            fill()
            p.tt("dve", Him[:, :, hcols], cVr[:], cVi[:], ALU.add, r=["cVr", "cVi", "Him"], w=["Him"])
            while fillers:
                fillers.pop(0)()

            def part_b():
              for ct in range(2):
                  ykey = "b%d" % (4 + ct)
                  for i in range(TCH):
                      yo_ = banks[4 + ct][:, i:W:TCH]
                      js = list(range(0, i + 1)) if not rev else list(range(i, TCH))
                      nmm = len(js) + 8
                      n = 0
                      for j in js:
                          tau = abs(i - j)
                          p.mm(yo_, Kt[:, ct, tau, :], su_t[:, ct, j:W:TCH], n == 0, n == nmm - 1, r=[sukey, "Kt"], w=[ykey])
                          n += 1
                      for gp in range(4):
                          tile = ct * 4 + gp
                          p.mm(yo_, WC[:, tile, i, 0, :], Hre[:, tile, hrhs], n == 0, n == nmm - 1, r=["WC", "Hre"], w=[ykey])
                          n += 1
                          p.mm(yo_, WC[:, tile, i, 1, :], Him[:, tile, hrhs], n == 0, n == nmm - 1, r=["WC", "Him"], w=[ykey])
                          n += 1
                  after_ct(ct, banks[4 + ct][:, 0:W], ykey)
            return part_b

        def p1_pieces(s, src_v, srckeyf):
            c0, c1 = s * W, (s + 1) * W
            xb = s % 2
            xkey = "xs%d" % xb
            sus, suk = su_s[s % 2], "su_s%d" % (s % 2)
            pcs = []

            def f_norm():
                norm_b(xs[xb], xkey, eng=DBG.get("p1_norm_eng", "pool"))
            pcs.append(f_norm)

            def f_sq_next():
                if s + 1 < NS:
                    norm_sq(xs[(s + 1) % 2], "xs%d" % ((s + 1) % 2))
            pcs.append(f_sq_next)

            def f_su():
                ps, key = proj_fm(w1b, "w1b", 0)
                p.acopy(sus[:], ps, r=[key], w=[suk])
                p.dmas(susc[:, :, c0:c1], sus[:], "st_su%d" % (s % 2), r=[suk], w=["susc"])
            pcs.append(f_su)

            def f_pu():
                ps, key = proj_fm(w1b, "w1b", 256)
                p.acopy(pu_s[:], ps, r=[key], w=["pu_s"])
                p.dmas(pusc[:, :, 8 + c0:8 + c1], pu_s[:], "st_pu", r=["pu_s"], w=["pusc"])
            pcs.append(f_pu)

            def f_kt():
                ps, key = proj_fm(w1b, "w1b", 512)
                p.acopy(kt_s[:], ps, r=[key], w=["kt_s"])
                p.dmas(ktsc[:, :, c0:c1], kt_s[:], "st_kt", r=["kt_s"], w=["ktsc"])
            pcs.append(f_kt)

            def f_v():
                b = proj_slot()
                key = "b%d" % b
                for blk in range(2):
                    for kc in range(8):
                        p.mm(banks[b][:, blk * 128:(blk + 1) * 128], hT[:, kc, blk * 128:(blk + 1) * 128], w1b[:, kc, 768:896],
                             kc == 0, kc == 7, r=["hT", "w1b"], w=[key])
                p.acopy(v_s[:], banks[b][:, 0:256].rearrange("p (a c) -> p a c", a=2), r=[key], w=["v_s"])
                p.dmas(vsc[:, 2 * s:2 * s + 2, :], v_s[:], "st_v", r=["v_s"], w=["vsc"])
            pcs.append(f_v)

            def f_next_norm():
                if s + 1 < NS:
                    nb = (s + 1) % 2
                    norm_a(xs[nb], "xs%d" % nb, sq=False)
                p.actf(dmy[:, 0:1], zer[:, 0:1], AF.Sin, r=["zer"], w=["dmy"])
                if s + 2 < NS:
                    p.dmas(xs[xb][:], src_v[:, :, c1 + W:c1 + 2 * W], "xs%d" % xb, r=[srckeyf(s + 2)], w=["xs%d" % xb])
            pcs.append(f_next_norm)
            return pcs

        def interleave(pieces, thunks):
            thunks = list(thunks)
            per = -(-len(thunks) // max(len(pieces), 1))
            for pc in pieces:
                pc()
                for _ in range(per):
                    if thunks:
                        thunks.pop(0)()
            while thunks:
                thunks.pop(0)()

        def p1_prologue(l, src_v, srckeyf, thunks=()):
            def f0():
                p.dmas(xs[0][:], src_v[:, :, 0:W], "xs0", r=[srckeyf(0)], w=["xs0"])
                p.dmas(xs[1][:], src_v[:, :, W:2 * W], "xs1", r=[srckeyf(1)], w=["xs1"])
                norm_a(xs[0], "xs0")
            interleave([f0] + p1_pieces(0, src_v, srckeyf), thunks)

        def phase1(l, src_v, srckeyf, deferred=()):
            deferred = list(deferred)
            pend = None
            for s in range(DBG.get("p1", NS)):
                c0, c1 = s * W, (s + 1) * W
                nxt_p = p1_pieces(s + 1, src_v, srckeyf) if s + 1 < NS else []
                if nxt_p:
                    nxt_p.pop(0)()
                fill = []
                for _ in range(2):
                    if deferred:
                        fill.append(deferred.pop(0))
                fill = nxt_p + fill

                def after_f(ct, ps, key, c0=c0, c1=c1):
                    p.acopy(yf_s[ct][:], ps, r=[key], w=["yf_s%d" % ct])
                    p.dmas(yfsc[:, ct, c0:c1], yf_s[ct][:], "st_yf%d" % ct, r=["yf_s%d" % ct], w=["yfsc"])
                pend = ssm_slab3(0, s, su_s[s % 2], "su_s%d" % (s % 2), after_f, fillers=fill, after_s=pend)
            if pend is not None:
                pend()
            while deferred:
                deferred.pop(0)()

        s_slot = pslot(2)
        od_slot = pslot(2)
        od_bank = lambda: 0 + proj_slot()
        pt_slot = pslot(4)
        pe_slot = pslot(2)

        def attention_pairs(s, nb_lo):
            v3 = lambda ap: ap.rearrange("p (a b) -> p a b", a=2)
            pairs = [(nloc, jp, jj) for nloc in range(2) for jp in range(2) for jj in range(2)]
            state = {}

            def stage1(nloc, jp, jj):
                n = 2 * s + nloc
                kbs = [kb for kb in (n - 1, n, n + 1) if 0 <= kb < 32]
                rel0 = kbs[0] - (n - 1)
                nk = len(kbs)
                qc0 = nloc * 128
                j = jp * 2 + jj
                kv = j // 2
                pts = []
                for hh in range(2):
                    h = 2 * j + hh
                    ss = s_slot()
                    skey = "b%d" % (6 + ss)
                    s_ps = banks[6 + ss]
                    rows = slice(64 * hh, 64 * hh + 64)
                    for ki_, kb in enumerate(kbs):
                        kl = kb - nb_lo
                        p.mm(s_ps[:, ki_ * 128:(ki_ + 1) * 128], kt_l[rows, kv, kl * 128:(kl + 1) * 128],
                             qT[rows, j, qc0:qc0 + 128], True, True, r=["kt_l", "qT"], w=[skey])
                    pe_i = pe_slot()
                    pe_, pk = pexp[pe_i], "pexp%d" % pe_i
                    p.actf(pe_[:, 0:nk * 128], s_ps[:, 0:nk * 128], AF.Exp, r=[skey], w=[pk])
                    pi_ = pt_slot()
                    pt_, tk = pT[pi_], "pT%d" % pi_
                    p.tt("dve", pt_[:, 0:nk * 128], pe_[:, 0:nk * 128], Eb[:, h, rel0 * 128:(rel0 + nk) * 128], ALU.mult,
                         r=[pk, "Eb"], w=[tk])
                    pts.append((pt_, tk))
                state[(nloc, jp, jj)] = (pts, kbs)

            def stage2(nloc, jp, jj):
                pts, kbs = state.pop((nloc, jp, jj))
                nk = len(kbs)
                qc0 = nloc * 128
                j = jp * 2 + jj
                kv = j // 2
                if jj == 0:
                    state[("od", nloc, jp)] = proj_slot()
                ob = state[("od", nloc, jp)]
                okey = "b%d" % ob
                odv = banks[ob][:].rearrange("p (j t c) -> p j t c", j=2, t=2)
                for t in range(2):
                    for hh in range(2):
                        pt_, tk = pts[hh]
                        for ki_, kb in enumerate(kbs):
                            kl = kb - nb_lo
                            first = (hh == 0 and ki_ == 0)
                            last = (hh == 1 and ki_ == nk - 1)
                            lhs = VLR[:, kl, kv, hh, :] if t == 0 else onesLR[:, hh, :]
                            p.mm(odv[:, jj, t, :], lhs, pt_[:, ki_ * 128:(ki_ + 1) * 128], first, last,
                                 r=[tk, "VLR", "onesLR"], w=[okey])
                if jj == 1:
                    state.pop(("od", nloc, jp))
                    p.tt("dve", v3(dn[:]), odv[:, :, 1, :], esink[:, 2 * jp:2 * jp + 2].unsqueeze(2).to_broadcast([128, 2, 128]), ALU.add,
                         r=[okey, "esink"], w=["dn"])
                    p.actf(dn[:], dn[:], AF.Ln, r=["dn"], w=["dn"])
                    p.actf(dn[:], dn[:], AF.Exp, r=["dn"], w=["dn"], scale=-1.0)
                    p.tt("dve", v3(o1[:]), odv[:, :, 0, :], v3(dn[:]), ALU.mult, r=[okey, "dn"], w=["o1"])
                    p.tt("dve", mix[:, 4 + 2 * jp:6 + 2 * jp, qc0:qc0 + 128], v3(o1[:]), gate_a[:, 2 * jp:2 * jp + 2, qc0:qc0 + 128],
                         ALU.mult, r=["o1", "gate_a"], w=["mix"])

            pcs = [lambda: stage1(*pairs[0])]
            for k in range(1, len(pairs)):
                pcs.append(lambda k=k: (stage1(*pairs[k]), stage2(*pairs[k - 1])))
            pcs.append(lambda: stage2(*pairs[-1]))
            return pcs

        def pool_slab(s):
            first, last = (s == 0), (s == NS - 1)
            n = W + 16
            lo, hi = slice(0, 64), slice(64, 128)
            for ct in range(2):
                Xc = pu_l[:, ct, :]
                p.tt("pool", pA[:, 1:n], Xc[:, 0:n - 1], Xc[:, 1:n], ALU.add, r=["pu_l"], w=["pA"])
                if ct == 0:
                    p.tt("pool", pB[hi, 2:n - 1], pA[hi, 1:n - 2], pA[hi, 3:n], ALU.add, r=["pA"], w=["pB"])
                    srcs = ((lo, pA, "pA"), (hi, pB, "pB"))
                else:
                    p.tt("pool", pB[:, 2:n - 1], pA[:, 1:n - 2], pA[:, 3:n], ALU.add, r=["pA"], w=["pB"])
                    p.tt("pool", pC[:, 4:n - 3], pB[:, 2:n - 5], pB[:, 6:n - 1], ALU.add, r=["pB"], w=["pC"])
                    p.tt("pool", pD[hi, 8:n - 7], pC[hi, 4:n - 11], pC[hi, 12:n - 3], ALU.add, r=["pC"], w=["pD"])
                    srcs = ((lo, pC, "pC"), (hi, pD, "pD"))
                for (rows, buf, bk) in srcs:
                    p.ts("pool", pmean[rows, :], buf[rows, 8:8 + W], psm[rows, C_INVW + ct:C_INVW + ct + 1], None, ALU.mult,
                         r=[bk, "psm"], w=["pmean"])
                if first:
                    p.tt("pool", pmean[:, 0:8], pmean[:, 0:8], ratio[:, ct, 0, :], ALU.mult, r=["pmean", "ratio"], w=["pmean"])
                if last:
                    p.tt("pool", pmean[:, W - 8:W], pmean[:, W - 8:W], ratio[:, ct, 1, :], ALU.mult, r=["pmean", "ratio"], w=["pmean"])
                p.tt("pool", mixed[:, ct, :], pmean[:], Xc[:, 8:8 + W], ALU.subtract, r=["pmean", "pu_l"], w=["mixed"])
                b = proj_slot()
                key = "b%d" % b
                p.mm(banks[b][:, 0:W], plwb[:, ct, :], mixed[:, ct, :], True, True, r=["mixed", "plwb"], w=[key])
                p.stt("dve", mix[:, 2 + ct, :], banks[b][:, 0:W], psm[:, C_PSC + ct:C_PSC + ct + 1], gate_p[:, ct, :], ALU.mult, ALU.mult,
                      r=[key, "psm", "gate_p"], w=["mix"])

        def ssm_b_loads(s):
            c0, c1 = s * W, (s + 1) * W
            pb = s % 2
            p.dmas(su_l[pb][:], susc[:, :, c0:c1], "ld_su%d" % pb, r=["susc"], w=["su_l%d" % pb])
            p.dmas(yf_l[pb][:], yfsc[:, :, c0:c1], "ld_yf%d" % pb, r=["yfsc"], w=["yf_l%d" % pb])

        def ssm_b(s, fillers=(), after_s=None):
            pb = s % 2

            def after_b(ct, ps, key):
                p.tt("dve", ytmp[:], ps, yf_l[pb][:, ct, :], ALU.add, r=[key, "yf_l%d" % pb], w=["ytmp"])
                p.stt("dve", ytmp[:], su_l[pb][:, ct, :], psm[:, C_SD + ct:C_SD + ct + 1], ytmp[:], ALU.mult, ALU.add,
                      r=["su_l%d" % pb, "psm", "ytmp"], w=["ytmp"])
                p.actf(yg[pb][:, ct, :], ytmp[:], AF.Gelu_apprx_tanh, r=["ytmp"], w=["yg%d" % pb])
            return ssm_slab3(1, s, su_l[pb], "su_l%d" % pb, after_b, fillers=fillers, after_s=after_s)

        out_stores = []

        def flush_stores():
            while out_stores:
                out_stores.pop(0)()

        def main_loads(s):
            c0 = s * W
            nb_lo = 2 * s - 1
            p.dmas(pu_l[:], pusc[:, :, c0:c0 + W + 16], "ld_pu", r=["pusc", "pusc_pad"], w=["pu_l"])
            blo, bhi = max(nb_lo, 0), min(nb_lo + 4, 32)
            p.dmas(kt_l[:, :, (blo - nb_lo) * 128:(bhi - nb_lo) * 128], ktsc[:, :, blo * 128:bhi * 128], "ld_kt",
                   r=["ktsc"], w=["kt_l"])
            p.dmas(v_l[:, blo - nb_lo:bhi - nb_lo, :], vsc[:, blo:bhi, :], "ld_v", r=["vsc"], w=["v_l"])

        def main_pieces(s, src_v, srckeyf):
            c0, c1 = s * W, (s + 1) * W
            xb = s % 2
            xkey = "xs%d" % xb
            nb_lo = 2 * s - 1
            pb = s % 2
            pcs = []

            def f_loads():
                if s - 1 >= 0:
                    nb = (s - 1) % 2
                    p.dmas(xs[nb][:], src_v[:, :, c0 - W:c0], "xs%d" % nb, r=[srckeyf(s - 1)], w=["xs%d" % nb])
                for kv in range(2):
                    for hh in range(2):
                        p.copy("pool", VLR[:, :, kv, hh, 64 * hh:64 * hh + 64], v_l[:, :, 64 * kv:64 * kv + 64],
                               r=["v_l"], w=["VLR"])
                norm_b(xs[xb], xkey)
            pcs.append(f_loads)


            def f_gs():
                ps, key = proj_fm(w2b, "w2b", 0)
                p.actf(gate_s[:], ps, AF.Silu, r=[key], w=["gate_s"])
            pcs.append(f_gs)

            def f_gp():
                ps, key = proj_fm(w2b, "w2b", 256)
                p.actf(gate_p[:], ps, AF.Silu, r=[key], w=["gate_p"])
            pcs.append(f_gp)
            for jp in range(2):
                def f_q(jp=jp):
                    ps, key = proj_fm(w2b, "w2b", 512 + jp * 256)
                    p.amul(qT[:, 2 * jp:2 * jp + 2, :], ps, 0.125, r=[key], w=["qT"])
                pcs.append(f_q)
            for jp in range(2):
                def f_ga(jp=jp):
                    ps, key = proj_fm(w2b, "w2b", 1024 + jp * 256)
                    p.actf(gate_a[:, 2 * jp:2 * jp + 2, :], ps, AF.Silu, r=[key], w=["gate_a"])
                pcs.append(f_ga)
            pcs.append(lambda: pool_slab(s))
            pcs.extend(attention_pairs(s, nb_lo))
            pass
            for mt in range(2):
                def f_glu(mt=mt):
                    b = proj_slot()
                    bk = "b%d" % b
                    for (hx, col) in ((0, mt * 128), (1, 256 + mt * 128)):
                        for kc in range(2):
                            p.mm(banks[b][:, hx * W:(hx + 1) * W], glub[:, kc, col:col + 128], yg[pb][:, kc, :], kc == 0, kc == 1,
                                 r=["yg%d" % pb, "glub"], w=[bk])
                    p.actf(sig[:], banks[b][:, W:2 * W], AF.Sigmoid, r=[bk, "psm"], w=["sig"], bias=psm[:, C_GB + 2 + mt:C_GB + 3 + mt])
                    p.stt("dve", glt[:], banks[b][:, 0:W], psm[:, C_GB + mt:C_GB + mt + 1], sig[:], ALU.add, ALU.mult,
                          r=[bk, "psm", "sig"], w=["glt"])
                    p.tt("dve", mix[:, mt, :], glt[:], gate_s[:, mt, :], ALU.mult, r=["glt", "gate_s"], w=["mix"])
                pcs.append(f_glu)
            for mp in range(4):
                def f_op(mp=mp):
                    b = proj_slot()
                    key = "b%d" % b
                    for t in range(2):
                        mt = 2 * mp + t
                        for kt in range(8):
                            p.mm(banks[b][:, t * W:(t + 1) * W], wob[:, kt, mt * 128:(mt + 1) * 128], mix[:, kt, :], kt == 0, kt == 7,
                                 r=["mix", "wob"], w=[key])
                    for t in range(2):
                        mt = 2 * mp + t
                        p.amul(yo2[:, mt, :], banks[b][:, t * W:(t + 1) * W], psm[:, C_GPOST + mt:C_GPOST + mt + 1],
                               r=[key, "psm"], w=["yo2"])
                    p.actf(sqb[:, 2 * mp:2 * mp + 2, :], bview(b, 2), AF.Square, r=[key], w=["sqb"])
                pcs.append(f_op)

            def f_post():
                b = proj_slot()
                bk = "b%d" % b
                for mt in range(8):
                    p.mm(banks[b][:, 0:W], onesb[:], sqb[:, mt, :], mt == 0, mt == 7, r=["sqb", "onesb"], w=[bk])
                p.actf(sd[:], banks[b][:, 0:W], AF.Ln, r=[bk], w=["sd"], scale=1.0 / D, bias=EPS)
                p.actf(rstd[:], sd[:], AF.Exp, r=["sd"], w=["rstd"], scale=-0.5)
                p.tt("dve", yo2[:], yo2[:], rstd[:].unsqueeze(1).to_broadcast([128, 8, W]), ALU.mult, r=["yo2", "rstd"], w=["yo2"])
                p.tt("dve", xs[xb][:], xs[xb][:], yo2[:], ALU.add, r=["yo2", xkey], w=[xkey])
                out_stores.append(lambda: p.dmas(out_v[:, :, c0:c1], xs[xb][:], "st_o%d" % xb, r=[xkey], w=["out%d" % s]))
            pcs.append(f_post)

            def f_tail():
                if s - 1 >= 0:
                    main_loads(s - 1)
                    norm_a(xs[(s - 1) % 2], "xs%d" % ((s - 1) % 2), sq=True)
            pcs.append(f_tail)
            return pcs

        EARLY = list(range(17))

        def main_early(l, src_v, srckeyf, thunks=()):
            lastb = (NS - 1) % 2
            p.dmas(xs[lastb][:], src_v[:, :, (NS - 1) * W:NS * W], "xs%d" % lastb, r=[srckeyf(NS - 1)], w=["xs%d" % lastb])
            main_loads(NS - 1)
            ssm_b_loads(NS - 1)
            norm_a(xs[lastb], "xs%d" % lastb)
            pcs = main_pieces(NS - 1, src_v, srckeyf)
            interleave([pcs[i_] for i_ in EARLY], thunks)
            return [pc for i_, pc in enumerate(pcs) if i_ not in EARLY]

        def phase_main(l, src_v, srckeyf, rest_first, deferred=()):
            deferred = list(deferred)
            nmain = DBG.get("main", NS)
            if nmain == 0:
                return
            pend = ssm_b(NS - 1)
            for s in range(NS - 1, NS - 1 - nmain, -1):
                fill = rest_first if s == NS - 1 else main_pieces(s, src_v, srckeyf)
                if s - 1 >= 0:
                    ssm_b_loads(s - 1)
                flush_stores()
                if s != NS - 1:
                    fill.pop(0)()
                if deferred:
                    fill.insert(1, deferred.pop(0))
                if s - 1 >= 0:
                    pend = ssm_b(s - 1, fillers=fill, after_s=pend)
                else:
                    if pend is not None:
                        pend()
                        pend = None
                    for f_ in fill:
                        f_()
            flush_stores()
            while deferred:
                deferred.pop(0)()

        prep_gains(0)
        for f_ in prep_weights_p1(0):
            f_()
        for l in range(L):
            first_src = (l == 0 and not from_out_first)
            src_v = xT_v if first_src else out_v
            srckeyf = (lambda s: "xin") if first_src else (lambda s: "out%d" % s)
            if DBG.get("stop") == "const":
                break
            prep_small(l)
            rec = Rec()
            prep_tables(l, 0, rec=rec, which="A")
            p1_prologue(l, src_v, srckeyf, thunks=rec.thunks(p))
            prep_tables(l, 0, which="B")
            if DBG.get("stop") == "prep":
                break
            phase1(l, src_v, srckeyf, deferred=prep_weights_main(l))
            rec = Rec()
            prep_tables(l, 1, rec=rec, which="A")
            rest_first = main_early(l, src_v, srckeyf, thunks=rec.thunks(p))
            prep_tables(l, 1, which="B")
            nxt = []
            if l + 1 < L:
                nxt = [lambda l=l: prep_gains(l + 1)] + prep_weights_p1(l + 1)
            phase_main(l, src_v, srckeyf, rest_first, deferred=nxt)

        p.emit(nc, es)
    return nc


def _consts():
    sp = np.arange(128)[:, None, None]
    rel = np.arange(3)[None, :, None]
    qi = np.arange(128)[None, None, :]
    dist = np.abs(qi - (rel - 1) * 128 - sp).astype(np.float32)
    valid = (dist <= 128).astype(np.float32)
    iota = np.broadcast_to(np.arange(W, dtype=np.float32)[None, :], (128, W)).copy()
    ratio = np.zeros((128, 2, 2, 8), np.float32)
    for ct in range(2):
        for row in range(128):
            w = (2, 4, 8, 16)[2 * ct + row // 64]
            for side in range(2):
                for c in range(8):
                    t = c if side == 0 else SEQ - 8 + c
                    cnt = min(t + w // 2, SEQ) - max(t - w // 2, 0)
                    ratio[row, ct, side, c] = w / cnt
    return (dist.reshape(128, 384), valid.reshape(128, 384), iota, ratio.reshape(128, 32), np.eye(128, dtype=np.float32))


def _prep_layer_inputs(inp, ls):
    L = len(ls)
    f = lambda a: np.ascontiguousarray(a, dtype=np.float32)
    w_in = inp["w_in"][ls]
    su, sg, pu, pg, q, k, v, ag = np.split(w_in, [256, 512, 768, 1024, 1536, 1664, 1792], axis=-1)
    k0, k1 = k[..., :64], k[..., 64:]
    w1 = f(np.concatenate([su, pu, k0, k0, k1, k1, v], axis=-1))
    w2 = f(np.concatenate([sg, pg, q, ag], axis=-1))
    wo = f(inp["w_out"][ls])
    gluw = f(inp["ssm_glu_w"][ls])
    pw = inp["pool_w"][ls]
    poolw = np.zeros((L, 128, 2, 128), np.float32)
    for ct in range(2):
        for hg in range(2):
            poolw[:, hg * 64:(hg + 1) * 64, ct, hg * 64:(hg + 1) * 64] = pw[:, 2 * ct + hg]
    poolw = poolw.reshape(L, 128, 256)

    a_re, a_im, ldt = inp["ssm_a_re"][ls], inp["ssm_a_im"][ls], inp["ssm_log_dt"][ls]
    b_re, b_im = inp["ssm_b_re"][ls], inp["ssm_b_im"][ls]
    c_re, c_im = inp["ssm_c_re"][ls], inp["ssm_c_im"][ls]

    psm = np.zeros((L, 128, NSM), np.float32)
    psm[:, :, C_GPRE:C_GPRE + 8] = inp["pre_norm_g"][ls].reshape(L, 8, 128).transpose(0, 2, 1)
    psm[:, :, C_GPOST:C_GPOST + 8] = inp["post_norm_g"][ls].reshape(L, 8, 128).transpose(0, 2, 1)
    psm[:, :, C_SD:C_SD + 2] = inp["ssm_d"][ls].reshape(L, 2, 128).transpose(0, 2, 1)
    psm[:, :, C_GB:C_GB + 4] = inp["ssm_glu_b"][ls].reshape(L, 4, 128).transpose(0, 2, 1)
    psm[:, :, C_PSC:C_PSC + 2] = inp["pool_scale"][ls].reshape(L, 2, 128).transpose(0, 2, 1)
    sink = inp["attn_sink"][ls]
    for j in range(4):
        psm[:, 0:64, C_SINK + j] = sink[:, 2 * j][:, None]
        psm[:, 64:128, C_SINK + j] = sink[:, 2 * j + 1][:, None]
    ldt_r = ldt.reshape(L, 2, 2, 8)
    psm[:, :, C_LDTR:C_LDTR + 4] = np.repeat(ldt_r.transpose(0, 3, 1, 2).reshape(L, 8, 4), 16, axis=1)
    ldt_s = ldt.reshape(L, 2, 2, 4, 2)
    psm[:, :, C_LDTS:C_LDTS + 16] = np.repeat(ldt_s.transpose(0, 4, 1, 2, 3).reshape(L, 2, 16), 64, axis=1)
    ars = a_re.reshape(L, 2, 2, 4, 2, 64)
    psm[:, :, C_ARS:C_ARS + 16] = ars.transpose(0, 4, 5, 1, 2, 3).reshape(L, 128, 16)
    ais = a_im.reshape(L, 2, 2, 4, 2, 64)
    psm[:, :, C_AIS:C_AIS + 16] = ais.transpose(0, 4, 5, 1, 2, 3).reshape(L, 128, 16)
    mask = np.zeros((128, 4, 2), np.float32)
    for g8 in range(8):
        mask[g8 * 16:(g8 + 1) * 16, g8 // 2, g8 % 2] = 1.0
    psm[:, :, C_MASK:C_MASK + 8] = mask.reshape(128, 8)[None]
    for pp_ in range(128):
        psm[:, pp_, C_MASK2 + (pp_ // 16) % 2] = 1.0
    psm[:, 96:128, C_MASK3] = 1.0
    for ct in range(2):
        psm[:, 0:64, C_INVW + ct] = 1.0 / (2, 4, 8, 16)[2 * ct]
        psm[:, 64:128, C_INVW + ct] = 1.0 / (2, 4, 8, 16)[2 * ct + 1]

    pch = np.zeros((L, 128, 4, 4, 64), np.float32)
    br = b_re.reshape(L, 2, 2, 8, 64, 16)
    bi = b_im.reshape(L, 2, 2, 8, 64, 16)
    pch[:, :, 0] = br.transpose(0, 3, 5, 1, 2, 4).reshape(L, 128, 4, 64)
    pch[:, :, 1] = bi.transpose(0, 3, 5, 1, 2, 4).reshape(L, 128, 4, 64)
    ar = a_re.reshape(L, 2, 2, 8, 64)
    ai = a_im.reshape(L, 2, 2, 8, 64)
    pch[:, :, 2] = np.repeat(ar.transpose(0, 3, 1, 2, 4).reshape(L, 8, 4, 64), 16, axis=1)
    pch[:, :, 3] = np.repeat(ai.transpose(0, 3, 1, 2, 4).reshape(L, 8, 4, 64), 16, axis=1)
    pch = pch.reshape(L, 128, 1024)

    cpad = np.zeros((L, 2, 2, 64, 2, 2, 4, 8, 16), np.float32)
    cr = c_re.reshape(L, 2, 2, 4, 2, 16, 64)
    ci = c_im.reshape(L, 2, 2, 4, 2, 16, 64)
    for gp in range(4):
        for gg in range(2):
            cpad[:, 0, gg, :, :, :, gp, 2 * gp + gg, :] = cr[:, :, :, gp, gg].transpose(0, 4, 1, 2, 3)
            cpad[:, 1, gg, :, :, :, gp, 2 * gp + gg, :] = ci[:, :, :, gp, gg].transpose(0, 4, 1, 2, 3)
    cpad = cpad.reshape(L, 2, 128, 2048)
    return dict(w1=w1, w2=w2, wo=wo, gluw=gluw, poolw=f(poolw), psmall=psm, pchan=f(pch), cpad=f(cpad))


_NC_CACHE = {}


def _get_nc(L, from_out_first=False):
    key = (L, from_out_first)
    if key not in _NC_CACHE:
        _NC_CACHE[key] = build(L, from_out_first)
    return _NC_CACHE[key]


FUSED = True
DBG = {}


def kernel(**inputs):
    x = np.asarray(inputs["x"], dtype=np.float32)
    B = x.shape[0]
    inp = {k: np.asarray(v, dtype=np.float32) for k, v in inputs.items()}
    cd, cv, cio, crat, cid = _consts()
    xT = [np.ascontiguousarray(x[b].T) for b in range(B)]
    if FUSED:
        groups = [list(range(DEPTH))]
    else:
        groups = [[l] for l in range(DEPTH)]
    for ls in groups:
        nc = _get_nc(len(ls))
        lw = _prep_layer_inputs(inp, ls)
        in_maps = []
        for b in range(B):
            m = dict(lw)
            m.update(xT=xT[b], cdist=cd, cvalid=cv, ciota=cio, cratio=crat, cident=cid)
            in_maps.append(m)
        res = run_bass_kernel_spmd(nc, in_maps, core_ids=list(range(B)))
        xT = [np.asarray(res.results[b]["out"]) for b in range(B)]
    return np.stack([xT[b].T for b in range(B)], axis=0).astype(np.float32)
```

```python
import math
from contextlib import ExitStack

import numpy as np
import concourse.bass as bass
import concourse.mybir as mybir
from concourse.bass_utils import run_bass_kernel_spmd

F32 = mybir.dt.float32
BF16 = mybir.dt.bfloat16
I32 = mybir.dt.int32
ALU = mybir.AluOpType
AF = mybir.ActivationFunctionType

D = 1024
SEQ = 4096
DEPTH = 4
W = 256
NS = SEQ // W
NSM = 93
TCH = 4
NCH = W // TCH
EPS = 1e-6
TWO_PI = 2.0 * math.pi
SC2 = TWO_PI * (1.0 - 2e-6)
SC1 = math.pi * (1.0 - 2e-6)

C_GPRE, C_GPOST, C_SD, C_GB, C_PSC, C_SINK, C_LDTR, C_LDTS, C_ARS, C_AIS, C_MASK, C_INVW = (
    0, 8, 16, 18, 22, 24, 28, 32, 48, 64, 80, 88)
C_MASK2, C_MASK3 = 90, 92


class Prog:
    ENGS = ["sp", "pe", "act", "dve", "pool"]
    SAME_SKIP = {"pe"}

    def __init__(self):
        self.ops = []
        self.lastw = {}
        self.readers = {}
        self.dmacnt = {}

    def add(self, eng, fn, r=(), w=(), dma=None, force=False):
        i = len(self.ops)
        deps = set()
        for k in r:
            if k in self.lastw:
                deps.add(self.lastw[k])
        for k in w:
            if k in self.lastw:
                deps.add(self.lastw[k])
            deps.update(self.readers.get(k, ()))
        for k in r:
            self.readers.setdefault(k, []).append(i)
        for k in w:
            self.lastw[k] = i
            self.readers[k] = []
        op = dict(eng=eng, fn=fn, deps=deps, dma=dma, needed=False, force=force)
        if dma is not None:
            self.dmacnt[dma] = self.dmacnt.get(dma, 0) + 16
            op["dcount"] = self.dmacnt[dma]
        self.ops.append(op)
        return i

    def pe(self, fn, r=(), w=()):
        return self.add("pe", fn, r, w)

    def act(self, fn, r=(), w=()):
        return self.add("act", fn, r, w)

    def dve(self, fn, r=(), w=()):
        return self.add("dve", fn, r, w)

    def pool(self, fn, r=(), w=()):
        return self.add("pool", fn, r, w)

    def dma(self, fn, key, r=(), w=()):
        return self.add("sp", fn, r, w, dma=key)


    def mm(self, out, lhsT, rhs, start, stop, r=(), w=(), force=False):
        return self.add("pe", lambda e: e.matmul(out, lhsT=lhsT, rhs=rhs, start=start, stop=stop), r, w, force=force)

    def actf(self, out, in_, func, r=(), w=(), scale=None, bias=None):
        kw = {}
        if scale is not None:
            kw["scale"] = scale
        if bias is not None:
            kw["bias"] = bias
        return self.add("act", lambda e: e.activation(out=out, in_=in_, func=func, **kw), r, w)

    def amul(self, out, in_, mul, r=(), w=()):
        return self.add("act", lambda e: e.mul(out, in_, mul), r, w)

    def acopy(self, out, in_, r=(), w=()):
        return self.add("act", lambda e: e.copy(out, in_), r, w)

    def tt(self, eng, out, in0, in1, op, r=(), w=()):
        return self.add(eng, lambda e: e.tensor_tensor(out=out, in0=in0, in1=in1, op=op), r, w)

    def ts(self, eng, out, in0, s1, s2, op0, op1=None, r=(), w=()):
        if op1 is None:
            return self.add(eng, lambda e: e.tensor_scalar(out=out, in0=in0, scalar1=s1, scalar2=None, op0=op0), r, w)
        return self.add(eng, lambda e: e.tensor_scalar(out=out, in0=in0, scalar1=s1, scalar2=s2, op0=op0, op1=op1), r, w)

    def stt(self, eng, out, in0, scalar, in1, op0, op1, r=(), w=()):
        return self.add(eng, lambda e: e.scalar_tensor_tensor(out=out, in0=in0, scalar=scalar, in1=in1, op0=op0, op1=op1), r, w)

    def scan(self, out, d0, d1, init, r=(), w=()):
        return self.add("dve", lambda e: e.tensor_tensor_scan(out=out, data0=d0, data1=d1, initial=init,
                                                              op0=ALU.mult, op1=ALU.add), r, w)

    def recip(self, out, in_, r=(), w=()):
        return self.add("dve", lambda e: e.reciprocal(out=out, in_=in_), r, w)

    def copy(self, eng, out, in_, r=(), w=()):
        return self.add(eng, lambda e: e.tensor_copy(out=out, in_=in_), r, w)

    def memset(self, eng, ap, val, w=()):
        return self.add(eng, lambda e: e.memset(ap, val), (), w)

    def dmas(self, out, in_, key, r=(), w=(), q="sp"):
        return self.add(q, lambda e: e.dma_start(out=out, in_=in_), r, w, dma=key)

    def _skip(self, op, dop):
        if op["force"]:
            return False
        return dop["dma"] is None and dop["eng"] == op["eng"] and op["eng"] in self.SAME_SKIP

    def emit(self, nc, es):
        ops = self.ops
        for op in ops:
            for d in op["deps"]:
                dop = ops[d]
                if dop["dma"] is None and not self._skip(op, dop):
                    dop["needed"] = True
        cnt = {e: 0 for e in self.ENGS}
        for op in ops:
            if op["dma"] is None and op["needed"]:
                cnt[op["eng"]] += 1
                op["sig"] = cnt[op["eng"]]
        csem = {e: es.enter_context(nc.semaphore("c_" + e)) for e in ["pe", "act", "dve", "pool"]}
        dsem = {k: es.enter_context(nc.semaphore("d_%d" % i)) for i, k in enumerate(self.dmacnt)}
        block = es.enter_context(nc.Block())
        regs = {"sp": block.sync, "pe": block.tensor, "act": block.scalar, "dve": block.vector,
                "pool": block.gpsimd}
        for eng in self.ENGS:
            def body(e, eng=eng):
                waited = {}
                for op in ops:
                    if op["eng"] != eng:
                        continue
                    for d in sorted(op["deps"]):
                        dop = ops[d]
                        if dop["dma"] is not None:
                            key, val, sem = ("d", dop["dma"]), dop["dcount"], dsem[dop["dma"]]
                        else:
                            if self._skip(op, dop):
                                continue
                            key, val, sem = ("c", dop["eng"]), dop["sig"], csem[dop["eng"]]
                        if waited.get(key, 0) >= val:
                            continue
                        e.wait_ge(sem, val)
                        waited[key] = val
                    ins = op["fn"](e)
                    if op["dma"] is not None:
                        ins.then_inc(dsem[op["dma"]], 16)
                    elif op["needed"]:
                        ins.then_inc(csem[eng], 1)
                if eng == "sp":
                    for k, v in self.dmacnt.items():
                        e.wait_ge(dsem[k], v)
            regs[eng](body)


class Rec:
    def __init__(self):
        self.calls = []

    def __getattr__(self, name):
        def f(*a, **k):
            self.calls.append((name, a, k))
        return f

    def thunks(self, prog):
        return [lambda n=n, a=a, k=k: getattr(prog, n)(*a, **k) for (n, a, k) in self.calls]


def build(L, from_out_first=False):
    nc = bass.Bass("TRN2", target_bir_lowering=False)
    dr = lambda name, shape, dt=F32, kind="ExternalInput": nc.dram_tensor(name, shape, dt, kind=kind).ap()
    xT = dr("xT", [D, SEQ])
    w1 = dr("w1", [L, D, 896])
    w2 = dr("w2", [L, D, 1536])
    wo = dr("wo", [L, D, D])
    gluw = dr("gluw", [L, 256, 512])
    poolw = dr("poolw", [L, 128, 256])
    psmall = dr("psmall", [L, 128, NSM])
    pchan = dr("pchan", [L, 128, 1024])
    cpad = dr("cpad", [L, 2, 128, 2048])
    cdist = dr("cdist", [128, 384])
    cvalid = dr("cvalid", [128, 384])
    ciota = dr("ciota", [128, W])
    cratio = dr("cratio", [128, 32])
    cident = dr("cident", [128, 128])
    out = dr("out", [D, SEQ], kind="ExternalOutput")
    skind = "ExternalOutput" if DBG.get("dump") else "Internal"
    susc = nc.dram_tensor("susc", [128, 2, SEQ], BF16, kind=skind).ap()
    ktsc = nc.dram_tensor("ktsc", [128, 2, SEQ], BF16, kind=skind).ap()
    pusc = nc.dram_tensor("pusc", [128, 2, SEQ + 16], F32, kind=skind).ap()
    vsc = nc.dram_tensor("vsc", [128, 32, 128], BF16, kind=skind).ap()
    yfsc = nc.dram_tensor("yfsc", [128, 2, SEQ], F32, kind=skind).ap()

    if DBG.get("dump"):
        dbg32 = nc.dram_tensor("dbg32", [128, 96], F32, kind="ExternalOutput").ap()
        dbgb = nc.dram_tensor("dbgb", [128, 5, 2048], BF16, kind="ExternalOutput").ap()
    xT_v = xT.rearrange("(kc p) t -> p kc t", p=128)
    out_v = out.rearrange("(kc p) t -> p kc t", p=128)

    es = ExitStack()
    with es:
        sb = lambda name, shape, dt=F32: es.enter_context(nc.sbuf_tensor(name, shape, dt))
        w1b = sb("w1b", [128, 8, 896], BF16)
        w2b = sb("w2b", [128, 8, 1536], BF16)
        wob = sb("wob", [128, 8, 1024], BF16)
        glub = sb("glub", [128, 2, 512], BF16)
        plwb = sb("plwb", [128, 2, 128], BF16)
        WS = sb("WS", [128, 2, 4, 2, 128], BF16)
        WS3 = sb("WS3", [128, 2, 4, 2, 128], BF16)
        WC = sb("WC", [128, 8, 4, 2, 128], BF16)
        Kt = sb("Kt", [128, 2, 4, 128], BF16)
        identb = sb("identb", [128, 128], BF16)
        stage = [sb("stage%d" % i, [128, 1024]) for i in range(2)]
        psm = sb("psm", [128, NSM])
        psm_g = sb("psm_g", [128, 16])
        Eb = sb("Eb", [128, 8, 384], BF16)
        onesb = sb("onesb", [128, 128], BF16)
        onesLR = sb("onesLR", [128, 2, 128], BF16)
        iota = sb("iota", [128, W])
        ratio = sb("ratio", [128, 2, 2, 8])
        zer = sb("zer", [128, 16])
        dmy = sb("dmy", [128, 2])
        rr = sb("rr", [128, 16])
        ff = sb("ff", [128, 16])
        dts = sb("dts", [128, 16])
        k16 = sb("k16", [128, 16], I32)
        u16 = sb("u16", [128, 16])
        off = sb("off", [128, 16])
        bo1 = sb("bo1", [128, 16])
        bo2 = sb("bo2", [128, 16])
        carry = sb("carry", [128, 16, 2])
        esink = sb("esink", [128, 4])
        dtc = sb("dtc", [128, 4])
        xs = [sb("xs%d" % i, [128, 8, W]) for i in range(2)]
        sqb = sb("sqb", [128, 8, W], BF16)
        hT = sb("hT", [128, 8, W], BF16)
        sd = sb("sd", [128, W])
        rstd = sb("rstd", [128, W])
        cA = sb("cA", [128, 8, NCH]); cK = sb("cK", [128, 8, NCH], I32)
        cSn = sb("cSn", [128, 8, NCH]); cCs = sb("cCs", [128, 8, NCH])
        cVr = sb("cVr", [128, 8, NCH]); cVi = sb("cVi", [128, 8, NCH])
        cGr = sb("cGr", [128, 8, NCH]); cGi = sb("cGi", [128, 8, NCH])
        Hre = sb("Hre", [128, 8, NCH + 2], BF16); Him = sb("Him", [128, 8, NCH + 2], BF16)
        ffT = sb("ffT", [128, 16]); rrT = sb("rrT", [128, 16]); offT = sb("offT", [128, 16]); ffp = sb("ffp", [128, 16])
        zsm = sb("zsm", [128, 5, 2, 8])
        carc = sb("carc", [128, 2, 8])
        su_s = [sb("su_s%d" % i, [128, 2, W], BF16) for i in range(2)]
        kt_s = sb("kt_s", [128, 2, W], BF16)
        pu_s = sb("pu_s", [128, 2, W])
        v_s = sb("v_s", [128, 2, 128], BF16)
        yf_s = [sb("yf_s%d" % i, [128, W]) for i in range(2)]
        su_l = [sb("su_l%d" % i, [128, 2, W], BF16) for i in range(2)]
        kt_l = sb("kt_l", [128, 2, 4 * 128], BF16)
        pu_l = sb("pu_l", [128, 2, W + 16])
        v_l = sb("v_l", [128, 4, 128], BF16)
        VLR = sb("VLR", [128, 4, 2, 2, 128], BF16)
        yf_l = [sb("yf_l%d" % i, [128, 2, W]) for i in range(2)]
        gate_s = sb("gate_s", [128, 2, W])
        gate_p = sb("gate_p", [128, 2, W])
        gate_a = sb("gate_a", [128, 4, W])
        qT = sb("qT", [128, 4, W], BF16)
        yg = [sb("yg%d" % i, [128, 2, W], BF16) for i in range(2)]
        yo2 = sb("yo2", [128, 8, W])
        pch = yo2[:, 0:4, :].rearrange("p a (b c) -> p a b c", b=4)
        ytmp = sb("ytmp", [128, W])
        mix = sb("mix", [128, 8, W], BF16)
        sig = sb("sig", [128, W])
        glt = sb("glt", [128, W])
        pA = sb("pA", [128, W + 16])
        pB = sb("pB", [128, W + 16])
        pC = sb("pC", [128, W + 16])
        pD = sb("pD", [128, W + 16])
        pmean = sb("pmean", [128, W])
        mixed = sb("mixed", [128, 2, W], BF16)
        pexp = [sb("pexp%d" % i, [128, 384], BF16) for i in range(2)]
        pT = [sb("pT%d" % i, [128, 384], BF16) for i in range(4)]
        dn = sb("dn", [128, 256])
        o1 = sb("o1", [128, 256])
        banks = [es.enter_context(nc.psum_tensor("bank%d" % i, [128, 512], F32)) for i in range(8)]

        def bview(b, n):
            return banks[b][:, 0:n * 256].rearrange("p (a c) -> p a c", a=n)

        if DBG.get('mem'):
            print('SBUF bytes remaining', nc.sbuf_bytes_remaining)
        p = Prog()
        p_real = p
        WKALL = ["wk%d" % i for i in range(8)]
        CARRYALL = ["carry%d" % i for i in range(16)]

        def pslot(n):
            st = {"i": 0}

            def nxt():
                st["i"] += 1
                return (st["i"] - 1) % n
            return nxt

        fl = lambda ap, pat: ap.rearrange(pat)

        p.memset("dve", onesb[:], 1.0, w=["onesb"])
        p.memset("dve", onesLR[:].rearrange("p a b -> p (a b)"), 0.0, w=["onesLR"])
        p.memset("dve", onesLR[:, 0, 0:64], 1.0, w=["onesLR"])
        p.memset("dve", onesLR[:, 1, 64:128], 1.0, w=["onesLR"])
        p.memset("dve", VLR[:].rearrange("p a b c d -> p (a b c d)"), 0.0, w=["VLR"])
        p.memset("dve", zer[:], 0.0, w=["zer"])
        p.memset("dve", v_l[:].rearrange("p a b -> p (a b)"), 0.0, w=["v_l"])
        p.memset("dve", kt_l[:].rearrange("p a b -> p (a b)"), 0.0, w=["kt_l"])
        p.dmas(iota[:], ciota, "c0", w=["iota"])
        p.dmas(stage[0][:, 0:128], cident, "stg0", w=["stage0", "stage0b"])
        p.copy("dve", identb[:], stage[0][:, 0:128], r=["stage0"], w=["identb"])
        p.dmas(ratio[:].rearrange("p a b c -> p (a b c)"), cratio, "c1", w=["ratio"])
        for ct in range(2):
            p.dmas(pusc[:, ct, 0:8], zer[:, 0:8], "c2", r=["zer"], w=["pusc_pad"])
            p.dmas(pusc[:, ct, SEQ + 8:SEQ + 16], zer[:, 0:8], "c2", r=["zer"], w=["pusc_pad"])
        p.dmas(stage[0][:, 0:384], cdist, "stg0", w=["stage0", "stage0b"])
        p.dmas(stage[1][:, 0:384], cvalid, "stg1", w=["stage1", "stage1b"])
        for h in range(8):
            slope = 2.0 ** (-(h + 1))
            p.actf(stage[0][:, 384:768], stage[0][:, 0:384], AF.Exp, r=["stage0"], w=["stage0b"], scale=-slope)
            p.tt("dve", Eb[:, h, :], stage[0][:, 384:768], stage[1][:, 0:384], ALU.mult,
                 r=["stage0b", "stage1"], w=["Eb"])

        stg_i = [0]

        def stage_load(src_ap, ncols):
            s = stg_i[0] % 2
            stg_i[0] += 1
            key = "stage%d" % s
            p.dmas(stage[s][:, 0:ncols], src_ap, "stg%d" % s, w=[key, key + "b"])
            return s, key

        def prep_small(l):
            p.dmas(psm[:], psmall[l], "psm", w=["psm"])
            p.actf(esink[:], psm[:, C_SINK:C_SINK + 4], AF.Exp, r=["psm"], w=["esink"])

        def prep_weights_p1(l):
            ch = []
            for kc in range(8):
                def f(kc=kc):
                    s, key = stage_load(w1[l, kc * 128:(kc + 1) * 128, :], 896)
                    p.amul(w1b[:, kc, :], stage[s][:, 0:896], psm_g[:, C_GPRE + kc:C_GPRE + kc + 1], r=[key, "psm_g"], w=["w1b"])
                ch.append(f)
            return ch

        def prep_weights_main(l):
            ch = []
            for kc in range(8):
                for hcol in range(2):
                    def f(kc=kc, hcol=hcol):
                        c0 = hcol * 768
                        s, key = stage_load(w2[l, kc * 128:(kc + 1) * 128, c0:c0 + 768], 768)
                        p.amul(w2b[:, kc, c0:c0 + 768], stage[s][:, 0:768], psm_g[:, C_GPRE + kc:C_GPRE + kc + 1],
                               r=[key, "psm_g"], w=["w2b"])
                    ch.append(f)
            for kc in range(8):
                def f(kc=kc):
                    s, key = stage_load(wo[l, kc * 128:(kc + 1) * 128, :], 1024)
                    p.acopy(wob[:, kc, :], stage[s][:, 0:1024], r=[key], w=["wob"])
                ch.append(f)
            for kc in range(2):
                def f(kc=kc):
                    s, key = stage_load(gluw[l, kc * 128:(kc + 1) * 128, :], 512)
                    p.acopy(glub[:, kc, :], stage[s][:, 0:512], r=[key], w=["glub"])
                ch.append(f)

            def f():
                s, key = stage_load(poolw[l], 256)
                p.acopy(plwb[:].rearrange("p a b -> p (a b)"), stage[s][:, 0:256], r=[key], w=["plwb"])
            ch.append(f)
            return ch

        def prep_gains(l):
            p.dmas(psm_g[:], psmall[l, :, 0:16], "psmg", w=["psm_g"])

        cbase = [yo2[:, 0:2, :].rearrange("p a b -> p (a b)").rearrange("p (a b) -> p a b", a=8),
                 pu_s[:].rearrange("p a b -> p (a b)").rearrange("p (a b) -> p a b", a=8)]
        cbase_key = ["yo2", "pu_s"]
        CB = [cA, cSn, cCs, cVr, cVi, cGr, cGi]
        CBK = ["cA", "cSn", "cCs", "cVr", "cVi", "cGr", "cGi"]
        GRALL = ["cGr"] + ["cGr_%d" % t_ for t_ in range(8)]
        GIALL = ["cGi"] + ["cGi_%d" % t_ for t_ in range(8)]
        ALLC = CBK + ["cK", "Hre", "Him"] + GRALL[1:] + GIALL[1:]

        def prep_tables(l, d, rec=None, which="AB"):
            p = rec if (rec is not None) else p_real
            fl2 = lambda t: t[:].rearrange("p a b -> p (a b)")
            sl = slice(8 * d, 8 * d + 8)
            if "A" in which:
                fl2 = lambda t: t[:].rearrange("p a b -> p (a b)")
                T = lambda i: fl2(CB[i // 2])[:, (i % 2) * 256:(i % 2) * 256 + 256]
                T3 = lambda i: T(i).rearrange("p (a b) -> p a b", a=4)
                H3 = lambda i: T3(i)[:, 2 * d:2 * d + 2, :]
                R = ALLC + ["yo2", "psm", "dtc"]
                Wk = ALLC
                p.dmas(yo2[:, 0:4, :].rearrange("p a b -> p (a b)"), pchan[l], "pch", w=["yo2"])
                BTre, BTim, AR, AI = (pch[:, 0, :, :], pch[:, 1, :, :], pch[:, 2, :, :], pch[:, 3, :, :])
                p.actf(dtc[:], psm[:, C_LDTR:C_LDTR + 4], AF.Exp, r=["psm"], w=["dtc"])
                dtb = dtc[:].unsqueeze(2).to_broadcast([128, 4, 64])
                kint = fl2(cK)[:, 0:256]
                p.tt("dve", T3(0), AR, dtb, ALU.mult, r=R, w=Wk)
                p.tt("dve", T3(1), AI, dtb, ALU.mult, r=R, w=Wk)
                p.actf(T(2), T(0), AF.Exp, r=R, w=Wk)
                p.ts("dve", kint, T(1), 1.0 / TWO_PI, None, ALU.mult, r=R, w=Wk)
                p.stt("dve", T(3), T(1), 1.0 / TWO_PI, kint, ALU.mult, ALU.subtract, r=R, w=Wk)
                p.actf(T(4), T(3), AF.Sin, r=R, w=Wk, scale=SC2)
                p.actf(T(5), T(3), AF.Sin, r=R, w=Wk, scale=SC1)
                p.actf(T(5), T(5), AF.Square, r=R, w=Wk)
                p.ts("dve", T(5), T(5), -2.0, 1.0, ALU.mult, ALU.add, r=R, w=Wk)
                p.tt("dve", T(6), T(2), T(5), ALU.mult, r=R, w=Wk)
                p.tt("dve", T(7), T(2), T(4), ALU.mult, r=R, w=Wk)
                p.ts("dve", T(8), T(6), -1.0, None, ALU.add, r=R, w=Wk)
                p.tt("dve", T3(0), AR, AR, ALU.mult, r=R, w=Wk)
                p.tt("dve", T3(1), AI, AI, ALU.mult, r=R, w=Wk)
                p.tt("dve", T(0), T(0), T(1), ALU.add, r=R, w=Wk)
                p.recip(T(0), T(0), r=R, w=Wk)
                p.tt("dve", T3(1), T3(8), AR, ALU.mult, r=R, w=Wk)
                p.tt("dve", T3(2), T3(7), AI, ALU.mult, r=R, w=Wk)
                p.tt("dve", T(1), T(1), T(2), ALU.add, r=R, w=Wk)
                p.tt("dve", T(1), T(1), T(0), ALU.mult, r=R, w=Wk)
                p.tt("dve", T3(2), T3(7), AR, ALU.mult, r=R, w=Wk)
                p.tt("dve", T3(3), T3(8), AI, ALU.mult, r=R, w=Wk)
                p.tt("dve", T(2), T(2), T(3), ALU.subtract, r=R, w=Wk)
                p.tt("dve", T(2), T(2), T(0), ALU.mult, r=R, w=Wk)
                ZR, ZI, LR, LI = 1, 2, 6, 7
                mk2 = psm[:, C_MASK2:C_MASK2 + 2].unsqueeze(1).unsqueeze(3).to_broadcast([128, 2, 2, 64])
                for k in range(4):
                    j = (3 - k) if d == 0 else k
                    p.tt("dve", H3(3), H3(ZR), BTre[:, 2 * d:2 * d + 2, :], ALU.mult, r=R, w=Wk)
                    p.tt("dve", H3(4), H3(ZI), BTim[:, 2 * d:2 * d + 2, :], ALU.mult, r=R, w=Wk)
                    p.tt("dve", H3(3), H3(3), H3(4), ALU.subtract, r=R, w=Wk)
                    p.tt("dve", H3(4), H3(ZR), BTim[:, 2 * d:2 * d + 2, :], ALU.mult, r=R, w=Wk)
                    p.tt("dve", H3(5), H3(ZI), BTre[:, 2 * d:2 * d + 2, :], ALU.mult, r=R, w=Wk)
                    p.tt("dve", H3(4), H3(4), H3(5), ALU.add, r=R, w=Wk)
                    for ri, src in ((0, 3), (1, 4)):
                        p.tt("dve", WS[:, :, j, ri, :].rearrange("p a (b c) -> p a b c", b=2),
                             H3(src).unsqueeze(2).to_broadcast([128, 2, 2, 64]), mk2, ALU.mult, r=R, w=["WS"])
                    if k < 3:
                        p.tt("dve", H3(3), H3(ZR), H3(LR), ALU.mult, r=R, w=Wk)
                        p.tt("dve", H3(4), H3(ZI), H3(LI), ALU.mult, r=R, w=Wk)
                        p.tt("dve", H3(5), H3(ZR), H3(LI), ALU.mult, r=R, w=Wk)
                        p.tt("dve", H3(9), H3(ZI), H3(LR), ALU.mult, r=R, w=Wk)
                        p.tt("dve", H3(ZR), H3(3), H3(4), ALU.subtract, r=R, w=Wk)
                        p.tt("dve", H3(ZI), H3(5), H3(9), ALU.add, r=R, w=Wk)
                p.ts("dve", WS3[64:128].rearrange("p a b c d -> p (a b c d)"), WS[64:128].rearrange("p a b c d -> p (a b c d)"),
                     psm[64:128, C_MASK3:C_MASK3 + 1], None, ALU.mult, r=["WS", "psm"], w=["WS3"])

                sl = slice(8 * d, 8 * d + 8)
                p.actf(dts[:], psm[:, C_LDTS:C_LDTS + 16], AF.Exp, r=["psm"], w=["dts"])
                p.tt("dve", u16[:], psm[:, C_ARS:C_ARS + 16], dts[:], ALU.mult, r=["psm", "dts"], w=["u16"])
                p.actf(rr[:], u16[:], AF.Exp, r=["u16"], w=["rr"])
                p.tt("dve", u16[:], psm[:, C_AIS:C_AIS + 16], dts[:], ALU.mult, r=["psm", "dts", "rr"], w=["u16"])
                p.ts("dve", k16[:], u16[:], 1.0 / TWO_PI, None, ALU.mult, r=["u16"], w=["k16"])
                p.stt("dve", ffp[:], u16[:], 1.0 / TWO_PI, k16[:], ALU.mult, ALU.subtract, r=["u16", "k16"], w=["ffp"])
                p.actf(off[:], ffp[:], AF.Sin, r=["ffp"], w=["off"], scale=SC2)
                p.actf(bo1[:], ffp[:], AF.Sin, r=["ffp"], w=["bo1"], scale=SC1)
                p.actf(bo1[:], bo1[:], AF.Square, r=["bo1"], w=["bo1"])
                p.ts("dve", bo1[:], bo1[:], -2.0, 1.0, ALU.mult, ALU.add, r=["bo1"], w=["bo1"])
                p.tt("dve", zsm[:, 1, 0, :], rr[:, sl], bo1[:, sl], ALU.mult, r=["rr", "bo1"], w=["zsm"])
                p.tt("dve", zsm[:, 1, 1, :], rr[:, sl], off[:, sl], ALU.mult, r=["rr", "off"], w=["zsm"])
                for k in range(1, 4):
                    a_r, a_i = zsm[:, k, 0, :], zsm[:, k, 1, :]
                    l_r, l_i = zsm[:, 1, 0, :], zsm[:, 1, 1, :]
                    p.tt("dve", bo2[:, 0:8], a_r, l_r, ALU.mult, r=["zsm"], w=["bo2"])
                    p.tt("dve", bo2[:, 8:16], a_i, l_i, ALU.mult, r=["zsm"], w=["bo2"])
                    p.tt("dve", zsm[:, k + 1, 0, :], bo2[:, 0:8], bo2[:, 8:16], ALU.subtract, r=["bo2"], w=["zsm"])
                    p.tt("dve", bo2[:, 0:8], a_r, l_i, ALU.mult, r=["zsm"], w=["bo2"])
                    p.tt("dve", bo2[:, 8:16], a_i, l_r, ALU.mult, r=["zsm"], w=["bo2"])
                    p.tt("dve", zsm[:, k + 1, 1, :], bo2[:, 0:8], bo2[:, 8:16], ALU.add, r=["bo2"], w=["zsm"])
                sgn = 1.0 if d == 0 else -1.0
                p.ts("dve", k16[:, sl], ffp[:, sl], sgn * TCH, None, ALU.mult, r=["ffp"], w=["k16"])
                p.stt("dve", ffT[:, sl], ffp[:, sl], sgn * TCH, k16[:, sl], ALU.mult, ALU.subtract, r=["ffp", "k16"], w=["ffT"])
                p.tt("dve", rrT[:, sl], rr[:, sl], rr[:, sl], ALU.mult, r=["rr"], w=["rrT"])
                p.tt("dve", rrT[:, sl], rrT[:, sl], rrT[:, sl], ALU.mult, r=["rrT"], w=["rrT"])


            p = p_real
            if "B" not in which:
                return
            p.dmas(stage[0][:, 0:1024], cpad[l, 0, :, d * 1024:(d + 1) * 1024], "stg0", w=["stage0", "stage0b"])
            p.dmas(stage[1][:, 0:1024], cpad[l, 1, :, d * 1024:(d + 1) * 1024], "stg1", w=["stage1", "stage1b"])
            c3 = lambda t: t[:, 0:1024].rearrange("p (a b) -> p a b", a=8)
            cre, cim = c3(stage[0]), c3(stage[1])
            t1 = yo2[:, 0:4, :].rearrange("p a b -> p (a b)").rearrange("p (a b) -> p a b", a=8)
            t2 = yo2[:, 4:8, :].rearrange("p a b -> p (a b)").rearrange("p (a b) -> p a b", a=8)
            RW = ["yo2", "stage0", "stage1", "zsm"]
            Ta0 = cGr[:].bitcast(BF16)
            Tb0 = cGi[:].bitcast(BF16)
            assert tuple(Ta0.shape) == (128, 8, 128), Ta0.shape
            p.acopy(Ta0, cre, r=["stage0"] + ALLC, w=GRALL)
            p.amul(Tb0, cim, -1.0, r=["stage1"] + ALLC, w=GIALL)

            x2 = lambda t, o: fl2(t)[:, o:o + 256].rearrange("p (a b) -> p a b", a=2)
            Xs = [x2(Hre, 0), x2(Hre, 256), x2(Him, 0), x2(Him, 256)]
            XK = ["Xs0", "Xs1", "Xs2", "Xs3"]
            p.memset("dve", fl2(Hre), 0.0, w=["Hre", "Xs0", "Xs1"])
            p.memset("dve", fl2(Him), 0.0, w=["Him", "Xs2", "Xs3"])
            def kg_transpose(ct, tau, gp):
                j = (3 - tau) if d == 0 else tau
                tb = proj_slot()
                tkey = "b%d" % tb
                if gp < 3:
                    rows, ncol, cbase, src = slice(32 * gp, 32 * gp + 32), 32, 32 * gp, WS
                else:
                    rows, ncol, cbase, src = slice(64, 128), 64, 64, WS3
                for ri in range(2):
                    p.mm(banks[tb][:, ri * 64:ri * 64 + ncol], src[rows, ct, j, ri, :], identb[rows, cbase:cbase + ncol],
                         True, True, r=["WS", "WS3", "identb"], w=[tkey])
                p.acopy(Xs[gp][:, :, cbase:cbase + ncol],
                        banks[tb][:, 0:128].rearrange("p (a b) -> p a b", a=2)[:, :, 0:ncol], r=[tkey], w=[XK[gp]])

            steps = [(ct, tau, gp) for ct in range(2) for tau in range(4) for gp in range(4)]
            kg_transpose(*steps[0])
            for si, (ct, tau, gp) in enumerate(steps):
                if si + 1 < len(steps):
                    kg_transpose(*steps[si + 1])
                kb_ = 2
                tile = ct * 4 + gp
                p.mm(banks[kb_][:, 0:128], Xs[gp][:, 0, :], Ta0[:, tile, :], gp == 0, False, r=[XK[gp], "cGr"], w=["b2"])
                p.mm(banks[kb_][:, 0:128], Xs[gp][:, 1, :], Tb0[:, tile, :], False, gp == 3, r=[XK[gp], "cGi"], w=["b2"])
                if gp == 3:
                    p.acopy(Kt[:, ct, tau, :], banks[kb_][:, 0:128], r=["b2"], w=["Kt"])
            for i in range(4):
                k = (i + 1) if d == 0 else (TCH - i)
                zr = zsm[:, k, 0, :].unsqueeze(2).to_broadcast([128, 8, 128])
                zi = zsm[:, k, 1, :].unsqueeze(2).to_broadcast([128, 8, 128])
                p.tt("dve", t1, cre, zr, ALU.mult, r=RW, w=["yo2"])
                p.tt("dve", t2, cim, zi, ALU.mult, r=RW, w=["yo2"])
                p.tt("dve", WC[:, :, i, 0, :], t1, t2, ALU.subtract, r=RW, w=["WC"])
                p.tt("dve", t1, cre, zi, ALU.mult, r=RW, w=["yo2"])
                p.tt("dve", t2, cim, zr, ALU.mult, r=RW, w=["yo2"])
                p.stt("dve", WC[:, :, i, 1, :], t1, -1.0, t2, ALU.mult, ALU.subtract, r=RW, w=["WC"])
            p.memset("dve", fl2(Hre), 0.0, w=["Hre", "Xs0", "Xs1"])
            p.memset("dve", fl2(Him), 0.0, w=["Him", "Xs2", "Xs3"])
            p.memset("dve", carc[:].rearrange("p a b -> p (a b)"), 0.0, w=["carc"])
            p.tt("dve", cbase[d], iota[:, 0:NCH].unsqueeze(1).to_broadcast([128, 8, NCH]),
                 ffT[:, sl].unsqueeze(2).to_broadcast([128, 8, NCH]), ALU.mult, r=["iota", "ffT", cbase_key[d]], w=[cbase_key[d]])

        proj_slot = pslot(2)

        def norm_sq(xs_t, xkey):
            p.actf(sqb[:].rearrange("p a b -> p (a b)"), xs_t[:].rearrange("p a b -> p (a b)"), AF.Square,
                   r=[xkey], w=["sqb"])

        def norm_a(xs_t, xkey, sq=True):
            if sq:
                norm_sq(xs_t, xkey)
            b = proj_slot()
            bk = "b%d" % b
            for kc in range(8):
                p.mm(banks[b][:, 0:W], onesb[:], sqb[:, kc, :], kc == 0, kc == 7, r=["sqb", "onesb"], w=[bk])
            p.actf(sd[:], banks[b][:, 0:W], AF.Ln, r=[bk], w=["sd"], scale=1.0 / D, bias=EPS)
            p.actf(rstd[:], sd[:], AF.Exp, r=["sd"], w=["rstd"], scale=-0.5)

        def norm_b(xs_t, xkey, eng="dve"):
            p.tt(eng, hT[:], xs_t[:], rstd[:].unsqueeze(1).to_broadcast([128, 8, W]), ALU.mult,
                 r=[xkey, "rstd"], w=["hT"])

        def norm_slab(xs_t, xkey):
            norm_a(xs_t, xkey)
            norm_b(xs_t, xkey)

        def proj_fm(wb, wkey, col0, nt=2):
            b = proj_slot()
            key = "b%d" % b
            for t in range(nt):
                for kc in range(8):
                    p.mm(banks[b][:, t * W:(t + 1) * W], wb[:, kc, col0 + t * 128:col0 + (t + 1) * 128], hT[:, kc, :],
                         kc == 0, kc == 7, r=["hT", wkey], w=[key])
            return bview(b, nt), key

        tab_slot = pslot(2)
        pp_slot = pslot(2)
        brbi_slot = pslot(2)

        def ssm_slab3(direction, s, su_t, sukey, after_ct, fillers=(), after_s=None):
            d = direction
            fillers = list(fillers)
            sl = slice(8 * d, 8 * d + 8)
            rev = (d == 1)
            nops = 30
            per_op = max(1, -(-len(fillers) // nops)) if fillers else 0
            fl2 = lambda t: t[:].rearrange("p a b -> p (a b)")

            def fill(n=1):
                for _ in range(n * per_op):
                    if fillers:
                        fillers.pop(0)()

            for tile in range(8):
                ct, gp = tile // 4, tile % 4
                if gp < 3:
                    rows, src = slice(32 * gp, 32 * gp + 32), WS
                else:
                    rows, src = slice(64, 128), WS3
                if gp not in DBG.get("gps", (0, 1, 2, 3)):
                    continue
                for ri in range(2):
                    for j in range(TCH):
                        p.mm(banks[2 + ri][:, tile * NCH:(tile + 1) * NCH], src[rows, ct, j, ri, :], su_t[rows, ct, j:W:TCH],
                             j == 0, j == TCH - 1, r=[sukey, "WS", "WS3"], w=["b%d" % (2 + ri), "rgser"],
                             force=(ri == 0 and j == 0))
            Sre = banks[2][:, 0:8 * NCH].rearrange("p (a b) -> p a b", a=8)
            Sim = banks[3][:, 0:8 * NCH].rearrange("p (a b) -> p a b", a=8)
            c0 = float(s * NCH)
            p.ts("dve", k16[:, sl], ffT[:, sl], c0, None, ALU.mult, r=["ffT"], w=["k16"])
            p.stt("dve", offT[:, sl], ffT[:, sl], c0, k16[:, sl], ALU.mult, ALU.subtract, r=["ffT", "k16"], w=["offT"])
            bc_t = lambda t: t[:, sl].unsqueeze(2).to_broadcast([128, 8, NCH])
            io_b = iota[:, 0:NCH].unsqueeze(1).to_broadcast([128, 8, NCH])
            bk_ = cbase_key[d]
            p.tt("dve", cK[:], cbase[d], bc_t(offT), ALU.add, r=[bk_, "offT"], w=["cK"])
            p.tt("dve", cA[:], cbase[d], bc_t(offT), ALU.add, r=[bk_, "offT", "cA"], w=["cA"])
            p.tt("dve", cA[:], cA[:], cK[:], ALU.subtract, r=["cA", "cK"], w=["cA"])
            p.actf(fl2(cSn), fl2(cA), AF.Sin, r=["cA"], w=["cSn"], scale=SC2)
            p.actf(fl2(cCs), fl2(cA), AF.Sin, r=["cA"], w=["cCs"], scale=SC1)
            p.actf(fl2(cCs), fl2(cCs), AF.Square, r=["cCs"], w=["cCs"])
            p.actf(fl2(cCs), fl2(cCs), AF.Identity, r=["cCs"], w=["cCs"], scale=-2.0, bias=1.0)
            if after_s is not None:
                after_s()
            fill(4)
            p.tt("dve", cVr[:], Sre, cCs[:], ALU.mult, r=["b2", "cCs"], w=["cVr"])
            p.tt("dve", cA[:], Sim, cSn[:], ALU.mult, r=["b3", "cSn", "cA"], w=["cA"])
            fill()
            p.tt("dve", cVi[:], Sim, cCs[:], ALU.mult, r=["b3", "cCs"], w=["cVi"])
            p.tt("dve", cGr[:], Sre, cSn[:], ALU.mult, r=["b2", "cSn"], w=GRALL)
            fill()
            p.tt("dve", cVr[:], cVr[:], cA[:], ALU.add, r=["cVr", "cA"], w=["cVr"])
            p.tt("dve", cVi[:], cVi[:], cGr[:], ALU.subtract, r=["cVi"] + GRALL, w=["cVi"])
            fill()
            fw = (lambda ap: ap[:, ::-1]) if rev else (lambda ap: ap)
            for tile in range(8):
                st = 8 * d + tile
                rb = rrT[:, st:st + 1].to_broadcast([128, NCH])
                p.scan(fw(cGr[:, tile, :]), rb, fw(cVr[:, tile, :]), carc[:, 0, tile:tile + 1], r=["cVr", "rrT", "carc"],
                       w=["cGr_%d" % tile])
                p.scan(fw(cGi[:, tile, :]), rb, fw(cVi[:, tile, :]), carc[:, 1, tile:tile + 1], r=["cVi", "rrT", "carc"],
                       w=["cGi_%d" % tile])
                if tile % 2 == 1:
                    fill()
            lastc = 0 if rev else NCH - 1
            GRK = ["cGr_%d" % t_ for t_ in range(8)]
            GIK = ["cGi_%d" % t_ for t_ in range(8)]
            p.copy("dve", carc[:, 0, :], cGr[:, :, lastc], r=GRK, w=["carc"])
            p.copy("dve", carc[:, 1, :], cGi[:, :, lastc], r=GIK, w=["carc"])
            hcols = slice(1, NCH + 1)
            if not rev:
                hdst, hsrc, hrhs = 0, NCH, slice(0, NCH)
            else:
                hdst, hsrc, hrhs = NCH + 1, 1, slice(2, NCH + 2)
            p.copy("dve", Hre[:, :, hdst], Hre[:, :, hsrc], r=["Hre"], w=["Hre"])
            p.copy("dve", Him[:, :, hdst], Him[:, :, hsrc], r=["Him"], w=["Him"])
            fill()
            p.tt("dve", cVr[:], cGr[:], cCs[:], ALU.mult, r=GRK + ["cCs", "cVr"], w=["cVr"])
            p.tt("dve", cVi[:], cGi[:], cSn[:], ALU.mult, r=GIK + ["cSn", "cVi"], w=["cVi"])
            fill()
            p.tt("dve", Hre[:, :, hcols], cVr[:], cVi[:], ALU.subtract, r=["cVr", "cVi", "Hre"], w=["Hre"])
            p.tt("dve", cVr[:], cGr[:], cSn[:], ALU.mult, r=GRK + ["cSn", "cVr", "Hre"], w=["cVr"])
            fill()
            p.tt("dve", cVi[:], cGi[:], cCs[:], ALU.mult, r=GIK + ["cCs", "cVi", "Hre"], w=["cVi"])
            fill()
            p.tt("dve", Him[:, :, hcols], cVr[:], cVi[:], ALU.add, r=["cVr", "cVi", "Him"], w=["Him"])
            while fillers:
                fillers.pop(0)()

            def part_b():
              for ct in range(2):
                  ykey = "b%d" % (4 + ct)
                  for i in range(TCH):
                      yo_ = banks[4 + ct][:, i:W:TCH]
                      js = list(range(0, i + 1)) if not rev else list(range(i, TCH))
                      nmm = len(js) + 8
                      n = 0
                      for j in js:
                          tau = abs(i - j)
                          p.mm(yo_, Kt[:, ct, tau, :], su_t[:, ct, j:W:TCH], n == 0, n == nmm - 1, r=[sukey, "Kt"], w=[ykey])
                          n += 1
                      for gp in range(4):
                          tile = ct * 4 + gp
                          p.mm(yo_, WC[:, tile, i, 0, :], Hre[:, tile, hrhs], n == 0, n == nmm - 1, r=["WC", "Hre"], w=[ykey])
                          n += 1
                          p.mm(yo_, WC[:, tile, i, 1, :], Him[:, tile, hrhs], n == 0, n == nmm - 1, r=["WC", "Him"], w=[ykey])
                          n += 1
                  after_ct(ct, banks[4 + ct][:, 0:W], ykey)
            return part_b

        def p1_pieces(s, src_v, srckeyf):
            c0, c1 = s * W, (s + 1) * W
            xb = s % 2
            xkey = "xs%d" % xb
            sus, suk = su_s[s % 2], "su_s%d" % (s % 2)
            pcs = []

            def f_norm():
                norm_b(xs[xb], xkey, eng=DBG.get("p1_norm_eng", "pool"))
            pcs.append(f_norm)

            def f_sq_next():
                if s + 1 < NS:
                    norm_sq(xs[(s + 1) % 2], "xs%d" % ((s + 1) % 2))
            pcs.append(f_sq_next)

            def f_su():
                ps, key = proj_fm(w1b, "w1b", 0)
                p.acopy(sus[:], ps, r=[key], w=[suk])
                p.dmas(susc[:, :, c0:c1], sus[:], "st_su%d" % (s % 2), r=[suk], w=["susc"])
            pcs.append(f_su)

            def f_pu():
                ps, key = proj_fm(w1b, "w1b", 256)
                p.acopy(pu_s[:], ps, r=[key], w=["pu_s"])
                p.dmas(pusc[:, :, 8 + c0:8 + c1], pu_s[:], "st_pu", r=["pu_s"], w=["pusc"])
            pcs.append(f_pu)

            def f_kt():
                ps, key = proj_fm(w1b, "w1b", 512)
                p.acopy(kt_s[:], ps, r=[key], w=["kt_s"])
                p.dmas(ktsc[:, :, c0:c1], kt_s[:], "st_kt", r=["kt_s"], w=["ktsc"])
            pcs.append(f_kt)

            def f_v():
                b = proj_slot()
                key = "b%d" % b
                for blk in range(2):
                    for kc in range(8):
                        p.mm(banks[b][:, blk * 128:(blk + 1) * 128], hT[:, kc, blk * 128:(blk + 1) * 128], w1b[:, kc, 768:896],
                             kc == 0, kc == 7, r=["hT", "w1b"], w=[key])
                p.acopy(v_s[:], banks[b][:, 0:256].rearrange("p (a c) -> p a c", a=2), r=[key], w=["v_s"])
                p.dmas(vsc[:, 2 * s:2 * s + 2, :], v_s[:], "st_v", r=["v_s"], w=["vsc"])
            pcs.append(f_v)

            def f_next_norm():
                if s + 1 < NS:
                    nb = (s + 1) % 2
                    norm_a(xs[nb], "xs%d" % nb, sq=False)
                p.actf(dmy[:, 0:1], zer[:, 0:1], AF.Sin, r=["zer"], w=["dmy"])
                if s + 2 < NS:
                    p.dmas(xs[xb][:], src_v[:, :, c1 + W:c1 + 2 * W], "xs%d" % xb, r=[srckeyf(s + 2)], w=["xs%d" % xb])
            pcs.append(f_next_norm)
            return pcs

        def interleave(pieces, thunks):
            thunks = list(thunks)
            per = -(-len(thunks) // max(len(pieces), 1))
            for pc in pieces:
                pc()
                for _ in range(per):
                    if thunks:
                        thunks.pop(0)()
            while thunks:
                thunks.pop(0)()

        def p1_prologue(l, src_v, srckeyf, thunks=()):
            def f0():
                p.dmas(xs[0][:], src_v[:, :, 0:W], "xs0", r=[srckeyf(0)], w=["xs0"])
                p.dmas(xs[1][:], src_v[:, :, W:2 * W], "xs1", r=[srckeyf(1)], w=["xs1"])
                norm_a(xs[0], "xs0")
            interleave([f0] + p1_pieces(0, src_v, srckeyf), thunks)

        def phase1(l, src_v, srckeyf, deferred=()):
            deferred = list(deferred)
            pend = None
            for s in range(DBG.get("p1", NS)):
                c0, c1 = s * W, (s + 1) * W
                nxt_p = p1_pieces(s + 1, src_v, srckeyf) if s + 1 < NS else []
                if nxt_p:
                    nxt_p.pop(0)()
                fill = []
                for _ in range(2):
                    if deferred:
                        fill.append(deferred.pop(0))
                fill = nxt_p + fill

                def after_f(ct, ps, key, c0=c0, c1=c1):
                    p.acopy(yf_s[ct][:], ps, r=[key], w=["yf_s%d" % ct])
                    p.dmas(yfsc[:, ct, c0:c1], yf_s[ct][:], "st_yf%d" % ct, r=["yf_s%d" % ct], w=["yfsc"])
                pend = ssm_slab3(0, s, su_s[s % 2], "su_s%d" % (s % 2), after_f, fillers=fill, after_s=pend)
            if pend is not None:
                pend()
            while deferred:
                deferred.pop(0)()

        s_slot = pslot(2)
        od_slot = pslot(2)
        od_bank = lambda: 0 + proj_slot()
        pt_slot = pslot(4)
        pe_slot = pslot(2)

        def attention_pairs(s, nb_lo):
            v3 = lambda ap: ap.rearrange("p (a b) -> p a b", a=2)
            pairs = [(nloc, jp, jj) for nloc in range(2) for jp in range(2) for jj in range(2)]
            state = {}

            def stage1(nloc, jp, jj):
                n = 2 * s + nloc
                kbs = [kb for kb in (n - 1, n, n + 1) if 0 <= kb < 32]
                rel0 = kbs[0] - (n - 1)
                nk = len(kbs)
                qc0 = nloc * 128
                j = jp * 2 + jj
                kv = j // 2
                pts = []
                for hh in range(2):
                    h = 2 * j + hh
                    ss = s_slot()
                    skey = "b%d" % (6 + ss)
                    s_ps = banks[6 + ss]
                    rows = slice(64 * hh, 64 * hh + 64)
                    for ki_, kb in enumerate(kbs):
                        kl = kb - nb_lo
                        p.mm(s_ps[:, ki_ * 128:(ki_ + 1) * 128], kt_l[rows, kv, kl * 128:(kl + 1) * 128],
                             qT[rows, j, qc0:qc0 + 128], True, True, r=["kt_l", "qT"], w=[skey])
                    pe_i = pe_slot()
                    pe_, pk = pexp[pe_i], "pexp%d" % pe_i
                    p.actf(pe_[:, 0:nk * 128], s_ps[:, 0:nk * 128], AF.Exp, r=[skey], w=[pk])
                    pi_ = pt_slot()
                    pt_, tk = pT[pi_], "pT%d" % pi_
                    p.tt("dve", pt_[:, 0:nk * 128], pe_[:, 0:nk * 128], Eb[:, h, rel0 * 128:(rel0 + nk) * 128], ALU.mult,
                         r=[pk, "Eb"], w=[tk])
                    pts.append((pt_, tk))
                state[(nloc, jp, jj)] = (pts, kbs)

            def stage2(nloc, jp, jj):
                pts, kbs = state.pop((nloc, jp, jj))
                nk = len(kbs)
                qc0 = nloc * 128
                j = jp * 2 + jj
                kv = j // 2
                if jj == 0:
                    state[("od", nloc, jp)] = proj_slot()
                ob = state[("od", nloc, jp)]
                okey = "b%d" % ob
                odv = banks[ob][:].rearrange("p (j t c) -> p j t c", j=2, t=2)
                for t in range(2):
                    for hh in range(2):
                        pt_, tk = pts[hh]
                        for ki_, kb in enumerate(kbs):
                            kl = kb - nb_lo
                            first = (hh == 0 and ki_ == 0)
                            last = (hh == 1 and ki_ == nk - 1)
                            lhs = VLR[:, kl, kv, hh, :] if t == 0 else onesLR[:, hh, :]
                            p.mm(odv[:, jj, t, :], lhs, pt_[:, ki_ * 128:(ki_ + 1) * 128], first, last,
                                 r=[tk, "VLR", "onesLR"], w=[okey])
                if jj == 1:
                    state.pop(("od", nloc, jp))
                    p.tt("dve", v3(dn[:]), odv[:, :, 1, :], esink[:, 2 * jp:2 * jp + 2].unsqueeze(2).to_broadcast([128, 2, 128]), ALU.add,
                         r=[okey, "esink"], w=["dn"])
                    p.actf(dn[:], dn[:], AF.Ln, r=["dn"], w=["dn"])
                    p.actf(dn[:], dn[:], AF.Exp, r=["dn"], w=["dn"], scale=-1.0)
                    p.tt("dve", v3(o1[:]), odv[:, :, 0, :], v3(dn[:]), ALU.mult, r=[okey, "dn"], w=["o1"])
                    p.tt("dve", mix[:, 4 + 2 * jp:6 + 2 * jp, qc0:qc0 + 128], v3(o1[:]), gate_a[:, 2 * jp:2 * jp + 2, qc0:qc0 + 128],
                         ALU.mult, r=["o1", "gate_a"], w=["mix"])

            pcs = [lambda: stage1(*pairs[0])]
            for k in range(1, len(pairs)):
                pcs.append(lambda k=k: (stage1(*pairs[k]), stage2(*pairs[k - 1])))
            pcs.append(lambda: stage2(*pairs[-1]))
            return pcs

        def pool_slab(s):
            first, last = (s == 0), (s == NS - 1)
            n = W + 16
            lo, hi = slice(0, 64), slice(64, 128)
            for ct in range(2):
                Xc = pu_l[:, ct, :]
                p.tt("pool", pA[:, 1:n], Xc[:, 0:n - 1], Xc[:, 1:n], ALU.add, r=["pu_l"], w=["pA"])
                if ct == 0:
                    p.tt("pool", pB[hi, 2:n - 1], pA[hi, 1:n - 2], pA[hi, 3:n], ALU.add, r=["pA"], w=["pB"])
                    srcs = ((lo, pA, "pA"), (hi, pB, "pB"))
                else:
                    p.tt("pool", pB[:, 2:n - 1], pA[:, 1:n - 2], pA[:, 3:n], ALU.add, r=["pA"], w=["pB"])
                    p.tt("pool", pC[:, 4:n - 3], pB[:, 2:n - 5], pB[:, 6:n - 1], ALU.add, r=["pB"], w=["pC"])
                    p.tt("pool", pD[hi, 8:n - 7], pC[hi, 4:n - 11], pC[hi, 12:n - 3], ALU.add, r=["pC"], w=["pD"])
                    srcs = ((lo, pC, "pC"), (hi, pD, "pD"))
                for (rows, buf, bk) in srcs:
                    p.ts("pool", pmean[rows, :], buf[rows, 8:8 + W], psm[rows, C_INVW + ct:C_INVW + ct + 1], None, ALU.mult,
                         r=[bk, "psm"], w=["pmean"])
                if first:
                    p.tt("pool", pmean[:, 0:8], pmean[:, 0:8], ratio[:, ct, 0, :], ALU.mult, r=["pmean", "ratio"], w=["pmean"])
                if last:
                    p.tt("pool", pmean[:, W - 8:W], pmean[:, W - 8:W], ratio[:, ct, 1, :], ALU.mult, r=["pmean", "ratio"], w=["pmean"])
                p.tt("pool", mixed[:, ct, :], pmean[:], Xc[:, 8:8 + W], ALU.subtract, r=["pmean", "pu_l"], w=["mixed"])
                b = proj_slot()
                key = "b%d" % b
                p.mm(banks[b][:, 0:W], plwb[:, ct, :], mixed[:, ct, :], True, True, r=["mixed", "plwb"], w=[key])
                p.stt("dve", mix[:, 2 + ct, :], banks[b][:, 0:W], psm[:, C_PSC + ct:C_PSC + ct + 1], gate_p[:, ct, :], ALU.mult, ALU.mult,
                      r=[key, "psm", "gate_p"], w=["mix"])

        def ssm_b_loads(s):
            c0, c1 = s * W, (s + 1) * W
            pb = s % 2
            p.dmas(su_l[pb][:], susc[:, :, c0:c1], "ld_su%d" % pb, r=["susc"], w=["su_l%d" % pb])
            p.dmas(yf_l[pb][:], yfsc[:, :, c0:c1], "ld_yf%d" % pb, r=["yfsc"], w=["yf_l%d" % pb])

        def ssm_b(s, fillers=(), after_s=None):
            pb = s % 2

            def after_b(ct, ps, key):
                p.tt("dve", ytmp[:], ps, yf_l[pb][:, ct, :], ALU.add, r=[key, "yf_l%d" % pb], w=["ytmp"])
                p.stt("dve", ytmp[:], su_l[pb][:, ct, :], psm[:, C_SD + ct:C_SD + ct + 1], ytmp[:], ALU.mult, ALU.add,
                      r=["su_l%d" % pb, "psm", "ytmp"], w=["ytmp"])
                p.actf(yg[pb][:, ct, :], ytmp[:], AF.Gelu_apprx_tanh, r=["ytmp"], w=["yg%d" % pb])
            return ssm_slab3(1, s, su_l[pb], "su_l%d" % pb, after_b, fillers=fillers, after_s=after_s)

        out_stores = []

        def flush_stores():
            while out_stores:
                out_stores.pop(0)()

        def main_loads(s):
            c0 = s * W
            nb_lo = 2 * s - 1
            p.dmas(pu_l[:], pusc[:, :, c0:c0 + W + 16], "ld_pu", r=["pusc", "pusc_pad"], w=["pu_l"])
            blo, bhi = max(nb_lo, 0), min(nb_lo + 4, 32)
            p.dmas(kt_l[:, :, (blo - nb_lo) * 128:(bhi - nb_lo) * 128], ktsc[:, :, blo * 128:bhi * 128], "ld_kt",
                   r=["ktsc"], w=["kt_l"])
            p.dmas(v_l[:, blo - nb_lo:bhi - nb_lo, :], vsc[:, blo:bhi, :], "ld_v", r=["vsc"], w=["v_l"])

        def main_pieces(s, src_v, srckeyf):
            c0, c1 = s * W, (s + 1) * W
            xb = s % 2
            xkey = "xs%d" % xb
            nb_lo = 2 * s - 1
            pb = s % 2
            pcs = []

            def f_loads():
                if s - 1 >= 0:
                    nb = (s - 1) % 2
                    p.dmas(xs[nb][:], src_v[:, :, c0 - W:c0], "xs%d" % nb, r=[srckeyf(s - 1)], w=["xs%d" % nb])
                for kv in range(2):
                    for hh in range(2):
                        p.copy("pool", VLR[:, :, kv, hh, 64 * hh:64 * hh + 64], v_l[:, :, 64 * kv:64 * kv + 64],
                               r=["v_l"], w=["VLR"])
                norm_b(xs[xb], xkey)
            pcs.append(f_loads)


            def f_gs():
                ps, key = proj_fm(w2b, "w2b", 0)
                p.actf(gate_s[:], ps, AF.Silu, r=[key], w=["gate_s"])
            pcs.append(f_gs)

            def f_gp():
                ps, key = proj_fm(w2b, "w2b", 256)
                p.actf(gate_p[:], ps, AF.Silu, r=[key], w=["gate_p"])
            pcs.append(f_gp)
            for jp in range(2):
                def f_q(jp=jp):
                    ps, key = proj_fm(w2b, "w2b", 512 + jp * 256)
                    p.amul(qT[:, 2 * jp:2 * jp + 2, :], ps, 0.125, r=[key], w=["qT"])
                pcs.append(f_q)
            for jp in range(2):
                def f_ga(jp=jp):
                    ps, key = proj_fm(w2b, "w2b", 1024 + jp * 256)
                    p.actf(gate_a[:, 2 * jp:2 * jp + 2, :], ps, AF.Silu, r=[key], w=["gate_a"])
                pcs.append(f_ga)
            pcs.append(lambda: pool_slab(s))
            pcs.extend(attention_pairs(s, nb_lo))
            pass
            for mt in range(2):
                def f_glu(mt=mt):
                    b = proj_slot()
                    bk = "b%d" % b
                    for (hx, col) in ((0, mt * 128), (1, 256 + mt * 128)):
                        for kc in range(2):
                            p.mm(banks[b][:, hx * W:(hx + 1) * W], glub[:, kc, col:col + 128], yg[pb][:, kc, :], kc == 0, kc == 1,
                                 r=["yg%d" % pb, "glub"], w=[bk])
                    p.actf(sig[:], banks[b][:, W:2 * W], AF.Sigmoid, r=[bk, "psm"], w=["sig"], bias=psm[:, C_GB + 2 + mt:C_GB + 3 + mt])
                    p.stt("dve", glt[:], banks[b][:, 0:W], psm[:, C_GB + mt:C_GB + mt + 1], sig[:], ALU.add, ALU.mult,
                          r=[bk, "psm", "sig"], w=["glt"])
                    p.tt("dve", mix[:, mt, :], glt[:], gate_s[:, mt, :], ALU.mult, r=["glt", "gate_s"], w=["mix"])
                pcs.append(f_glu)
            for mp in range(4):
                def f_op(mp=mp):
                    b = proj_slot()
                    key = "b%d" % b
                    for t in range(2):
                        mt = 2 * mp + t
                        for kt in range(8):
                            p.mm(banks[b][:, t * W:(t + 1) * W], wob[:, kt, mt * 128:(mt + 1) * 128], mix[:, kt, :], kt == 0, kt == 7,
                                 r=["mix", "wob"], w=[key])
                    for t in range(2):
                        mt = 2 * mp + t
                        p.amul(yo2[:, mt, :], banks[b][:, t * W:(t + 1) * W], psm[:, C_GPOST + mt:C_GPOST + mt + 1],
                               r=[key, "psm"], w=["yo2"])
                    p.actf(sqb[:, 2 * mp:2 * mp + 2, :], bview(b, 2), AF.Square, r=[key], w=["sqb"])
                pcs.append(f_op)

            def f_post():
                b = proj_slot()
                bk = "b%d" % b
                for mt in range(8):
                    p.mm(banks[b][:, 0:W], onesb[:], sqb[:, mt, :], mt == 0, mt == 7, r=["sqb", "onesb"], w=[bk])
                p.actf(sd[:], banks[b][:, 0:W], AF.Ln, r=[bk], w=["sd"], scale=1.0 / D, bias=EPS)
                p.actf(rstd[:], sd[:], AF.Exp, r=["sd"], w=["rstd"], scale=-0.5)
                p.tt("dve", yo2[:], yo2[:], rstd[:].unsqueeze(1).to_broadcast([128, 8, W]), ALU.mult, r=["yo2", "rstd"], w=["yo2"])
                p.tt("dve", xs[xb][:], xs[xb][:], yo2[:], ALU.add, r=["yo2", xkey], w=[xkey])
                out_stores.append(lambda: p.dmas(out_v[:, :, c0:c1], xs[xb][:], "st_o%d" % xb, r=[xkey], w=["out%d" % s]))
            pcs.append(f_post)

            def f_tail():
                if s - 1 >= 0:
                    main_loads(s - 1)
                    norm_a(xs[(s - 1) % 2], "xs%d" % ((s - 1) % 2), sq=True)
            pcs.append(f_tail)
            return pcs

        EARLY = list(range(17))

        def main_early(l, src_v, srckeyf, thunks=()):
            lastb = (NS - 1) % 2
            p.dmas(xs[lastb][:], src_v[:, :, (NS - 1) * W:NS * W], "xs%d" % lastb, r=[srckeyf(NS - 1)], w=["xs%d" % lastb])
            main_loads(NS - 1)
            ssm_b_loads(NS - 1)
            norm_a(xs[lastb], "xs%d" % lastb)
            pcs = main_pieces(NS - 1, src_v, srckeyf)
            interleave([pcs[i_] for i_ in EARLY], thunks)
            return [pc for i_, pc in enumerate(pcs) if i_ not in EARLY]

        def phase_main(l, src_v, srckeyf, rest_first, deferred=()):
            deferred = list(deferred)
            nmain = DBG.get("main", NS)
            if nmain == 0:
                return
            pend = ssm_b(NS - 1)
            for s in range(NS - 1, NS - 1 - nmain, -1):
                fill = rest_first if s == NS - 1 else main_pieces(s, src_v, srckeyf)
                if s - 1 >= 0:
                    ssm_b_loads(s - 1)
                flush_stores()
                if s != NS - 1:
                    fill.pop(0)()
                if deferred:
                    fill.insert(1, deferred.pop(0))
                if s - 1 >= 0:
                    pend = ssm_b(s - 1, fillers=fill, after_s=pend)
                else:
                    if pend is not None:
                        pend()
                        pend = None
                    for f_ in fill:
                        f_()
            flush_stores()
            while deferred:
                deferred.pop(0)()

        prep_gains(0)
        for f_ in prep_weights_p1(0):
            f_()
        for l in range(L):
            first_src = (l == 0 and not from_out_first)
            src_v = xT_v if first_src else out_v
            srckeyf = (lambda s: "xin") if first_src else (lambda s: "out%d" % s)
            if DBG.get("stop") == "const":
                break
            prep_small(l)
            rec = Rec()
            prep_tables(l, 0, rec=rec, which="A")
            p1_prologue(l, src_v, srckeyf, thunks=rec.thunks(p))
            prep_tables(l, 0, which="B")
            if DBG.get("stop") == "prep":
                break
            phase1(l, src_v, srckeyf, deferred=prep_weights_main(l))
            rec = Rec()
            prep_tables(l, 1, rec=rec, which="A")
            rest_first = main_early(l, src_v, srckeyf, thunks=rec.thunks(p))
            prep_tables(l, 1, which="B")
            nxt = []
            if l + 1 < L:
                nxt = [lambda l=l: prep_gains(l + 1)] + prep_weights_p1(l + 1)
            phase_main(l, src_v, srckeyf, rest_first, deferred=nxt)

        p.emit(nc, es)
    return nc


def _consts():
    sp = np.arange(128)[:, None, None]
    rel = np.arange(3)[None, :, None]
    qi = np.arange(128)[None, None, :]
    dist = np.abs(qi - (rel - 1) * 128 - sp).astype(np.float32)
    valid = (dist <= 128).astype(np.float32)
    iota = np.broadcast_to(np.arange(W, dtype=np.float32)[None, :], (128, W)).copy()
    ratio = np.zeros((128, 2, 2, 8), np.float32)
    for ct in range(2):
        for row in range(128):
            w = (2, 4, 8, 16)[2 * ct + row // 64]
            for side in range(2):
                for c in range(8):
                    t = c if side == 0 else SEQ - 8 + c
                    cnt = min(t + w // 2, SEQ) - max(t - w // 2, 0)
                    ratio[row, ct, side, c] = w / cnt
    return (dist.reshape(128, 384), valid.reshape(128, 384), iota, ratio.reshape(128, 32), np.eye(128, dtype=np.float32))


def _prep_layer_inputs(inp, ls):
    L = len(ls)
    f = lambda a: np.ascontiguousarray(a, dtype=np.float32)
    w_in = inp["w_in"][ls]
    su, sg, pu, pg, q, k, v, ag = np.split(w_in, [256, 512, 768, 1024, 1536, 1664, 1792], axis=-1)
    k0, k1 = k[..., :64], k[..., 64:]
    w1 = f(np.concatenate([su, pu, k0, k0, k1, k1, v], axis=-1))
    w2 = f(np.concatenate([sg, pg, q, ag], axis=-1))
    wo = f(inp["w_out"][ls])
    gluw = f(inp["ssm_glu_w"][ls])
    pw = inp["pool_w"][ls]
    poolw = np.zeros((L, 128, 2, 128), np.float32)
    for ct in range(2):
        for hg in range(2):
            poolw[:, hg * 64:(hg + 1) * 64, ct, hg * 64:(hg + 1) * 64] = pw[:, 2 * ct + hg]
    poolw = poolw.reshape(L, 128, 256)

    a_re, a_im, ldt = inp["ssm_a_re"][ls], inp["ssm_a_im"][ls], inp["ssm_log_dt"][ls]
    b_re, b_im = inp["ssm_b_re"][ls], inp["ssm_b_im"][ls]
    c_re, c_im = inp["ssm_c_re"][ls], inp["ssm_c_im"][ls]

    psm = np.zeros((L, 128, NSM), np.float32)
    psm[:, :, C_GPRE:C_GPRE + 8] = inp["pre_norm_g"][ls].reshape(L, 8, 128).transpose(0, 2, 1)
    psm[:, :, C_GPOST:C_GPOST + 8] = inp["post_norm_g"][ls].reshape(L, 8, 128).transpose(0, 2, 1)
    psm[:, :, C_SD:C_SD + 2] = inp["ssm_d"][ls].reshape(L, 2, 128).transpose(0, 2, 1)
    psm[:, :, C_GB:C_GB + 4] = inp["ssm_glu_b"][ls].reshape(L, 4, 128).transpose(0, 2, 1)
    psm[:, :, C_PSC:C_PSC + 2] = inp["pool_scale"][ls].reshape(L, 2, 128).transpose(0, 2, 1)
    sink = inp["attn_sink"][ls]
    for j in range(4):
        psm[:, 0:64, C_SINK + j] = sink[:, 2 * j][:, None]
        psm[:, 64:128, C_SINK + j] = sink[:, 2 * j + 1][:, None]
    ldt_r = ldt.reshape(L, 2, 2, 8)
    psm[:, :, C_LDTR:C_LDTR + 4] = np.repeat(ldt_r.transpose(0, 3, 1, 2).reshape(L, 8, 4), 16, axis=1)
    ldt_s = ldt.reshape(L, 2, 2, 4, 2)
    psm[:, :, C_LDTS:C_LDTS + 16] = np.repeat(ldt_s.transpose(0, 4, 1, 2, 3).reshape(L, 2, 16), 64, axis=1)
    ars = a_re.reshape(L, 2, 2, 4, 2, 64)
    psm[:, :, C_ARS:C_ARS + 16] = ars.transpose(0, 4, 5, 1, 2, 3).reshape(L, 128, 16)
    ais = a_im.reshape(L, 2, 2, 4, 2, 64)
    psm[:, :, C_AIS:C_AIS + 16] = ais.transpose(0, 4, 5, 1, 2, 3).reshape(L, 128, 16)
    mask = np.zeros((128, 4, 2), np.float32)
    for g8 in range(8):
        mask[g8 * 16:(g8 + 1) * 16, g8 // 2, g8 % 2] = 1.0
    psm[:, :, C_MASK:C_MASK + 8] = mask.reshape(128, 8)[None]
    for pp_ in range(128):
        psm[:, pp_, C_MASK2 + (pp_ // 16) % 2] = 1.0
    psm[:, 96:128, C_MASK3] = 1.0
    for ct in range(2):
        psm[:, 0:64, C_INVW + ct] = 1.0 / (2, 4, 8, 16)[2 * ct]
        psm[:, 64:128, C_INVW + ct] = 1.0 / (2, 4, 8, 16)[2 * ct + 1]

    pch = np.zeros((L, 128, 4, 4, 64), np.float32)
    br = b_re.reshape(L, 2, 2, 8, 64, 16)
    bi = b_im.reshape(L, 2, 2, 8, 64, 16)
    pch[:, :, 0] = br.transpose(0, 3, 5, 1, 2, 4).reshape(L, 128, 4, 64)
    pch[:, :, 1] = bi.transpose(0, 3, 5, 1, 2, 4).reshape(L, 128, 4, 64)
    ar = a_re.reshape(L, 2, 2, 8, 64)
    ai = a_im.reshape(L, 2, 2, 8, 64)
    pch[:, :, 2] = np.repeat(ar.transpose(0, 3, 1, 2, 4).reshape(L, 8, 4, 64), 16, axis=1)
    pch[:, :, 3] = np.repeat(ai.transpose(0, 3, 1, 2, 4).reshape(L, 8, 4, 64), 16, axis=1)
    pch = pch.reshape(L, 128, 1024)

    cpad = np.zeros((L, 2, 2, 64, 2, 2, 4, 8, 16), np.float32)
    cr = c_re.reshape(L, 2, 2, 4, 2, 16, 64)
    ci = c_im.reshape(L, 2, 2, 4, 2, 16, 64)
    for gp in range(4):
        for gg in range(2):
            cpad[:, 0, gg, :, :, :, gp, 2 * gp + gg, :] = cr[:, :, :, gp, gg].transpose(0, 4, 1, 2, 3)
            cpad[:, 1, gg, :, :, :, gp, 2 * gp + gg, :] = ci[:, :, :, gp, gg].transpose(0, 4, 1, 2, 3)
    cpad = cpad.reshape(L, 2, 128, 2048)
    return dict(w1=w1, w2=w2, wo=wo, gluw=gluw, poolw=f(poolw), psmall=psm, pchan=f(pch), cpad=f(cpad))


_NC_CACHE = {}


def _get_nc(L, from_out_first=False):
    key = (L, from_out_first)
    if key not in _NC_CACHE:
        _NC_CACHE[key] = build(L, from_out_first)
    return _NC_CACHE[key]


FUSED = True
DBG = {}


def kernel(**inputs):
    x = np.asarray(inputs["x"], dtype=np.float32)
    B = x.shape[0]
    inp = {k: np.asarray(v, dtype=np.float32) for k, v in inputs.items()}
    cd, cv, cio, crat, cid = _consts()
    xT = [np.ascontiguousarray(x[b].T) for b in range(B)]
    if FUSED:
        groups = [list(range(DEPTH))]
    else:
        groups = [[l] for l in range(DEPTH)]
    for ls in groups:
        nc = _get_nc(len(ls))
        lw = _prep_layer_inputs(inp, ls)
        in_maps = []
        for b in range(B):
            m = dict(lw)
            m.update(xT=xT[b], cdist=cd, cvalid=cv, ciota=cio, cratio=crat, cident=cid)
            in_maps.append(m)
        res = run_bass_kernel_spmd(nc, in_maps, core_ids=list(range(B)))
        xT = [np.asarray(res.results[b]["out"]) for b in range(B)]
    return np.stack([xT[b].T for b in range(B)], axis=0).astype(np.float32)
```

```python
import math
from contextlib import ExitStack

import numpy as np
import concourse.bass as bass
import concourse.mybir as mybir
from concourse.bass_utils import run_bass_kernel_spmd

F32 = mybir.dt.float32
BF16 = mybir.dt.bfloat16
I32 = mybir.dt.int32
ALU = mybir.AluOpType
AF = mybir.ActivationFunctionType

D = 1024
SEQ = 4096
DEPTH = 4
W = 256
NS = SEQ // W
NSM = 93
TCH = 4
NCH = W // TCH
EPS = 1e-6
TWO_PI = 2.0 * math.pi
SC2 = TWO_PI * (1.0 - 2e-6)
SC1 = math.pi * (1.0 - 2e-6)

C_GPRE, C_GPOST, C_SD, C_GB, C_PSC, C_SINK, C_LDTR, C_LDTS, C_ARS, C_AIS, C_MASK, C_INVW = (
    0, 8, 16, 18, 22, 24, 28, 32, 48, 64, 80, 88)
C_MASK2, C_MASK3 = 90, 92


class Prog:
    ENGS = ["sp", "pe", "act", "dve", "pool"]
    SAME_SKIP = {"pe"}

    def __init__(self):
        self.ops = []
        self.lastw = {}
        self.readers = {}
        self.dmacnt = {}

    def add(self, eng, fn, r=(), w=(), dma=None, force=False):
        i = len(self.ops)
        deps = set()
        for k in r:
            if k in self.lastw:
                deps.add(self.lastw[k])
        for k in w:
            if k in self.lastw:
                deps.add(self.lastw[k])
            deps.update(self.readers.get(k, ()))
        for k in r:
            self.readers.setdefault(k, []).append(i)
        for k in w:
            self.lastw[k] = i
            self.readers[k] = []
        op = dict(eng=eng, fn=fn, deps=deps, dma=dma, needed=False, force=force)
        if dma is not None:
            self.dmacnt[dma] = self.dmacnt.get(dma, 0) + 16
            op["dcount"] = self.dmacnt[dma]
        self.ops.append(op)
        return i

    def pe(self, fn, r=(), w=()):
        return self.add("pe", fn, r, w)

    def act(self, fn, r=(), w=()):
        return self.add("act", fn, r, w)

    def dve(self, fn, r=(), w=()):
        return self.add("dve", fn, r, w)

    def pool(self, fn, r=(), w=()):
        return self.add("pool", fn, r, w)

    def dma(self, fn, key, r=(), w=()):
        return self.add("sp", fn, r, w, dma=key)


    def mm(self, out, lhsT, rhs, start, stop, r=(), w=(), force=False):
        return self.add("pe", lambda e: e.matmul(out, lhsT=lhsT, rhs=rhs, start=start, stop=stop), r, w, force=force)

    def actf(self, out, in_, func, r=(), w=(), scale=None, bias=None):
        kw = {}
        if scale is not None:
            kw["scale"] = scale
        if bias is not None:
            kw["bias"] = bias
        return self.add("act", lambda e: e.activation(out=out, in_=in_, func=func, **kw), r, w)

    def amul(self, out, in_, mul, r=(), w=()):
        return self.add("act", lambda e: e.mul(out, in_, mul), r, w)

    def acopy(self, out, in_, r=(), w=()):
        return self.add("act", lambda e: e.copy(out, in_), r, w)

    def tt(self, eng, out, in0, in1, op, r=(), w=()):
        return self.add(eng, lambda e: e.tensor_tensor(out=out, in0=in0, in1=in1, op=op), r, w)

    def ts(self, eng, out, in0, s1, s2, op0, op1=None, r=(), w=()):
        if op1 is None:
            return self.add(eng, lambda e: e.tensor_scalar(out=out, in0=in0, scalar1=s1, scalar2=None, op0=op0), r, w)
        return self.add(eng, lambda e: e.tensor_scalar(out=out, in0=in0, scalar1=s1, scalar2=s2, op0=op0, op1=op1), r, w)

    def stt(self, eng, out, in0, scalar, in1, op0, op1, r=(), w=()):
        return self.add(eng, lambda e: e.scalar_tensor_tensor(out=out, in0=in0, scalar=scalar, in1=in1, op0=op0, op1=op1), r, w)

    def scan(self, out, d0, d1, init, r=(), w=()):
        return self.add("dve", lambda e: e.tensor_tensor_scan(out=out, data0=d0, data1=d1, initial=init,
                                                              op0=ALU.mult, op1=ALU.add), r, w)

    def recip(self, out, in_, r=(), w=()):
        return self.add("dve", lambda e: e.reciprocal(out=out, in_=in_), r, w)

    def copy(self, eng, out, in_, r=(), w=()):
        return self.add(eng, lambda e: e.tensor_copy(out=out, in_=in_), r, w)

    def memset(self, eng, ap, val, w=()):
        return self.add(eng, lambda e: e.memset(ap, val), (), w)

    def dmas(self, out, in_, key, r=(), w=(), q="sp"):
        return self.add(q, lambda e: e.dma_start(out=out, in_=in_), r, w, dma=key)

    def _skip(self, op, dop):
        if op["force"]:
            return False
        return dop["dma"] is None and dop["eng"] == op["eng"] and op["eng"] in self.SAME_SKIP

    def emit(self, nc, es):
        ops = self.ops
        for op in ops:
            for d in op["deps"]:
                dop = ops[d]
                if dop["dma"] is None and not self._skip(op, dop):
                    dop["needed"] = True
        cnt = {e: 0 for e in self.ENGS}
        for op in ops:
            if op["dma"] is None and op["needed"]:
                cnt[op["eng"]] += 1
                op["sig"] = cnt[op["eng"]]
        csem = {e: es.enter_context(nc.semaphore("c_" + e)) for e in ["pe", "act", "dve", "pool"]}
        dsem = {k: es.enter_context(nc.semaphore("d_%d" % i)) for i, k in enumerate(self.dmacnt)}
        block = es.enter_context(nc.Block())
        regs = {"sp": block.sync, "pe": block.tensor, "act": block.scalar, "dve": block.vector,
                "pool": block.gpsimd}
        for eng in self.ENGS:
            def body(e, eng=eng):
                waited = {}
                for op in ops:
                    if op["eng"] != eng:
                        continue
                    for d in sorted(op["deps"]):
                        dop = ops[d]
                        if dop["dma"] is not None:
                            key, val, sem = ("d", dop["dma"]), dop["dcount"], dsem[dop["dma"]]
                        else:
                            if self._skip(op, dop):
                                continue
                            key, val, sem = ("c", dop["eng"]), dop["sig"], csem[dop["eng"]]
                        if waited.get(key, 0) >= val:
                            continue
                        e.wait_ge(sem, val)
                        waited[key] = val
                    ins = op["fn"](e)
                    if op["dma"] is not None:
                        ins.then_inc(dsem[op["dma"]], 16)
                    elif op["needed"]:
                        ins.then_inc(csem[eng], 1)
                if eng == "sp":
                    for k, v in self.dmacnt.items():
                        e.wait_ge(dsem[k], v)
            regs[eng](body)


class Rec:
    def __init__(self):
        self.calls = []

    def __getattr__(self, name):
        def f(*a, **k):
            self.calls.append((name, a, k))
        return f

    def thunks(self, prog):
        return [lambda n=n, a=a, k=k: getattr(prog, n)(*a, **k) for (n, a, k) in self.calls]


def build(L, from_out_first=False):
    nc = bass.Bass("TRN2", target_bir_lowering=False)
    dr = lambda name, shape, dt=F32, kind="ExternalInput": nc.dram_tensor(name, shape, dt, kind=kind).ap()
    xT = dr("xT", [D, SEQ])
    w1 = dr("w1", [L, D, 896])
    w2 = dr("w2", [L, D, 1536])
    wo = dr("wo", [L, D, D])
    gluw = dr("gluw", [L, 256, 512])
    poolw = dr("poolw", [L, 128, 256])
    psmall = dr("psmall", [L, 128, NSM])
    pchan = dr("pchan", [L, 128, 1024])
    cpad = dr("cpad", [L, 2, 128, 2048])
    cdist = dr("cdist", [128, 384])
    cvalid = dr("cvalid", [128, 384])
    ciota = dr("ciota", [128, W])
    cratio = dr("cratio", [128, 32])
    cident = dr("cident", [128, 128])
    out = dr("out", [D, SEQ], kind="ExternalOutput")
    skind = "ExternalOutput" if DBG.get("dump") else "Internal"
    susc = nc.dram_tensor("susc", [128, 2, SEQ], BF16, kind=skind).ap()
    ktsc = nc.dram_tensor("ktsc", [128, 2, SEQ], BF16, kind=skind).ap()
    pusc = nc.dram_tensor("pusc", [128, 2, SEQ + 16], F32, kind=skind).ap()
    vsc = nc.dram_tensor("vsc", [128, 32, 128], BF16, kind=skind).ap()
    yfsc = nc.dram_tensor("yfsc", [128, 2, SEQ], F32, kind=skind).ap()

    if DBG.get("dump"):
        dbg32 = nc.dram_tensor("dbg32", [128, 96], F32, kind="ExternalOutput").ap()
        dbgb = nc.dram_tensor("dbgb", [128, 5, 2048], BF16, kind="ExternalOutput").ap()
    xT_v = xT.rearrange("(kc p) t -> p kc t", p=128)
    out_v = out.rearrange("(kc p) t -> p kc t", p=128)

    es = ExitStack()
    with es:
        sb = lambda name, shape, dt=F32: es.enter_context(nc.sbuf_tensor(name, shape, dt))
        w1b = sb("w1b", [128, 8, 896], BF16)
        w2b = sb("w2b", [128, 8, 1536], BF16)
        wob = sb("wob", [128, 8, 1024], BF16)
        glub = sb("glub", [128, 2, 512], BF16)
        plwb = sb("plwb", [128, 2, 128], BF16)
        WS = sb("WS", [128, 2, 4, 2, 128], BF16)
        WS3 = sb("WS3", [128, 2, 4, 2, 128], BF16)
        WC = sb("WC", [128, 8, 4, 2, 128], BF16)
        Kt = sb("Kt", [128, 2, 4, 128], BF16)
        identb = sb("identb", [128, 128], BF16)
        stage = [sb("stage%d" % i, [128, 1024]) for i in range(2)]
        psm = sb("psm", [128, NSM])
        psm_g = sb("psm_g", [128, 16])
        Eb = sb("Eb", [128, 8, 384], BF16)
        onesb = sb("onesb", [128, 128], BF16)
        onesLR = sb("onesLR", [128, 2, 128], BF16)
        iota = sb("iota", [128, W])
        ratio = sb("ratio", [128, 2, 2, 8])
        zer = sb("zer", [128, 16])
        dmy = sb("dmy", [128, 2])
        rr = sb("rr", [128, 16])
        ff = sb("ff", [128, 16])
        dts = sb("dts", [128, 16])
        k16 = sb("k16", [128, 16], I32)
        u16 = sb("u16", [128, 16])
        off = sb("off", [128, 16])
        bo1 = sb("bo1", [128, 16])
        bo2 = sb("bo2", [128, 16])
        carry = sb("carry", [128, 16, 2])
        esink = sb("esink", [128, 4])
        dtc = sb("dtc", [128, 4])
        xs = [sb("xs%d" % i, [128, 8, W]) for i in range(2)]
        sqb = sb("sqb", [128, 8, W], BF16)
        hT = sb("hT", [128, 8, W], BF16)
        sd = sb("sd", [128, W])
        rstd = sb("rstd", [128, W])
        cA = sb("cA", [128, 8, NCH]); cK = sb("cK", [128, 8, NCH], I32)
        cSn = sb("cSn", [128, 8, NCH]); cCs = sb("cCs", [128, 8, NCH])
        cVr = sb("cVr", [128, 8, NCH]); cVi = sb("cVi", [128, 8, NCH])
        cGr = sb("cGr", [128, 8, NCH]); cGi = sb("cGi", [128, 8, NCH])
        Hre = sb("Hre", [128, 8, NCH + 2], BF16); Him = sb("Him", [128, 8, NCH + 2], BF16)
        ffT = sb("ffT", [128, 16]); rrT = sb("rrT", [128, 16]); offT = sb("offT", [128, 16]); ffp = sb("ffp", [128, 16])
        zsm = sb("zsm", [128, 5, 2, 8])
        carc = sb("carc", [128, 2, 8])
        su_s = [sb("su_s%d" % i, [128, 2, W], BF16) for i in range(2)]
        kt_s = sb("kt_s", [128, 2, W], BF16)
        pu_s = sb("pu_s", [128, 2, W])
        v_s = sb("v_s", [128, 2, 128], BF16)
        yf_s = [sb("yf_s%d" % i, [128, W]) for i in range(2)]
        su_l = [sb("su_l%d" % i, [128, 2, W], BF16) for i in range(2)]
        kt_l = sb("kt_l", [128, 2, 4 * 128], BF16)
        pu_l = sb("pu_l", [128, 2, W + 16])
        v_l = sb("v_l", [128, 4, 128], BF16)
        VLR = sb("VLR", [128, 4, 2, 2, 128], BF16)
        yf_l = [sb("yf_l%d" % i, [128, 2, W]) for i in range(2)]
        gate_s = sb("gate_s", [128, 2, W])
        gate_p = sb("gate_p", [128, 2, W])
        gate_a = sb("gate_a", [128, 4, W])
        qT = sb("qT", [128, 4, W], BF16)
        yg = [sb("yg%d" % i, [128, 2, W], BF16) for i in range(2)]
        yo2 = sb("yo2", [128, 8, W])
        pch = yo2[:, 0:4, :].rearrange("p a (b c) -> p a b c", b=4)
        ytmp = sb("ytmp", [128, W])
        mix = sb("mix", [128, 8, W], BF16)
        sig = sb("sig", [128, W])
        glt = sb("glt", [128, W])
        pA = sb("pA", [128, W + 16])
        pB = sb("pB", [128, W + 16])
        pC = sb("pC", [128, W + 16])
        pD = sb("pD", [128, W + 16])
        pmean = sb("pmean", [128, W])
        mixed = sb("mixed", [128, 2, W], BF16)
        pexp = [sb("pexp%d" % i, [128, 384], BF16) for i in range(2)]
        pT = [sb("pT%d" % i, [128, 384], BF16) for i in range(4)]
        dn = sb("dn", [128, 256])
        o1 = sb("o1", [128, 256])
        banks = [es.enter_context(nc.psum_tensor("bank%d" % i, [128, 512], F32)) for i in range(8)]

        def bview(b, n):
            return banks[b][:, 0:n * 256].rearrange("p (a c) -> p a c", a=n)

        if DBG.get('mem'):
            print('SBUF bytes remaining', nc.sbuf_bytes_remaining)
        p = Prog()
        p_real = p
        WKALL = ["wk%d" % i for i in range(8)]
        CARRYALL = ["carry%d" % i for i in range(16)]

        def pslot(n):
            st = {"i": 0}

            def nxt():
                st["i"] += 1
                return (st["i"] - 1) % n
            return nxt

        fl = lambda ap, pat: ap.rearrange(pat)

        p.memset("dve", onesb[:], 1.0, w=["onesb"])
        p.memset("dve", onesLR[:].rearrange("p a b -> p (a b)"), 0.0, w=["onesLR"])
        p.memset("dve", onesLR[:, 0, 0:64], 1.0, w=["onesLR"])
        p.memset("dve", onesLR[:, 1, 64:128], 1.0, w=["onesLR"])
        p.memset("dve", VLR[:].rearrange("p a b c d -> p (a b c d)"), 0.0, w=["VLR"])
        p.memset("dve", zer[:], 0.0, w=["zer"])
        p.memset("dve", v_l[:].rearrange("p a b -> p (a b)"), 0.0, w=["v_l"])
        p.memset("dve", kt_l[:].rearrange("p a b -> p (a b)"), 0.0, w=["kt_l"])
        p.dmas(iota[:], ciota, "c0", w=["iota"])
        p.dmas(stage[0][:, 0:128], cident, "stg0", w=["stage0", "stage0b"])
        p.copy("dve", identb[:], stage[0][:, 0:128], r=["stage0"], w=["identb"])
        p.dmas(ratio[:].rearrange("p a b c -> p (a b c)"), cratio, "c1", w=["ratio"])
        for ct in range(2):
            p.dmas(pusc[:, ct, 0:8], zer[:, 0:8], "c2", r=["zer"], w=["pusc_pad"])
            p.dmas(pusc[:, ct, SEQ + 8:SEQ + 16], zer[:, 0:8], "c2", r=["zer"], w=["pusc_pad"])
        p.dmas(stage[0][:, 0:384], cdist, "stg0", w=["stage0", "stage0b"])
        p.dmas(stage[1][:, 0:384], cvalid, "stg1", w=["stage1", "stage1b"])
        for h in range(8):
            slope = 2.0 ** (-(h + 1))
            p.actf(stage[0][:, 384:768], stage[0][:, 0:384], AF.Exp, r=["stage0"], w=["stage0b"], scale=-slope)
            p.tt("dve", Eb[:, h, :], stage[0][:, 384:768], stage[1][:, 0:384], ALU.mult,
                 r=["stage0b", "stage1"], w=["Eb"])

        stg_i = [0]

        def stage_load(src_ap, ncols):
            s = stg_i[0] % 2
            stg_i[0] += 1
            key = "stage%d" % s
            p.dmas(stage[s][:, 0:ncols], src_ap, "stg%d" % s, w=[key, key + "b"])
            return s, key

        def prep_small(l):
            p.dmas(psm[:], psmall[l], "psm", w=["psm"])
            p.actf(esink[:], psm[:, C_SINK:C_SINK + 4], AF.Exp, r=["psm"], w=["esink"])

        def prep_weights_p1(l):
            ch = []
            for kc in range(8):
                def f(kc=kc):
                    s, key = stage_load(w1[l, kc * 128:(kc + 1) * 128, :], 896)
                    p.amul(w1b[:, kc, :], stage[s][:, 0:896], psm_g[:, C_GPRE + kc:C_GPRE + kc + 1], r=[key, "psm_g"], w=["w1b"])
                ch.append(f)
            return ch

        def prep_weights_main(l):
            ch = []
            for kc in range(8):
                for hcol in range(2):
                    def f(kc=kc, hcol=hcol):
                        c0 = hcol * 768
                        s, key = stage_load(w2[l, kc * 128:(kc + 1) * 128, c0:c0 + 768], 768)
                        p.amul(w2b[:, kc, c0:c0 + 768], stage[s][:, 0:768], psm_g[:, C_GPRE + kc:C_GPRE + kc + 1],
                               r=[key, "psm_g"], w=["w2b"])
                    ch.append(f)
            for kc in range(8):
                def f(kc=kc):
                    s, key = stage_load(wo[l, kc * 128:(kc + 1) * 128, :], 1024)
                    p.acopy(wob[:, kc, :], stage[s][:, 0:1024], r=[key], w=["wob"])
                ch.append(f)
            for kc in range(2):
                def f(kc=kc):
                    s, key = stage_load(gluw[l, kc * 128:(kc + 1) * 128, :], 512)
                    p.acopy(glub[:, kc, :], stage[s][:, 0:512], r=[key], w=["glub"])
                ch.append(f)

            def f():
                s, key = stage_load(poolw[l], 256)
                p.acopy(plwb[:].rearrange("p a b -> p (a b)"), stage[s][:, 0:256], r=[key], w=["plwb"])
            ch.append(f)
            return ch

        def prep_gains(l):
            p.dmas(psm_g[:], psmall[l, :, 0:16], "psmg", w=["psm_g"])

        cbase = [yo2[:, 0:2, :].rearrange("p a b -> p (a b)").rearrange("p (a b) -> p a b", a=8),
                 pu_s[:].rearrange("p a b -> p (a b)").rearrange("p (a b) -> p a b", a=8)]
        cbase_key = ["yo2", "pu_s"]
        CB = [cA, cSn, cCs, cVr, cVi, cGr, cGi]
        CBK = ["cA", "cSn", "cCs", "cVr", "cVi", "cGr", "cGi"]
        GRALL = ["cGr"] + ["cGr_%d" % t_ for t_ in range(8)]
        GIALL = ["cGi"] + ["cGi_%d" % t_ for t_ in range(8)]
        ALLC = CBK + ["cK", "Hre", "Him"] + GRALL[1:] + GIALL[1:]

        def prep_tables(l, d, rec=None, which="AB"):
            p = rec if (rec is not None) else p_real
            fl2 = lambda t: t[:].rearrange("p a b -> p (a b)")
            sl = slice(8 * d, 8 * d + 8)
            if "A" in which:
                fl2 = lambda t: t[:].rearrange("p a b -> p (a b)")
                T = lambda i: fl2(CB[i // 2])[:, (i % 2) * 256:(i % 2) * 256 + 256]
                T3 = lambda i: T(i).rearrange("p (a b) -> p a b", a=4)
                H3 = lambda i: T3(i)[:, 2 * d:2 * d + 2, :]
                R = ALLC + ["yo2", "psm", "dtc"]
                Wk = ALLC
                p.dmas(yo2[:, 0:4, :].rearrange("p a b -> p (a b)"), pchan[l], "pch", w=["yo2"])
                BTre, BTim, AR, AI = (pch[:, 0, :, :], pch[:, 1, :, :], pch[:, 2, :, :], pch[:, 3, :, :])
                p.actf(dtc[:], psm[:, C_LDTR:C_LDTR + 4], AF.Exp, r=["psm"], w=["dtc"])
                dtb = dtc[:].unsqueeze(2).to_broadcast([128, 4, 64])
                kint = fl2(cK)[:, 0:256]
                p.tt("dve", T3(0), AR, dtb, ALU.mult, r=R, w=Wk)
                p.tt("dve", T3(1), AI, dtb, ALU.mult, r=R, w=Wk)
                p.actf(T(2), T(0), AF.Exp, r=R, w=Wk)
                p.ts("dve", kint, T(1), 1.0 / TWO_PI, None, ALU.mult, r=R, w=Wk)
                p.stt("dve", T(3), T(1), 1.0 / TWO_PI, kint, ALU.mult, ALU.subtract, r=R, w=Wk)
                p.actf(T(4), T(3), AF.Sin, r=R, w=Wk, scale=SC2)
                p.actf(T(5), T(3), AF.Sin, r=R, w=Wk, scale=SC1)
                p.actf(T(5), T(5), AF.Square, r=R, w=Wk)
                p.ts("dve", T(5), T(5), -2.0, 1.0, ALU.mult, ALU.add, r=R, w=Wk)
                p.tt("dve", T(6), T(2), T(5), ALU.mult, r=R, w=Wk)
                p.tt("dve", T(7), T(2), T(4), ALU.mult, r=R, w=Wk)
                p.ts("dve", T(8), T(6), -1.0, None, ALU.add, r=R, w=Wk)
                p.tt("dve", T3(0), AR, AR, ALU.mult, r=R, w=Wk)
                p.tt("dve", T3(1), AI, AI, ALU.mult, r=R, w=Wk)
                p.tt("dve", T(0), T(0), T(1), ALU.add, r=R, w=Wk)
                p.recip(T(0), T(0), r=R, w=Wk)
                p.tt("dve", T3(1), T3(8), AR, ALU.mult, r=R, w=Wk)
                p.tt("dve", T3(2), T3(7), AI, ALU.mult, r=R, w=Wk)
                p.tt("dve", T(1), T(1), T(2), ALU.add, r=R, w=Wk)
                p.tt("dve", T(1), T(1), T(0), ALU.mult, r=R, w=Wk)
                p.tt("dve", T3(2), T3(7), AR, ALU.mult, r=R, w=Wk)
                p.tt("dve", T3(3), T3(8), AI, ALU.mult, r=R, w=Wk)
                p.tt("dve", T(2), T(2), T(3), ALU.subtract, r=R, w=Wk)
                p.tt("dve", T(2), T(2), T(0), ALU.mult, r=R, w=Wk)
                ZR, ZI, LR, LI = 1, 2, 6, 7
                mk2 = psm[:, C_MASK2:C_MASK2 + 2].unsqueeze(1).unsqueeze(3).to_broadcast([128, 2, 2, 64])
                for k in range(4):
                    j = (3 - k) if d == 0 else k
                    p.tt("dve", H3(3), H3(ZR), BTre[:, 2 * d:2 * d + 2, :], ALU.mult, r=R, w=Wk)
                    p.tt("dve", H3(4), H3(ZI), BTim[:, 2 * d:2 * d + 2, :], ALU.mult, r=R, w=Wk)
                    p.tt("dve", H3(3), H3(3), H3(4), ALU.subtract, r=R, w=Wk)
                    p.tt("dve", H3(4), H3(ZR), BTim[:, 2 * d:2 * d + 2, :], ALU.mult, r=R, w=Wk)
                    p.tt("dve", H3(5), H3(ZI), BTre[:, 2 * d:2 * d + 2, :], ALU.mult, r=R, w=Wk)
                    p.tt("dve", H3(4), H3(4), H3(5), ALU.add, r=R, w=Wk)
                    for ri, src in ((0, 3), (1, 4)):
                        p.tt("dve", WS[:, :, j, ri, :].rearrange("p a (b c) -> p a b c", b=2),
                             H3(src).unsqueeze(2).to_broadcast([128, 2, 2, 64]), mk2, ALU.mult, r=R, w=["WS"])
                    if k < 3:
                        p.tt("dve", H3(3), H3(ZR), H3(LR), ALU.mult, r=R, w=Wk)
                        p.tt("dve", H3(4), H3(ZI), H3(LI), ALU.mult, r=R, w=Wk)
                        p.tt("dve", H3(5), H3(ZR), H3(LI), ALU.mult, r=R, w=Wk)
                        p.tt("dve", H3(9), H3(ZI), H3(LR), ALU.mult, r=R, w=Wk)
                        p.tt("dve", H3(ZR), H3(3), H3(4), ALU.subtract, r=R, w=Wk)
                        p.tt("dve", H3(ZI), H3(5), H3(9), ALU.add, r=R, w=Wk)
                p.ts("dve", WS3[64:128].rearrange("p a b c d -> p (a b c d)"), WS[64:128].rearrange("p a b c d -> p (a b c d)"),
                     psm[64:128, C_MASK3:C_MASK3 + 1], None, ALU.mult, r=["WS", "psm"], w=["WS3"])

                sl = slice(8 * d, 8 * d + 8)
                p.actf(dts[:], psm[:, C_LDTS:C_LDTS + 16], AF.Exp, r=["psm"], w=["dts"])
                p.tt("dve", u16[:], psm[:, C_ARS:C_ARS + 16], dts[:], ALU.mult, r=["psm", "dts"], w=["u16"])
                p.actf(rr[:], u16[:], AF.Exp, r=["u16"], w=["rr"])
                p.tt("dve", u16[:], psm[:, C_AIS:C_AIS + 16], dts[:], ALU.mult, r=["psm", "dts", "rr"], w=["u16"])
                p.ts("dve", k16[:], u16[:], 1.0 / TWO_PI, None, ALU.mult, r=["u16"], w=["k16"])
                p.stt("dve", ffp[:], u16[:], 1.0 / TWO_PI, k16[:], ALU.mult, ALU.subtract, r=["u16", "k16"], w=["ffp"])
                p.actf(off[:], ffp[:], AF.Sin, r=["ffp"], w=["off"], scale=SC2)
                p.actf(bo1[:], ffp[:], AF.Sin, r=["ffp"], w=["bo1"], scale=SC1)
                p.actf(bo1[:], bo1[:], AF.Square, r=["bo1"], w=["bo1"])
                p.ts("dve", bo1[:], bo1[:], -2.0, 1.0, ALU.mult, ALU.add, r=["bo1"], w=["bo1"])
                p.tt("dve", zsm[:, 1, 0, :], rr[:, sl], bo1[:, sl], ALU.mult, r=["rr", "bo1"], w=["zsm"])
                p.tt("dve", zsm[:, 1, 1, :], rr[:, sl], off[:, sl], ALU.mult, r=["rr", "off"], w=["zsm"])
                for k in range(1, 4):
                    a_r, a_i = zsm[:, k, 0, :], zsm[:, k, 1, :]
                    l_r, l_i = zsm[:, 1, 0, :], zsm[:, 1, 1, :]
                    p.tt("dve", bo2[:, 0:8], a_r, l_r, ALU.mult, r=["zsm"], w=["bo2"])
                    p.tt("dve", bo2[:, 8:16], a_i, l_i, ALU.mult, r=["zsm"], w=["bo2"])
                    p.tt("dve", zsm[:, k + 1, 0, :], bo2[:, 0:8], bo2[:, 8:16], ALU.subtract, r=["bo2"], w=["zsm"])
                    p.tt("dve", bo2[:, 0:8], a_r, l_i, ALU.mult, r=["zsm"], w=["bo2"])
                    p.tt("dve", bo2[:, 8:16], a_i, l_r, ALU.mult, r=["zsm"], w=["bo2"])
                    p.tt("dve", zsm[:, k + 1, 1, :], bo2[:, 0:8], bo2[:, 8:16], ALU.add, r=["bo2"], w=["zsm"])
                sgn = 1.0 if d == 0 else -1.0
                p.ts("dve", k16[:, sl], ffp[:, sl], sgn * TCH, None, ALU.mult, r=["ffp"], w=["k16"])
                p.stt("dve", ffT[:, sl], ffp[:, sl], sgn * TCH, k16[:, sl], ALU.mult, ALU.subtract, r=["ffp", "k16"], w=["ffT"])
                p.tt("dve", rrT[:, sl], rr[:, sl], rr[:, sl], ALU.mult, r=["rr"], w=["rrT"])
                p.tt("dve", rrT[:, sl], rrT[:, sl], rrT[:, sl], ALU.mult, r=["rrT"], w=["rrT"])


            p = p_real
            if "B" not in which:
                return
            p.dmas(stage[0][:, 0:1024], cpad[l, 0, :, d * 1024:(d + 1) * 1024], "stg0", w=["stage0", "stage0b"])
            p.dmas(stage[1][:, 0:1024], cpad[l, 1, :, d * 1024:(d + 1) * 1024], "stg1", w=["stage1", "stage1b"])
            c3 = lambda t: t[:, 0:1024].rearrange("p (a b) -> p a b", a=8)
            cre, cim = c3(stage[0]), c3(stage[1])
            t1 = yo2[:, 0:4, :].rearrange("p a b -> p (a b)").rearrange("p (a b) -> p a b", a=8)
            t2 = yo2[:, 4:8, :].rearrange("p a b -> p (a b)").rearrange("p (a b) -> p a b", a=8)
            RW = ["yo2", "stage0", "stage1", "zsm"]
            Ta0 = cGr[:].bitcast(BF16)
            Tb0 = cGi[:].bitcast(BF16)
            assert tuple(Ta0.shape) == (128, 8, 128), Ta0.shape
            p.acopy(Ta0, cre, r=["stage0"] + ALLC, w=GRALL)
            p.amul(Tb0, cim, -1.0, r=["stage1"] + ALLC, w=GIALL)

            x2 = lambda t, o: fl2(t)[:, o:o + 256].rearrange("p (a b) -> p a b", a=2)
            Xs = [x2(Hre, 0), x2(Hre, 256), x2(Him, 0), x2(Him, 256)]
            XK = ["Xs0", "Xs1", "Xs2", "Xs3"]
            p.memset("dve", fl2(Hre), 0.0, w=["Hre", "Xs0", "Xs1"])
            p.memset("dve", fl2(Him), 0.0, w=["Him", "Xs2", "Xs3"])
            def kg_transpose(ct, tau, gp):
                j = (3 - tau) if d == 0 else tau
                tb = proj_slot()
                tkey = "b%d" % tb
                if gp < 3:
                    rows, ncol, cbase, src = slice(32 * gp, 32 * gp + 32), 32, 32 * gp, WS
                else:
                    rows, ncol, cbase, src = slice(64, 128), 64, 64, WS3
                for ri in range(2):
                    p.mm(banks[tb][:, ri * 64:ri * 64 + ncol], src[rows, ct, j, ri, :], identb[rows, cbase:cbase + ncol],
                         True, True, r=["WS", "WS3", "identb"], w=[tkey])
                p.acopy(Xs[gp][:, :, cbase:cbase + ncol],
                        banks[tb][:, 0:128].rearrange("p (a b) -> p a b", a=2)[:, :, 0:ncol], r=[tkey], w=[XK[gp]])

            steps = [(ct, tau, gp) for ct in range(2) for tau in range(4) for gp in range(4)]
            kg_transpose(*steps[0])
            for si, (ct, tau, gp) in enumerate(steps):
                if si + 1 < len(steps):
                    kg_transpose(*steps[si + 1])
                kb_ = 2
                tile = ct * 4 + gp
                p.mm(banks[kb_][:, 0:128], Xs[gp][:, 0, :], Ta0[:, tile, :], gp == 0, False, r=[XK[gp], "cGr"], w=["b2"])
                p.mm(banks[kb_][:, 0:128], Xs[gp][:, 1, :], Tb0[:, tile, :], False, gp == 3, r=[XK[gp], "cGi"], w=["b2"])
                if gp == 3:
                    p.acopy(Kt[:, ct, tau, :], banks[kb_][:, 0:128], r=["b2"], w=["Kt"])
            for i in range(4):
                k = (i + 1) if d == 0 else (TCH - i)
                zr = zsm[:, k, 0, :].unsqueeze(2).to_broadcast([128, 8, 128])
                zi = zsm[:, k, 1, :].unsqueeze(2).to_broadcast([128, 8, 128])
                p.tt("dve", t1, cre, zr, ALU.mult, r=RW, w=["yo2"])
                p.tt("dve", t2, cim, zi, ALU.mult, r=RW, w=["yo2"])
                p.tt("dve", WC[:, :, i, 0, :], t1, t2, ALU.subtract, r=RW, w=["WC"])
                p.tt("dve", t1, cre, zi, ALU.mult, r=RW, w=["yo2"])
                p.tt("dve", t2, cim, zr, ALU.mult, r=RW, w=["yo2"])
                p.stt("dve", WC[:, :, i, 1, :], t1, -1.0, t2, ALU.mult, ALU.subtract, r=RW, w=["WC"])
            p.memset("dve", fl2(Hre), 0.0, w=["Hre", "Xs0", "Xs1"])
            p.memset("dve", fl2(Him), 0.0, w=["Him", "Xs2", "Xs3"])
            p.memset("dve", carc[:].rearrange("p a b -> p (a b)"), 0.0, w=["carc"])
            p.tt("dve", cbase[d], iota[:, 0:NCH].unsqueeze(1).to_broadcast([128, 8, NCH]),
                 ffT[:, sl].unsqueeze(2).to_broadcast([128, 8, NCH]), ALU.mult, r=["iota", "ffT", cbase_key[d]], w=[cbase_key[d]])

        proj_slot = pslot(2)

        def norm_sq(xs_t, xkey):
            p.actf(sqb[:].rearrange("p a b -> p (a b)"), xs_t[:].rearrange("p a b -> p (a b)"), AF.Square,
                   r=[xkey], w=["sqb"])

        def norm_a(xs_t, xkey, sq=True):
            if sq:
                norm_sq(xs_t, xkey)
            b = proj_slot()
            bk = "b%d" % b
            for kc in range(8):
                p.mm(banks[b][:, 0:W], onesb[:], sqb[:, kc, :], kc == 0, kc == 7, r=["sqb", "onesb"], w=[bk])
            p.actf(sd[:], banks[b][:, 0:W], AF.Ln, r=[bk], w=["sd"], scale=1.0 / D, bias=EPS)
            p.actf(rstd[:], sd[:], AF.Exp, r=["sd"], w=["rstd"], scale=-0.5)

        def norm_b(xs_t, xkey, eng="dve"):
            p.tt(eng, hT[:], xs_t[:], rstd[:].unsqueeze(1).to_broadcast([128, 8, W]), ALU.mult,
                 r=[xkey, "rstd"], w=["hT"])

        def norm_slab(xs_t, xkey):
            norm_a(xs_t, xkey)
            norm_b(xs_t, xkey)

        def proj_fm(wb, wkey, col0, nt=2):
            b = proj_slot()
            key = "b%d" % b
            for t in range(nt):
                for kc in range(8):
                    p.mm(banks[b][:, t * W:(t + 1) * W], wb[:, kc, col0 + t * 128:col0 + (t + 1) * 128], hT[:, kc, :],
                         kc == 0, kc == 7, r=["hT", wkey], w=[key])
            return bview(b, nt), key

        tab_slot = pslot(2)
        pp_slot = pslot(2)
        brbi_slot = pslot(2)

        def ssm_slab3(direction, s, su_t, sukey, after_ct, fillers=(), after_s=None):
            d = direction
            fillers = list(fillers)
            sl = slice(8 * d, 8 * d + 8)
            rev = (d == 1)
            nops = 30
            per_op = max(1, -(-len(fillers) // nops)) if fillers else 0
            fl2 = lambda t: t[:].rearrange("p a b -> p (a b)")

            def fill(n=1):
                for _ in range(n * per_op):
                    if fillers:
                        fillers.pop(0)()

            for tile in range(8):
                ct, gp = tile // 4, tile % 4
                if gp < 3:
                    rows, src = slice(32 * gp, 32 * gp + 32), WS
                else:
                    rows, src = slice(64, 128), WS3
                if gp not in DBG.get("gps", (0, 1, 2, 3)):
                    continue
                for ri in range(2):
                    for j in range(TCH):
                        p.mm(banks[2 + ri][:, tile * NCH:(tile + 1) * NCH], src[rows, ct, j, ri, :], su_t[rows, ct, j:W:TCH],
                             j == 0, j == TCH - 1, r=[sukey, "WS", "WS3"], w=["b%d" % (2 + ri), "rgser"],
                             force=(ri == 0 and j == 0))
            Sre = banks[2][:, 0:8 * NCH].rearrange("p (a b) -> p a b", a=8)
            Sim = banks[3][:, 0:8 * NCH].rearrange("p (a b) -> p a b", a=8)
            c0 = float(s * NCH)
            p.ts("dve", k16[:, sl], ffT[:, sl], c0, None, ALU.mult, r=["ffT"], w=["k16"])
            p.stt("dve", offT[:, sl], ffT[:, sl], c0, k16[:, sl], ALU.mult, ALU.subtract, r=["ffT", "k16"], w=["offT"])
            bc_t = lambda t: t[:, sl].unsqueeze(2).to_broadcast([128, 8, NCH])
            io_b = iota[:, 0:NCH].unsqueeze(1).to_broadcast([128, 8, NCH])
            bk_ = cbase_key[d]
            p.tt("dve", cK[:], cbase[d], bc_t(offT), ALU.add, r=[bk_, "offT"], w=["cK"])
            p.tt("dve", cA[:], cbase[d], bc_t(offT), ALU.add, r=[bk_, "offT", "cA"], w=["cA"])
            p.tt("dve", cA[:], cA[:], cK[:], ALU.subtract, r=["cA", "cK"], w=["cA"])
            p.actf(fl2(cSn), fl2(cA), AF.Sin, r=["cA"], w=["cSn"], scale=SC2)
            p.actf(fl2(cCs), fl2(cA), AF.Sin, r=["cA"], w=["cCs"], scale=SC1)
            p.actf(fl2(cCs), fl2(cCs), AF.Square, r=["cCs"], w=["cCs"])
            p.actf(fl2(cCs), fl2(cCs), AF.Identity, r=["cCs"], w=["cCs"], scale=-2.0, bias=1.0)
            if after_s is not None:
                after_s()
            fill(4)
            p.tt("dve", cVr[:], Sre, cCs[:], ALU.mult, r=["b2", "cCs"], w=["cVr"])
            p.tt("dve", cA[:], Sim, cSn[:], ALU.mult, r=["b3", "cSn", "cA"], w=["cA"])
            fill()
            p.tt("dve", cVi[:], Sim, cCs[:], ALU.mult, r=["b3", "cCs"], w=["cVi"])
            p.tt("dve", cGr[:], Sre, cSn[:], ALU.mult, r=["b2", "cSn"], w=GRALL)
            fill()
            p.tt("dve", cVr[:], cVr[:], cA[:], ALU.add, r=["cVr", "cA"], w=["cVr"])
            p.tt("dve", cVi[:], cVi[:], cGr[:], ALU.subtract, r=["cVi"] + GRALL, w=["cVi"])
            fill()
            fw = (lambda ap: ap[:, ::-1]) if rev else (lambda ap: ap)
            for tile in range(8):
                st = 8 * d + tile
                rb = rrT[:, st:st + 1].to_broadcast([128, NCH])
                p.scan(fw(cGr[:, tile, :]), rb, fw(cVr[:, tile, :]), carc[:, 0, tile:tile + 1], r=["cVr", "rrT", "carc"],
                       w=["cGr_%d" % tile])
                p.scan(fw(cGi[:, tile, :]), rb, fw(cVi[:, tile, :]), carc[:, 1, tile:tile + 1], r=["cVi", "rrT", "carc"],
                       w=["cGi_%d" % tile])
                if tile % 2 == 1:
                    fill()
            lastc = 0 if rev else NCH - 1
            GRK = ["cGr_%d" % t_ for t_ in range(8)]
            GIK = ["cGi_%d" % t_ for t_ in range(8)]
            p.acopy(carc[:, 0, :], cGr[:, :, lastc], r=GRK, w=["carc"])
            p.acopy(carc[:, 1, :], cGi[:, :, lastc], r=GIK, w=["carc"])
            hcols = slice(1, NCH + 1)
            if not rev:
                hdst, hsrc, hrhs = 0, NCH, slice(0, NCH)
            else:
                hdst, hsrc, hrhs = NCH + 1, 1, slice(2, NCH + 2)
            p.copy("dve", Hre[:, :, hdst], Hre[:, :, hsrc], r=["Hre"], w=["Hre"])
            p.copy("dve", Him[:, :, hdst], Him[:, :, hsrc], r=["Him"], w=["Him"])
            fill()
            p.tt("dve", cVr[:], cGr[:], cCs[:], ALU.mult, r=GRK + ["cCs", "cVr"], w=["cVr"])
            p.tt("dve", cVi[:], cGi[:], cSn[:], ALU.mult, r=GIK + ["cSn", "cVi"], w=["cVi"])
            fill()
            p.tt("dve", Hre[:, :, hcols], cVr[:], cVi[:], ALU.subtract, r=["cVr", "cVi", "Hre"], w=["Hre"])
            p.tt("dve", cVr[:], cGr[:], cSn[:], ALU.mult, r=GRK + ["cSn", "cVr", "Hre"], w=["cVr"])
            fill()
            p.tt("dve", cVi[:], cGi[:], cCs[:], ALU.mult, r=GIK + ["cCs", "cVi", "Hre"], w=["cVi"])
            fill()
            p.tt("dve", Him[:, :, hcols], cVr[:], cVi[:], ALU.add, r=["cVr", "cVi", "Him"], w=["Him"])
            while fillers:
                fillers.pop(0)()

            def part_b():
              for ct in range(2):
                  ykey = "b%d" % (4 + ct)
                  for i in range(TCH):
                      yo_ = banks[4 + ct][:, i:W:TCH]
                      js = list(range(0, i + 1)) if not rev else list(range(i, TCH))
                      nmm = len(js) + 8
                      n = 0
                      for j in js:
                          tau = abs(i - j)
                          p.mm(yo_, Kt[:, ct, tau, :], su_t[:, ct, j:W:TCH], n == 0, n == nmm - 1, r=[sukey, "Kt"], w=[ykey])
                          n += 1
                      for gp in range(4):
                          tile = ct * 4 + gp
                          p.mm(yo_, WC[:, tile, i, 0, :], Hre[:, tile, hrhs], n == 0, n == nmm - 1, r=["WC", "Hre"], w=[ykey])
                          n += 1
                          p.mm(yo_, WC[:, tile, i, 1, :], Him[:, tile, hrhs], n == 0, n == nmm - 1, r=["WC", "Him"], w=[ykey])
                          n += 1
                  after_ct(ct, banks[4 + ct][:, 0:W], ykey)
            return part_b

        def p1_pieces(s, src_v, srckeyf):
            c0, c1 = s * W, (s + 1) * W
            xb = s % 2
            xkey = "xs%d" % xb
            sus, suk = su_s[s % 2], "su_s%d" % (s % 2)
            pcs = []

            def f_norm():
                norm_b(xs[xb], xkey, eng=DBG.get("p1_norm_eng", "pool"))
            pcs.append(f_norm)

            def f_sq_next():
                if s + 1 < NS:
                    norm_sq(xs[(s + 1) % 2], "xs%d" % ((s + 1) % 2))
            pcs.append(f_sq_next)

            def f_su():
                ps, key = proj_fm(w1b, "w1b", 0)
                p.acopy(sus[:], ps, r=[key], w=[suk])
                p.dmas(susc[:, :, c0:c1], sus[:], "st_su%d" % (s % 2), r=[suk], w=["susc"])
            pcs.append(f_su)

            def f_pu():
                ps, key = proj_fm(w1b, "w1b", 256)
                p.acopy(pu_s[:], ps, r=[key], w=["pu_s"])
                p.dmas(pusc[:, :, 8 + c0:8 + c1], pu_s[:], "st_pu", r=["pu_s"], w=["pusc"])
            pcs.append(f_pu)

            def f_kt():
                ps, key = proj_fm(w1b, "w1b", 512)
                p.acopy(kt_s[:], ps, r=[key], w=["kt_s"])
                p.dmas(ktsc[:, :, c0:c1], kt_s[:], "st_kt", r=["kt_s"], w=["ktsc"])
            pcs.append(f_kt)

            def f_v():
                b = proj_slot()
                key = "b%d" % b
                for blk in range(2):
                    for kc in range(8):
                        p.mm(banks[b][:, blk * 128:(blk + 1) * 128], hT[:, kc, blk * 128:(blk + 1) * 128], w1b[:, kc, 768:896],
                             kc == 0, kc == 7, r=["hT", "w1b"], w=[key])
                p.acopy(v_s[:], banks[b][:, 0:256].rearrange("p (a c) -> p a c", a=2), r=[key], w=["v_s"])
                p.dmas(vsc[:, 2 * s:2 * s + 2, :], v_s[:], "st_v", r=["v_s"], w=["vsc"])
            pcs.append(f_v)

            def f_next_norm():
                if s + 1 < NS:
                    nb = (s + 1) % 2
                    norm_a(xs[nb], "xs%d" % nb, sq=False)
                p.actf(dmy[:, 0:1], zer[:, 0:1], AF.Sin, r=["zer"], w=["dmy"])
                if s + 2 < NS:
                    p.dmas(xs[xb][:], src_v[:, :, c1 + W:c1 + 2 * W], "xs%d" % xb, r=[srckeyf(s + 2)], w=["xs%d" % xb])
            pcs.append(f_next_norm)
            return pcs

        def interleave(pieces, thunks):
            thunks = list(thunks)
            per = -(-len(thunks) // max(len(pieces), 1))
            for pc in pieces:
                pc()
                for _ in range(per):
                    if thunks:
                        thunks.pop(0)()
            while thunks:
                thunks.pop(0)()

        def p1_prologue(l, src_v, srckeyf, thunks=()):
            def f0():
                p.dmas(xs[0][:], src_v[:, :, 0:W], "xs0", r=[srckeyf(0)], w=["xs0"])
                p.dmas(xs[1][:], src_v[:, :, W:2 * W], "xs1", r=[srckeyf(1)], w=["xs1"])
                norm_a(xs[0], "xs0")
            interleave([f0] + p1_pieces(0, src_v, srckeyf), thunks)

        def phase1(l, src_v, srckeyf, deferred=()):
            deferred = list(deferred)
            pend = None
            for s in range(DBG.get("p1", NS)):
                c0, c1 = s * W, (s + 1) * W
                nxt_p = p1_pieces(s + 1, src_v, srckeyf) if s + 1 < NS else []
                if nxt_p:
                    nxt_p.pop(0)()
                fill = []
                for _ in range(2):
                    if deferred:
                        fill.append(deferred.pop(0))
                fill = nxt_p + fill

                def after_f(ct, ps, key, c0=c0, c1=c1):
                    p.acopy(yf_s[ct][:], ps, r=[key], w=["yf_s%d" % ct])
                    p.dmas(yfsc[:, ct, c0:c1], yf_s[ct][:], "st_yf%d" % ct, r=["yf_s%d" % ct], w=["yfsc"])
                pend = ssm_slab3(0, s, su_s[s % 2], "su_s%d" % (s % 2), after_f, fillers=fill, after_s=pend)
            if pend is not None:
                pend()
            while deferred:
                deferred.pop(0)()

        s_slot = pslot(2)
        od_slot = pslot(2)
        od_bank = lambda: 0 + proj_slot()
        pt_slot = pslot(4)
        pe_slot = pslot(2)

        def attention_pairs(s, nb_lo):
            v3 = lambda ap: ap.rearrange("p (a b) -> p a b", a=2)
            pairs = [(nloc, jp, jj) for nloc in range(2) for jp in range(2) for jj in range(2)]
            state = {}

            def stage1(nloc, jp, jj):
                n = 2 * s + nloc
                kbs = [kb for kb in (n - 1, n, n + 1) if 0 <= kb < 32]
                rel0 = kbs[0] - (n - 1)
                nk = len(kbs)
                qc0 = nloc * 128
                j = jp * 2 + jj
                kv = j // 2
                pts = []
                for hh in range(2):
                    h = 2 * j + hh
                    ss = s_slot()
                    skey = "b%d" % (6 + ss)
                    s_ps = banks[6 + ss]
                    rows = slice(64 * hh, 64 * hh + 64)
                    for ki_, kb in enumerate(kbs):
                        kl = kb - nb_lo
                        p.mm(s_ps[:, ki_ * 128:(ki_ + 1) * 128], kt_l[rows, kv, kl * 128:(kl + 1) * 128],
                             qT[rows, j, qc0:qc0 + 128], True, True, r=["kt_l", "qT"], w=[skey])
                    pe_i = pe_slot()
                    pe_, pk = pexp[pe_i], "pexp%d" % pe_i
                    p.actf(pe_[:, 0:nk * 128], s_ps[:, 0:nk * 128], AF.Exp, r=[skey], w=[pk])
                    pi_ = pt_slot()
                    pt_, tk = pT[pi_], "pT%d" % pi_
                    p.tt("dve", pt_[:, 0:nk * 128], pe_[:, 0:nk * 128], Eb[:, h, rel0 * 128:(rel0 + nk) * 128], ALU.mult,
                         r=[pk, "Eb"], w=[tk])
                    pts.append((pt_, tk))
                state[(nloc, jp, jj)] = (pts, kbs)

            def stage2(nloc, jp, jj):
                pts, kbs = state.pop((nloc, jp, jj))
                nk = len(kbs)
                qc0 = nloc * 128
                j = jp * 2 + jj
                kv = j // 2
                if jj == 0:
                    state[("od", nloc, jp)] = proj_slot()
                ob = state[("od", nloc, jp)]
                okey = "b%d" % ob
                odv = banks[ob][:].rearrange("p (j t c) -> p j t c", j=2, t=2)
                for t in range(2):
                    for hh in range(2):
                        pt_, tk = pts[hh]
                        for ki_, kb in enumerate(kbs):
                            kl = kb - nb_lo
                            first = (hh == 0 and ki_ == 0)
                            last = (hh == 1 and ki_ == nk - 1)
                            lhs = VLR[:, kl, kv, hh, :] if t == 0 else onesLR[:, hh, :]
                            p.mm(odv[:, jj, t, :], lhs, pt_[:, ki_ * 128:(ki_ + 1) * 128], first, last,
                                 r=[tk, "VLR", "onesLR"], w=[okey])
                if jj == 1:
                    state.pop(("od", nloc, jp))
                    p.tt("dve", v3(dn[:]), odv[:, :, 1, :], esink[:, 2 * jp:2 * jp + 2].unsqueeze(2).to_broadcast([128, 2, 128]), ALU.add,
                         r=[okey, "esink"], w=["dn"])
                    p.actf(dn[:], dn[:], AF.Ln, r=["dn"], w=["dn"])
                    p.actf(dn[:], dn[:], AF.Exp, r=["dn"], w=["dn"], scale=-1.0)
                    p.tt("dve", v3(o1[:]), odv[:, :, 0, :], v3(dn[:]), ALU.mult, r=[okey, "dn"], w=["o1"])
                    p.tt("dve", mix[:, 4 + 2 * jp:6 + 2 * jp, qc0:qc0 + 128], v3(o1[:]), gate_a[:, 2 * jp:2 * jp + 2, qc0:qc0 + 128],
                         ALU.mult, r=["o1", "gate_a"], w=["mix"])

            pcs = [lambda: stage1(*pairs[0])]
            for k in range(1, len(pairs)):
                pcs.append(lambda k=k: (stage1(*pairs[k]), stage2(*pairs[k - 1])))
            pcs.append(lambda: stage2(*pairs[-1]))
            return pcs

        def pool_slab(s):
            first, last = (s == 0), (s == NS - 1)
            n = W + 16
            lo, hi = slice(0, 64), slice(64, 128)
            for ct in range(2):
                Xc = pu_l[:, ct, :]
                p.tt("pool", pA[:, 1:n], Xc[:, 0:n - 1], Xc[:, 1:n], ALU.add, r=["pu_l"], w=["pA"])
                if ct == 0:
                    p.tt("pool", pB[hi, 2:n - 1], pA[hi, 1:n - 2], pA[hi, 3:n], ALU.add, r=["pA"], w=["pB"])
                    srcs = ((lo, pA, "pA"), (hi, pB, "pB"))
                else:
                    p.tt("pool", pB[:, 2:n - 1], pA[:, 1:n - 2], pA[:, 3:n], ALU.add, r=["pA"], w=["pB"])
                    p.tt("pool", pC[:, 4:n - 3], pB[:, 2:n - 5], pB[:, 6:n - 1], ALU.add, r=["pB"], w=["pC"])
                    p.tt("pool", pD[hi, 8:n - 7], pC[hi, 4:n - 11], pC[hi, 12:n - 3], ALU.add, r=["pC"], w=["pD"])
                    srcs = ((lo, pC, "pC"), (hi, pD, "pD"))
                for (rows, buf, bk) in srcs:
                    p.ts("pool", pmean[rows, :], buf[rows, 8:8 + W], psm[rows, C_INVW + ct:C_INVW + ct + 1], None, ALU.mult,
                         r=[bk, "psm"], w=["pmean"])
                if first:
                    p.tt("pool", pmean[:, 0:8], pmean[:, 0:8], ratio[:, ct, 0, :], ALU.mult, r=["pmean", "ratio"], w=["pmean"])
                if last:
                    p.tt("pool", pmean[:, W - 8:W], pmean[:, W - 8:W], ratio[:, ct, 1, :], ALU.mult, r=["pmean", "ratio"], w=["pmean"])
                p.tt("pool", mixed[:, ct, :], pmean[:], Xc[:, 8:8 + W], ALU.subtract, r=["pmean", "pu_l"], w=["mixed"])
                b = proj_slot()
                key = "b%d" % b
                p.mm(banks[b][:, 0:W], plwb[:, ct, :], mixed[:, ct, :], True, True, r=["mixed", "plwb"], w=[key])
                p.stt("dve", mix[:, 2 + ct, :], banks[b][:, 0:W], psm[:, C_PSC + ct:C_PSC + ct + 1], gate_p[:, ct, :], ALU.mult, ALU.mult,
                      r=[key, "psm", "gate_p"], w=["mix"])

        def ssm_b_loads(s):
            c0, c1 = s * W, (s + 1) * W
            pb = s % 2
            p.dmas(su_l[pb][:], susc[:, :, c0:c1], "ld_su%d" % pb, r=["susc"], w=["su_l%d" % pb])
            p.dmas(yf_l[pb][:], yfsc[:, :, c0:c1], "ld_yf%d" % pb, r=["yfsc"], w=["yf_l%d" % pb])

        def ssm_b(s, fillers=(), after_s=None):
            pb = s % 2

            def after_b(ct, ps, key):
                p.tt("dve", ytmp[:], ps, yf_l[pb][:, ct, :], ALU.add, r=[key, "yf_l%d" % pb], w=["ytmp"])
                p.stt("dve", ytmp[:], su_l[pb][:, ct, :], psm[:, C_SD + ct:C_SD + ct + 1], ytmp[:], ALU.mult, ALU.add,
                      r=["su_l%d" % pb, "psm", "ytmp"], w=["ytmp"])
                p.actf(yg[pb][:, ct, :], ytmp[:], AF.Gelu_apprx_tanh, r=["ytmp"], w=["yg%d" % pb])
            return ssm_slab3(1, s, su_l[pb], "su_l%d" % pb, after_b, fillers=fillers, after_s=after_s)

        out_stores = []

        def flush_stores():
            while out_stores:
                out_stores.pop(0)()

        def main_loads(s):
            c0 = s * W
            nb_lo = 2 * s - 1
            p.dmas(pu_l[:], pusc[:, :, c0:c0 + W + 16], "ld_pu", r=["pusc", "pusc_pad"], w=["pu_l"])
            blo, bhi = max(nb_lo, 0), min(nb_lo + 4, 32)
            p.dmas(kt_l[:, :, (blo - nb_lo) * 128:(bhi - nb_lo) * 128], ktsc[:, :, blo * 128:bhi * 128], "ld_kt",
                   r=["ktsc"], w=["kt_l"])
            p.dmas(v_l[:, blo - nb_lo:bhi - nb_lo, :], vsc[:, blo:bhi, :], "ld_v", r=["vsc"], w=["v_l"])

        def main_pieces(s, src_v, srckeyf):
            c0, c1 = s * W, (s + 1) * W
            xb = s % 2
            xkey = "xs%d" % xb
            nb_lo = 2 * s - 1
            pb = s % 2
            pcs = []

            def f_loads():
                if s - 1 >= 0:
                    nb = (s - 1) % 2
                    p.dmas(xs[nb][:], src_v[:, :, c0 - W:c0], "xs%d" % nb, r=[srckeyf(s - 1)], w=["xs%d" % nb])
                for kv in range(2):
                    for hh in range(2):
                        p.copy("pool", VLR[:, :, kv, hh, 64 * hh:64 * hh + 64], v_l[:, :, 64 * kv:64 * kv + 64],
                               r=["v_l"], w=["VLR"])
                norm_b(xs[xb], xkey)
            pcs.append(f_loads)


            def f_gs():
                ps, key = proj_fm(w2b, "w2b", 0)
                p.actf(gate_s[:], ps, AF.Silu, r=[key], w=["gate_s"])
            pcs.append(f_gs)

            def f_gp():
                ps, key = proj_fm(w2b, "w2b", 256)
                p.actf(gate_p[:], ps, AF.Silu, r=[key], w=["gate_p"])
            pcs.append(f_gp)
            for jp in range(2):
                def f_q(jp=jp):
                    ps, key = proj_fm(w2b, "w2b", 512 + jp * 256)
                    p.amul(qT[:, 2 * jp:2 * jp + 2, :], ps, 0.125, r=[key], w=["qT"])
                pcs.append(f_q)
            for jp in range(2):
                def f_ga(jp=jp):
                    ps, key = proj_fm(w2b, "w2b", 1024 + jp * 256)
                    p.actf(gate_a[:, 2 * jp:2 * jp + 2, :], ps, AF.Silu, r=[key], w=["gate_a"])
                pcs.append(f_ga)
            pcs.append(lambda: pool_slab(s))
            pcs.extend(attention_pairs(s, nb_lo))
            pass
            for mt in range(2):
                def f_glu(mt=mt):
                    b = proj_slot()
                    bk = "b%d" % b
                    for (hx, col) in ((0, mt * 128), (1, 256 + mt * 128)):
                        for kc in range(2):
                            p.mm(banks[b][:, hx * W:(hx + 1) * W], glub[:, kc, col:col + 128], yg[pb][:, kc, :], kc == 0, kc == 1,
                                 r=["yg%d" % pb, "glub"], w=[bk])
                    p.actf(sig[:], banks[b][:, W:2 * W], AF.Sigmoid, r=[bk, "psm"], w=["sig"], bias=psm[:, C_GB + 2 + mt:C_GB + 3 + mt])
                    p.stt("dve", glt[:], banks[b][:, 0:W], psm[:, C_GB + mt:C_GB + mt + 1], sig[:], ALU.add, ALU.mult,
                          r=[bk, "psm", "sig"], w=["glt"])
                    p.tt("dve", mix[:, mt, :], glt[:], gate_s[:, mt, :], ALU.mult, r=["glt", "gate_s"], w=["mix"])
                pcs.append(f_glu)
            for mp in range(4):
                def f_op(mp=mp):
                    b = proj_slot()
                    key = "b%d" % b
                    for t in range(2):
                        mt = 2 * mp + t
                        for kt in range(8):
                            p.mm(banks[b][:, t * W:(t + 1) * W], wob[:, kt, mt * 128:(mt + 1) * 128], mix[:, kt, :], kt == 0, kt == 7,
                                 r=["mix", "wob"], w=[key])
                    for t in range(2):
                        mt = 2 * mp + t
                        p.amul(yo2[:, mt, :], banks[b][:, t * W:(t + 1) * W], psm[:, C_GPOST + mt:C_GPOST + mt + 1],
                               r=[key, "psm"], w=["yo2"])
                    p.actf(sqb[:, 2 * mp:2 * mp + 2, :], bview(b, 2), AF.Square, r=[key], w=["sqb"])
                pcs.append(f_op)

            def f_post():
                b = proj_slot()
                bk = "b%d" % b
                for mt in range(8):
                    p.mm(banks[b][:, 0:W], onesb[:], sqb[:, mt, :], mt == 0, mt == 7, r=["sqb", "onesb"], w=[bk])
                p.actf(sd[:], banks[b][:, 0:W], AF.Ln, r=[bk], w=["sd"], scale=1.0 / D, bias=EPS)
                p.actf(rstd[:], sd[:], AF.Exp, r=["sd"], w=["rstd"], scale=-0.5)
                p.tt("dve", yo2[:], yo2[:], rstd[:].unsqueeze(1).to_broadcast([128, 8, W]), ALU.mult, r=["yo2", "rstd"], w=["yo2"])
                p.tt("dve", xs[xb][:], xs[xb][:], yo2[:], ALU.add, r=["yo2", xkey], w=[xkey])
                out_stores.append(lambda: p.dmas(out_v[:, :, c0:c1], xs[xb][:], "st_o%d" % xb, r=[xkey], w=["out%d" % s]))
            pcs.append(f_post)

            def f_tail():
                if s - 1 >= 0:
                    main_loads(s - 1)
                    norm_a(xs[(s - 1) % 2], "xs%d" % ((s - 1) % 2), sq=True)
            pcs.append(f_tail)
            return pcs

        EARLY = list(range(17))

        def main_early(l, src_v, srckeyf, thunks=()):
            lastb = (NS - 1) % 2
            p.dmas(xs[lastb][:], src_v[:, :, (NS - 1) * W:NS * W], "xs%d" % lastb, r=[srckeyf(NS - 1)], w=["xs%d" % lastb])
            main_loads(NS - 1)
            ssm_b_loads(NS - 1)
            norm_a(xs[lastb], "xs%d" % lastb)
            pcs = main_pieces(NS - 1, src_v, srckeyf)
            interleave([pcs[i_] for i_ in EARLY], thunks)
            return [pc for i_, pc in enumerate(pcs) if i_ not in EARLY]

        def phase_main(l, src_v, srckeyf, rest_first, deferred=()):
            deferred = list(deferred)
            nmain = DBG.get("main", NS)
            if nmain == 0:
                return
            pend = ssm_b(NS - 1)
            for s in range(NS - 1, NS - 1 - nmain, -1):
                fill = rest_first if s == NS - 1 else main_pieces(s, src_v, srckeyf)
                if s - 1 >= 0:
                    ssm_b_loads(s - 1)
                flush_stores()
                if s != NS - 1:
                    fill.pop(0)()
                if deferred:
                    fill.insert(1, deferred.pop(0))
                if s - 1 >= 0:
                    pend = ssm_b(s - 1, fillers=fill, after_s=pend)
                else:
                    if pend is not None:
                        pend()
                        pend = None
                    for f_ in fill:
                        f_()
            flush_stores()
            while deferred:
                deferred.pop(0)()

        prep_gains(0)
        for f_ in prep_weights_p1(0):
            f_()
        for l in range(L):
            first_src = (l == 0 and not from_out_first)
            src_v = xT_v if first_src else out_v
            srckeyf = (lambda s: "xin") if first_src else (lambda s: "out%d" % s)
            if DBG.get("stop") == "const":
                break
            prep_small(l)
            rec = Rec()
            prep_tables(l, 0, rec=rec, which="A")
            p1_prologue(l, src_v, srckeyf, thunks=rec.thunks(p))
            prep_tables(l, 0, which="B")
            if DBG.get("stop") == "prep":
                break
            phase1(l, src_v, srckeyf, deferred=prep_weights_main(l))
            rec = Rec()
            prep_tables(l, 1, rec=rec, which="A")
            rest_first = main_early(l, src_v, srckeyf, thunks=rec.thunks(p))
            prep_tables(l, 1, which="B")
            nxt = []
            if l + 1 < L:
                nxt = [lambda l=l: prep_gains(l + 1)] + prep_weights_p1(l + 1)
            phase_main(l, src_v, srckeyf, rest_first, deferred=nxt)

        p.emit(nc, es)
    return nc


def _consts():
    sp = np.arange(128)[:, None, None]
    rel = np.arange(3)[None, :, None]
    qi = np.arange(128)[None, None, :]
    dist = np.abs(qi - (rel - 1) * 128 - sp).astype(np.float32)
    valid = (dist <= 128).astype(np.float32)
    iota = np.broadcast_to(np.arange(W, dtype=np.float32)[None, :], (128, W)).copy()
    ratio = np.zeros((128, 2, 2, 8), np.float32)
    for ct in range(2):
        for row in range(128):
            w = (2, 4, 8, 16)[2 * ct + row // 64]
            for side in range(2):
                for c in range(8):
                    t = c if side == 0 else SEQ - 8 + c
                    cnt = min(t + w // 2, SEQ) - max(t - w // 2, 0)
                    ratio[row, ct, side, c] = w / cnt
    return (dist.reshape(128, 384), valid.reshape(128, 384), iota, ratio.reshape(128, 32), np.eye(128, dtype=np.float32))


def _prep_layer_inputs(inp, ls):
    L = len(ls)
    f = lambda a: np.ascontiguousarray(a, dtype=np.float32)
    w_in = inp["w_in"][ls]
    su, sg, pu, pg, q, k, v, ag = np.split(w_in, [256, 512, 768, 1024, 1536, 1664, 1792], axis=-1)
    k0, k1 = k[..., :64], k[..., 64:]
    w1 = f(np.concatenate([su, pu, k0, k0, k1, k1, v], axis=-1))
    w2 = f(np.concatenate([sg, pg, q, ag], axis=-1))
    wo = f(inp["w_out"][ls])
    gluw = f(inp["ssm_glu_w"][ls])
    pw = inp["pool_w"][ls]
    poolw = np.zeros((L, 128, 2, 128), np.float32)
    for ct in range(2):
        for hg in range(2):
            poolw[:, hg * 64:(hg + 1) * 64, ct, hg * 64:(hg + 1) * 64] = pw[:, 2 * ct + hg]
    poolw = poolw.reshape(L, 128, 256)

    a_re, a_im, ldt = inp["ssm_a_re"][ls], inp["ssm_a_im"][ls], inp["ssm_log_dt"][ls]
    b_re, b_im = inp["ssm_b_re"][ls], inp["ssm_b_im"][ls]
    c_re, c_im = inp["ssm_c_re"][ls], inp["ssm_c_im"][ls]

    psm = np.zeros((L, 128, NSM), np.float32)
    psm[:, :, C_GPRE:C_GPRE + 8] = inp["pre_norm_g"][ls].reshape(L, 8, 128).transpose(0, 2, 1)
    psm[:, :, C_GPOST:C_GPOST + 8] = inp["post_norm_g"][ls].reshape(L, 8, 128).transpose(0, 2, 1)
    psm[:, :, C_SD:C_SD + 2] = inp["ssm_d"][ls].reshape(L, 2, 128).transpose(0, 2, 1)
    psm[:, :, C_GB:C_GB + 4] = inp["ssm_glu_b"][ls].reshape(L, 4, 128).transpose(0, 2, 1)
    psm[:, :, C_PSC:C_PSC + 2] = inp["pool_scale"][ls].reshape(L, 2, 128).transpose(0, 2, 1)
    sink = inp["attn_sink"][ls]
    for j in range(4):
        psm[:, 0:64, C_SINK + j] = sink[:, 2 * j][:, None]
        psm[:, 64:128, C_SINK + j] = sink[:, 2 * j + 1][:, None]
    ldt_r = ldt.reshape(L, 2, 2, 8)
    psm[:, :, C_LDTR:C_LDTR + 4] = np.repeat(ldt_r.transpose(0, 3, 1, 2).reshape(L, 8, 4), 16, axis=1)
    ldt_s = ldt.reshape(L, 2, 2, 4, 2)
    psm[:, :, C_LDTS:C_LDTS + 16] = np.repeat(ldt_s.transpose(0, 4, 1, 2, 3).reshape(L, 2, 16), 64, axis=1)
    ars = a_re.reshape(L, 2, 2, 4, 2, 64)
    psm[:, :, C_ARS:C_ARS + 16] = ars.transpose(0, 4, 5, 1, 2, 3).reshape(L, 128, 16)
    ais = a_im.reshape(L, 2, 2, 4, 2, 64)
    psm[:, :, C_AIS:C_AIS + 16] = ais.transpose(0, 4, 5, 1, 2, 3).reshape(L, 128, 16)
    mask = np.zeros((128, 4, 2), np.float32)
    for g8 in range(8):
        mask[g8 * 16:(g8 + 1) * 16, g8 // 2, g8 % 2] = 1.0
    psm[:, :, C_MASK:C_MASK + 8] = mask.reshape(128, 8)[None]
    for pp_ in range(128):
        psm[:, pp_, C_MASK2 + (pp_ // 16) % 2] = 1.0
    psm[:, 96:128, C_MASK3] = 1.0
    for ct in range(2):
        psm[:, 0:64, C_INVW + ct] = 1.0 / (2, 4, 8, 16)[2 * ct]
        psm[:, 64:128, C_INVW + ct] = 1.0 / (2, 4, 8, 16)[2 * ct + 1]

    pch = np.zeros((L, 128, 4, 4, 64), np.float32)
    br = b_re.reshape(L, 2, 2, 8, 64, 16)
    bi = b_im.reshape(L, 2, 2, 8, 64, 16)
    pch[:, :, 0] = br.transpose(0, 3, 5, 1, 2, 4).reshape(L, 128, 4, 64)
    pch[:, :, 1] = bi.transpose(0, 3, 5, 1, 2, 4).reshape(L, 128, 4, 64)
    ar = a_re.reshape(L, 2, 2, 8, 64)
    ai = a_im.reshape(L, 2, 2, 8, 64)
    pch[:, :, 2] = np.repeat(ar.transpose(0, 3, 1, 2, 4).reshape(L, 8, 4, 64), 16, axis=1)
    pch[:, :, 3] = np.repeat(ai.transpose(0, 3, 1, 2, 4).reshape(L, 8, 4, 64), 16, axis=1)
    pch = pch.reshape(L, 128, 1024)

    cpad = np.zeros((L, 2, 2, 64, 2, 2, 4, 8, 16), np.float32)
    cr = c_re.reshape(L, 2, 2, 4, 2, 16, 64)
    ci = c_im.reshape(L, 2, 2, 4, 2, 16, 64)
    for gp in range(4):
        for gg in range(2):
            cpad[:, 0, gg, :, :, :, gp, 2 * gp + gg, :] = cr[:, :, :, gp, gg].transpose(0, 4, 1, 2, 3)
            cpad[:, 1, gg, :, :, :, gp, 2 * gp + gg, :] = ci[:, :, :, gp, gg].transpose(0, 4, 1, 2, 3)
    cpad = cpad.reshape(L, 2, 128, 2048)
    return dict(w1=w1, w2=w2, wo=wo, gluw=gluw, poolw=f(poolw), psmall=psm, pchan=f(pch), cpad=f(cpad))


_NC_CACHE = {}


def _get_nc(L, from_out_first=False):
    key = (L, from_out_first)
    if key not in _NC_CACHE:
        _NC_CACHE[key] = build(L, from_out_first)
    return _NC_CACHE[key]


FUSED = True
DBG = {}


def kernel(**inputs):
    x = np.asarray(inputs["x"], dtype=np.float32)
    B = x.shape[0]
    inp = {k: np.asarray(v, dtype=np.float32) for k, v in inputs.items()}
    cd, cv, cio, crat, cid = _consts()
    xT = [np.ascontiguousarray(x[b].T) for b in range(B)]
    if FUSED:
        groups = [list(range(DEPTH))]
    else:
        groups = [[l] for l in range(DEPTH)]
    for ls in groups:
        nc = _get_nc(len(ls))
        lw = _prep_layer_inputs(inp, ls)
        in_maps = []
        for b in range(B):
            m = dict(lw)
            m.update(xT=xT[b], cdist=cd, cvalid=cv, ciota=cio, cratio=crat, cident=cid)
            in_maps.append(m)
        res = run_bass_kernel_spmd(nc, in_maps, core_ids=list(range(B)))
        xT = [np.asarray(res.results[b]["out"]) for b in range(B)]
    return np.stack([xT[b].T for b in range(B)], axis=0).astype(np.float32)
```

```python
import math
from contextlib import ExitStack

import numpy as np
import concourse.bass as bass
import concourse.mybir as mybir
from concourse.bass_utils import run_bass_kernel_spmd

F32 = mybir.dt.float32
BF16 = mybir.dt.bfloat16
I32 = mybir.dt.int32
ALU = mybir.AluOpType
AF = mybir.ActivationFunctionType

D = 1024
SEQ = 4096
DEPTH = 4
W = 256
NS = SEQ // W
NSM = 93
TCH = 4
NCH = W // TCH
EPS = 1e-6
TWO_PI = 2.0 * math.pi
SC2 = TWO_PI * (1.0 - 2e-6)
SC1 = math.pi * (1.0 - 2e-6)

C_GPRE, C_GPOST, C_SD, C_GB, C_PSC, C_SINK, C_LDTR, C_LDTS, C_ARS, C_AIS, C_MASK, C_INVW = (
    0, 8, 16, 18, 22, 24, 28, 32, 48, 64, 80, 88)
C_MASK2, C_MASK3 = 90, 92


class Prog:
    ENGS = ["sp", "pe", "act", "dve", "pool"]
    SAME_SKIP = {"pe"}

    def __init__(self):
        self.ops = []
        self.lastw = {}
        self.readers = {}
        self.dmacnt = {}

    def add(self, eng, fn, r=(), w=(), dma=None, force=False):
        i = len(self.ops)
        deps = set()
        for k in r:
            if k in self.lastw:
                deps.add(self.lastw[k])
        for k in w:
            if k in self.lastw:
                deps.add(self.lastw[k])
            deps.update(self.readers.get(k, ()))
        for k in r:
            self.readers.setdefault(k, []).append(i)
        for k in w:
            self.lastw[k] = i
            self.readers[k] = []
        op = dict(eng=eng, fn=fn, deps=deps, dma=dma, needed=False, force=force)
        if dma is not None:
            self.dmacnt[dma] = self.dmacnt.get(dma, 0) + 16
            op["dcount"] = self.dmacnt[dma]
        self.ops.append(op)
        return i

    def pe(self, fn, r=(), w=()):
        return self.add("pe", fn, r, w)

    def act(self, fn, r=(), w=()):
        return self.add("act", fn, r, w)

    def dve(self, fn, r=(), w=()):
        return self.add("dve", fn, r, w)

    def pool(self, fn, r=(), w=()):
        return self.add("pool", fn, r, w)

    def dma(self, fn, key, r=(), w=()):
        return self.add("sp", fn, r, w, dma=key)


    def mm(self, out, lhsT, rhs, start, stop, r=(), w=(), force=False):
        return self.add("pe", lambda e: e.matmul(out, lhsT=lhsT, rhs=rhs, start=start, stop=stop), r, w, force=force)

    def actf(self, out, in_, func, r=(), w=(), scale=None, bias=None):
        kw = {}
        if scale is not None:
            kw["scale"] = scale
        if bias is not None:
            kw["bias"] = bias
        return self.add("act", lambda e: e.activation(out=out, in_=in_, func=func, **kw), r, w)

    def amul(self, out, in_, mul, r=(), w=()):
        return self.add("act", lambda e: e.mul(out, in_, mul), r, w)

    def acopy(self, out, in_, r=(), w=()):
        return self.add("act", lambda e: e.copy(out, in_), r, w)

    def tt(self, eng, out, in0, in1, op, r=(), w=()):
        return self.add(eng, lambda e: e.tensor_tensor(out=out, in0=in0, in1=in1, op=op), r, w)

    def ts(self, eng, out, in0, s1, s2, op0, op1=None, r=(), w=()):
        if op1 is None:
            return self.add(eng, lambda e: e.tensor_scalar(out=out, in0=in0, scalar1=s1, scalar2=None, op0=op0), r, w)
        return self.add(eng, lambda e: e.tensor_scalar(out=out, in0=in0, scalar1=s1, scalar2=s2, op0=op0, op1=op1), r, w)

    def stt(self, eng, out, in0, scalar, in1, op0, op1, r=(), w=()):
        return self.add(eng, lambda e: e.scalar_tensor_tensor(out=out, in0=in0, scalar=scalar, in1=in1, op0=op0, op1=op1), r, w)

    def scan(self, out, d0, d1, init, r=(), w=()):
        return self.add("dve", lambda e: e.tensor_tensor_scan(out=out, data0=d0, data1=d1, initial=init,
                                                              op0=ALU.mult, op1=ALU.add), r, w)

    def recip(self, out, in_, r=(), w=()):
        return self.add("dve", lambda e: e.reciprocal(out=out, in_=in_), r, w)

    def copy(self, eng, out, in_, r=(), w=()):
        return self.add(eng, lambda e: e.tensor_copy(out=out, in_=in_), r, w)

    def memset(self, eng, ap, val, w=()):
        return self.add(eng, lambda e: e.memset(ap, val), (), w)

    def dmas(self, out, in_, key, r=(), w=(), q="sp"):
        return self.add(q, lambda e: e.dma_start(out=out, in_=in_), r, w, dma=key)

    def _skip(self, op, dop):
        if op["force"]:
            return False
        return dop["dma"] is None and dop["eng"] == op["eng"] and op["eng"] in self.SAME_SKIP

    def emit(self, nc, es):
        ops = self.ops
        for op in ops:
            for d in op["deps"]:
                dop = ops[d]
                if dop["dma"] is None and not self._skip(op, dop):
                    dop["needed"] = True
        cnt = {e: 0 for e in self.ENGS}
        for op in ops:
            if op["dma"] is None and op["needed"]:
                cnt[op["eng"]] += 1
                op["sig"] = cnt[op["eng"]]
        csem = {e: es.enter_context(nc.semaphore("c_" + e)) for e in ["pe", "act", "dve", "pool"]}
        dsem = {k: es.enter_context(nc.semaphore("d_%d" % i)) for i, k in enumerate(self.dmacnt)}
        block = es.enter_context(nc.Block())
        regs = {"sp": block.sync, "pe": block.tensor, "act": block.scalar, "dve": block.vector,
                "pool": block.gpsimd}
        for eng in self.ENGS:
            def body(e, eng=eng):
                waited = {}
                for op in ops:
                    if op["eng"] != eng:
                        continue
                    for d in sorted(op["deps"]):
                        dop = ops[d]
                        if dop["dma"] is not None:
                            key, val, sem = ("d", dop["dma"]), dop["dcount"], dsem[dop["dma"]]
                        else:
                            if self._skip(op, dop):
                                continue
                            key, val, sem = ("c", dop["eng"]), dop["sig"], csem[dop["eng"]]
                        if waited.get(key, 0) >= val:
                            continue
                        e.wait_ge(sem, val)
                        waited[key] = val
                    ins = op["fn"](e)
                    if op["dma"] is not None:
                        ins.then_inc(dsem[op["dma"]], 16)
                    elif op["needed"]:
                        ins.then_inc(csem[eng], 1)
                if eng == "sp":
                    for k, v in self.dmacnt.items():
                        e.wait_ge(dsem[k], v)
            regs[eng](body)


class Rec:
    def __init__(self):
        self.calls = []

    def __getattr__(self, name):
        def f(*a, **k):
            self.calls.append((name, a, k))
        return f

    def thunks(self, prog):
        return [lambda n=n, a=a, k=k: getattr(prog, n)(*a, **k) for (n, a, k) in self.calls]


def build(L, from_out_first=False):
    nc = bass.Bass("TRN2", target_bir_lowering=False)
    dr = lambda name, shape, dt=F32, kind="ExternalInput": nc.dram_tensor(name, shape, dt, kind=kind).ap()
    xT = dr("xT", [D, SEQ])
    w1 = dr("w1", [L, D, 896])
    w2 = dr("w2", [L, D, 1536])
    wo = dr("wo", [L, D, D])
    gluw = dr("gluw", [L, 256, 512])
    poolw = dr("poolw", [L, 128, 256])
    psmall = dr("psmall", [L, 128, NSM])
    pchan = dr("pchan", [L, 128, 1024])
    cpad = dr("cpad", [L, 2, 128, 2048])
    cdist = dr("cdist", [128, 384])
    cvalid = dr("cvalid", [128, 384])
    ciota = dr("ciota", [128, W])
    cratio = dr("cratio", [128, 32])
    cident = dr("cident", [128, 128])
    out = dr("out", [D, SEQ], kind="ExternalOutput")
    skind = "ExternalOutput" if DBG.get("dump") else "Internal"
    susc = nc.dram_tensor("susc", [128, 2, SEQ], BF16, kind=skind).ap()
    ktsc = nc.dram_tensor("ktsc", [128, 2, SEQ], BF16, kind=skind).ap()
    pusc = nc.dram_tensor("pusc", [128, 2, SEQ + 16], F32, kind=skind).ap()
    vsc = nc.dram_tensor("vsc", [128, 32, 128], BF16, kind=skind).ap()
    yfsc = nc.dram_tensor("yfsc", [128, 2, SEQ], F32, kind=skind).ap()

    if DBG.get("dump"):
        dbg32 = nc.dram_tensor("dbg32", [128, 96], F32, kind="ExternalOutput").ap()
        dbgb = nc.dram_tensor("dbgb", [128, 5, 2048], BF16, kind="ExternalOutput").ap()
    xT_v = xT.rearrange("(kc p) t -> p kc t", p=128)
    out_v = out.rearrange("(kc p) t -> p kc t", p=128)

    es = ExitStack()
    with es:
        sb = lambda name, shape, dt=F32: es.enter_context(nc.sbuf_tensor(name, shape, dt))
        w1b = sb("w1b", [128, 8, 896], BF16)
        w2b = sb("w2b", [128, 8, 1536], BF16)
        wob = sb("wob", [128, 8, 1024], BF16)
        glub = sb("glub", [128, 2, 512], BF16)
        plwb = sb("plwb", [128, 2, 128], BF16)
        WS = sb("WS", [128, 2, 4, 2, 128], BF16)
        WS3 = sb("WS3", [128, 2, 4, 2, 128], BF16)
        WC = sb("WC", [128, 8, 4, 2, 128], BF16)
        Kt = sb("Kt", [128, 2, 4, 128], BF16)
        identb = sb("identb", [128, 128], BF16)
        stage = [sb("stage%d" % i, [128, 1024]) for i in range(2)]
        psm = sb("psm", [128, NSM])
        psm_g = sb("psm_g", [128, 16])
        Eb = sb("Eb", [128, 8, 384], BF16)
        onesb = sb("onesb", [128, 128], BF16)
        onesLR = sb("onesLR", [128, 2, 128], BF16)
        iota = sb("iota", [128, W])
        ratio = sb("ratio", [128, 2, 2, 8])
        zer = sb("zer", [128, 16])
        dmy = sb("dmy", [128, 2])
        rr = sb("rr", [128, 16])
        ff = sb("ff", [128, 16])
        dts = sb("dts", [128, 16])
        k16 = sb("k16", [128, 16], I32)
        u16 = sb("u16", [128, 16])
        off = sb("off", [128, 16])
        bo1 = sb("bo1", [128, 16])
        bo2 = sb("bo2", [128, 16])
        carry = sb("carry", [128, 16, 2])
        esink = sb("esink", [128, 4])
        dtc = sb("dtc", [128, 4])
        xs = [sb("xs%d" % i, [128, 8, W]) for i in range(2)]
        sqb = sb("sqb", [128, 8, W], BF16)
        hT = sb("hT", [128, 8, W], BF16)
        sd = sb("sd", [128, W])
        rstd = sb("rstd", [128, W])
        cA = sb("cA", [128, 8, NCH]); cK = sb("cK", [128, 8, NCH], I32)
        cSn = sb("cSn", [128, 8, NCH]); cCs = sb("cCs", [128, 8, NCH])
        cVr = sb("cVr", [128, 8, NCH]); cVi = sb("cVi", [128, 8, NCH])
        cGr = sb("cGr", [128, 8, NCH]); cGi = sb("cGi", [128, 8, NCH])
        Hre = sb("Hre", [128, 8, NCH + 2], BF16); Him = sb("Him", [128, 8, NCH + 2], BF16)
        ffT = sb("ffT", [128, 16]); rrT = sb("rrT", [128, 16]); offT = sb("offT", [128, 16]); ffp = sb("ffp", [128, 16])
        zsm = sb("zsm", [128, 5, 2, 8])
        carc = sb("carc", [128, 2, 8])
        su_s = [sb("su_s%d" % i, [128, 2, W], BF16) for i in range(2)]
        kt_s = sb("kt_s", [128, 2, W], BF16)
        pu_s = sb("pu_s", [128, 2, W])
        v_s = sb("v_s", [128, 2, 128], BF16)
        yf_s = [sb("yf_s%d" % i, [128, W]) for i in range(2)]
        su_l = [sb("su_l%d" % i, [128, 2, W], BF16) for i in range(2)]
        kt_l = sb("kt_l", [128, 2, 4 * 128], BF16)
        pu_l = sb("pu_l", [128, 2, W + 16])
        v_l = sb("v_l", [128, 4, 128], BF16)
        VLR = sb("VLR", [128, 4, 2, 2, 128], BF16)
        yf_l = [sb("yf_l%d" % i, [128, 2, W]) for i in range(2)]
        gate_s = sb("gate_s", [128, 2, W])
        gate_p = sb("gate_p", [128, 2, W])
        gate_a = sb("gate_a", [128, 4, W])
        qT = sb("qT", [128, 4, W], BF16)
        yg = [sb("yg%d" % i, [128, 2, W], BF16) for i in range(2)]
        yo2 = sb("yo2", [128, 8, W])
        pch = yo2[:, 0:4, :].rearrange("p a (b c) -> p a b c", b=4)
        ytmp = sb("ytmp", [128, W])
        mix = sb("mix", [128, 8, W], BF16)
        sig = sb("sig", [128, W])
        glt = sb("glt", [128, W])
        pA = sb("pA", [128, W + 16])
        pB = sb("pB", [128, W + 16])
        pC = sb("pC", [128, W + 16])
        pD = sb("pD", [128, W + 16])
        pmean = sb("pmean", [128, W])
        mixed = sb("mixed", [128, 2, W], BF16)
        pexp = [sb("pexp%d" % i, [128, 384], BF16) for i in range(2)]
        pT = [sb("pT%d" % i, [128, 384], BF16) for i in range(4)]
        dn = sb("dn", [128, 256])
        o1 = sb("o1", [128, 256])
        banks = [es.enter_context(nc.psum_tensor("bank%d" % i, [128, 512], F32)) for i in range(8)]

        def bview(b, n):
            return banks[b][:, 0:n * 256].rearrange("p (a c) -> p a c", a=n)

        if DBG.get('mem'):
            print('SBUF bytes remaining', nc.sbuf_bytes_remaining)
        p = Prog()
        p_real = p
        WKALL = ["wk%d" % i for i in range(8)]
        CARRYALL = ["carry%d" % i for i in range(16)]

        def pslot(n):
            st = {"i": 0}

            def nxt():
                st["i"] += 1
                return (st["i"] - 1) % n
            return nxt

        fl = lambda ap, pat: ap.rearrange(pat)

        p.memset("dve", onesb[:], 1.0, w=["onesb"])
        p.memset("dve", onesLR[:].rearrange("p a b -> p (a b)"), 0.0, w=["onesLR"])
        p.memset("dve", onesLR[:, 0, 0:64], 1.0, w=["onesLR"])
        p.memset("dve", onesLR[:, 1, 64:128], 1.0, w=["onesLR"])
        p.memset("dve", VLR[:].rearrange("p a b c d -> p (a b c d)"), 0.0, w=["VLR"])
        p.memset("dve", zer[:], 0.0, w=["zer"])
        p.memset("dve", v_l[:].rearrange("p a b -> p (a b)"), 0.0, w=["v_l"])
        p.memset("dve", kt_l[:].rearrange("p a b -> p (a b)"), 0.0, w=["kt_l"])
        p.dmas(iota[:], ciota, "c0", w=["iota"])
        p.dmas(stage[0][:, 0:128], cident, "stg0", w=["stage0", "stage0b"])
        p.copy("dve", identb[:], stage[0][:, 0:128], r=["stage0"], w=["identb"])
        p.dmas(ratio[:].rearrange("p a b c -> p (a b c)"), cratio, "c1", w=["ratio"])
        for ct in range(2):
            p.dmas(pusc[:, ct, 0:8], zer[:, 0:8], "c2", r=["zer"], w=["pusc_pad"])
            p.dmas(pusc[:, ct, SEQ + 8:SEQ + 16], zer[:, 0:8], "c2", r=["zer"], w=["pusc_pad"])
        p.dmas(stage[0][:, 0:384], cdist, "stg0", w=["stage0", "stage0b"])
        p.dmas(stage[1][:, 0:384], cvalid, "stg1", w=["stage1", "stage1b"])
        for h in range(8):
            slope = 2.0 ** (-(h + 1))
            p.actf(stage[0][:, 384:768], stage[0][:, 0:384], AF.Exp, r=["stage0"], w=["stage0b"], scale=-slope)
            p.tt("dve", Eb[:, h, :], stage[0][:, 384:768], stage[1][:, 0:384], ALU.mult,
                 r=["stage0b", "stage1"], w=["Eb"])

        stg_i = [0]

        def stage_load(src_ap, ncols):
            s = stg_i[0] % 2
            stg_i[0] += 1
            key = "stage%d" % s
            p.dmas(stage[s][:, 0:ncols], src_ap, "stg%d" % s, w=[key, key + "b"])
            return s, key

        def prep_small(l):
            p.dmas(psm[:], psmall[l], "psm", w=["psm"])
            p.actf(esink[:], psm[:, C_SINK:C_SINK + 4], AF.Exp, r=["psm"], w=["esink"])

        def prep_weights_p1(l):
            ch = []
            for kc in range(8):
                def f(kc=kc):
                    s, key = stage_load(w1[l, kc * 128:(kc + 1) * 128, :], 896)
                    p.amul(w1b[:, kc, :], stage[s][:, 0:896], psm_g[:, C_GPRE + kc:C_GPRE + kc + 1], r=[key, "psm_g"], w=["w1b"])
                ch.append(f)
            return ch

        def prep_weights_main(l):
            ch = []
            for kc in range(8):
                for hcol in range(2):
                    def f(kc=kc, hcol=hcol):
                        c0 = hcol * 768
                        s, key = stage_load(w2[l, kc * 128:(kc + 1) * 128, c0:c0 + 768], 768)
                        p.amul(w2b[:, kc, c0:c0 + 768], stage[s][:, 0:768], psm_g[:, C_GPRE + kc:C_GPRE + kc + 1],
                               r=[key, "psm_g"], w=["w2b"])
                    ch.append(f)
            for kc in range(8):
                def f(kc=kc):
                    s, key = stage_load(wo[l, kc * 128:(kc + 1) * 128, :], 1024)
                    p.acopy(wob[:, kc, :], stage[s][:, 0:1024], r=[key], w=["wob"])
                ch.append(f)
            for kc in range(2):
                def f(kc=kc):
                    s, key = stage_load(gluw[l, kc * 128:(kc + 1) * 128, :], 512)
                    p.acopy(glub[:, kc, :], stage[s][:, 0:512], r=[key], w=["glub"])
                ch.append(f)

            def f():
                s, key = stage_load(poolw[l], 256)
                p.acopy(plwb[:].rearrange("p a b -> p (a b)"), stage[s][:, 0:256], r=[key], w=["plwb"])
            ch.append(f)
            return ch

        def prep_gains(l):
            p.dmas(psm_g[:], psmall[l, :, 0:16], "psmg", w=["psm_g"])

        cbase = [yo2[:, 0:2, :].rearrange("p a b -> p (a b)").rearrange("p (a b) -> p a b", a=8),
                 pu_s[:].rearrange("p a b -> p (a b)").rearrange("p (a b) -> p a b", a=8)]
        cbase_key = ["yo2", "pu_s"]
        CB = [cA, cSn, cCs, cVr, cVi, cGr, cGi]
        CBK = ["cA", "cSn", "cCs", "cVr", "cVi", "cGr", "cGi"]
        GRALL = ["cGr"] + ["cGr_%d" % t_ for t_ in range(8)]
        GIALL = ["cGi"] + ["cGi_%d" % t_ for t_ in range(8)]
        ALLC = CBK + ["cK", "Hre", "Him"] + GRALL[1:] + GIALL[1:]

        def prep_tables(l, d, rec=None, which="AB"):
            p = rec if (rec is not None) else p_real
            fl2 = lambda t: t[:].rearrange("p a b -> p (a b)")
            sl = slice(8 * d, 8 * d + 8)
            if "A" in which:
                fl2 = lambda t: t[:].rearrange("p a b -> p (a b)")
                T = lambda i: fl2(CB[i // 2])[:, (i % 2) * 256:(i % 2) * 256 + 256]
                T3 = lambda i: T(i).rearrange("p (a b) -> p a b", a=4)
                H3 = lambda i: T3(i)[:, 2 * d:2 * d + 2, :]
                R = ALLC + ["yo2", "psm", "dtc"]
                Wk = ALLC
                p.dmas(yo2[:, 0:4, :].rearrange("p a b -> p (a b)"), pchan[l], "pch", w=["yo2"])
                BTre, BTim, AR, AI = (pch[:, 0, :, :], pch[:, 1, :, :], pch[:, 2, :, :], pch[:, 3, :, :])
                p.actf(dtc[:], psm[:, C_LDTR:C_LDTR + 4], AF.Exp, r=["psm"], w=["dtc"])
                dtb = dtc[:].unsqueeze(2).to_broadcast([128, 4, 64])
                kint = fl2(cK)[:, 0:256]
                p.tt("dve", T3(0), AR, dtb, ALU.mult, r=R, w=Wk)
                p.tt("dve", T3(1), AI, dtb, ALU.mult, r=R, w=Wk)
                p.actf(T(2), T(0), AF.Exp, r=R, w=Wk)
                p.ts("dve", kint, T(1), 1.0 / TWO_PI, None, ALU.mult, r=R, w=Wk)
                p.stt("dve", T(3), T(1), 1.0 / TWO_PI, kint, ALU.mult, ALU.subtract, r=R, w=Wk)
                p.actf(T(4), T(3), AF.Sin, r=R, w=Wk, scale=SC2)
                p.actf(T(5), T(3), AF.Sin, r=R, w=Wk, scale=SC1)
                p.actf(T(5), T(5), AF.Square, r=R, w=Wk)
                p.ts("dve", T(5), T(5), -2.0, 1.0, ALU.mult, ALU.add, r=R, w=Wk)
                p.tt("dve", T(6), T(2), T(5), ALU.mult, r=R, w=Wk)
                p.tt("dve", T(7), T(2), T(4), ALU.mult, r=R, w=Wk)
                p.ts("dve", T(8), T(6), -1.0, None, ALU.add, r=R, w=Wk)
                p.tt("dve", T3(0), AR, AR, ALU.mult, r=R, w=Wk)
                p.tt("dve", T3(1), AI, AI, ALU.mult, r=R, w=Wk)
                p.tt("dve", T(0), T(0), T(1), ALU.add, r=R, w=Wk)
                p.recip(T(0), T(0), r=R, w=Wk)
                p.tt("dve", T3(1), T3(8), AR, ALU.mult, r=R, w=Wk)
                p.tt("dve", T3(2), T3(7), AI, ALU.mult, r=R, w=Wk)
                p.tt("dve", T(1), T(1), T(2), ALU.add, r=R, w=Wk)
                p.tt("dve", T(1), T(1), T(0), ALU.mult, r=R, w=Wk)
                p.tt("dve", T3(2), T3(7), AR, ALU.mult, r=R, w=Wk)
                p.tt("dve", T3(3), T3(8), AI, ALU.mult, r=R, w=Wk)
                p.tt("dve", T(2), T(2), T(3), ALU.subtract, r=R, w=Wk)
                p.tt("dve", T(2), T(2), T(0), ALU.mult, r=R, w=Wk)
                ZR, ZI, LR, LI = 1, 2, 6, 7
                mk2 = psm[:, C_MASK2:C_MASK2 + 2].unsqueeze(1).unsqueeze(3).to_broadcast([128, 2, 2, 64])
                for k in range(4):
                    j = (3 - k) if d == 0 else k
                    p.tt("dve", H3(3), H3(ZR), BTre[:, 2 * d:2 * d + 2, :], ALU.mult, r=R, w=Wk)
                    p.tt("dve", H3(4), H3(ZI), BTim[:, 2 * d:2 * d + 2, :], ALU.mult, r=R, w=Wk)
                    p.tt("dve", H3(3), H3(3), H3(4), ALU.subtract, r=R, w=Wk)
                    p.tt("dve", H3(4), H3(ZR), BTim[:, 2 * d:2 * d + 2, :], ALU.mult, r=R, w=Wk)
                    p.tt("dve", H3(5), H3(ZI), BTre[:, 2 * d:2 * d + 2, :], ALU.mult, r=R, w=Wk)
                    p.tt("dve", H3(4), H3(4), H3(5), ALU.add, r=R, w=Wk)
                    for ri, src in ((0, 3), (1, 4)):
                        p.tt("dve", WS[:, :, j, ri, :].rearrange("p a (b c) -> p a b c", b=2),
                             H3(src).unsqueeze(2).to_broadcast([128, 2, 2, 64]), mk2, ALU.mult, r=R, w=["WS"])
                    if k < 3:
                        p.tt("dve", H3(3), H3(ZR), H3(LR), ALU.mult, r=R, w=Wk)
                        p.tt("dve", H3(4), H3(ZI), H3(LI), ALU.mult, r=R, w=Wk)
                        p.tt("dve", H3(5), H3(ZR), H3(LI), ALU.mult, r=R, w=Wk)
                        p.tt("dve", H3(9), H3(ZI), H3(LR), ALU.mult, r=R, w=Wk)
                        p.tt("dve", H3(ZR), H3(3), H3(4), ALU.subtract, r=R, w=Wk)
                        p.tt("dve", H3(ZI), H3(5), H3(9), ALU.add, r=R, w=Wk)
                p.ts("dve", WS3[64:128].rearrange("p a b c d -> p (a b c d)"), WS[64:128].rearrange("p a b c d -> p (a b c d)"),
                     psm[64:128, C_MASK3:C_MASK3 + 1], None, ALU.mult, r=["WS", "psm"], w=["WS3"])

                sl = slice(8 * d, 8 * d + 8)
                p.actf(dts[:], psm[:, C_LDTS:C_LDTS + 16], AF.Exp, r=["psm"], w=["dts"])
                p.tt("dve", u16[:], psm[:, C_ARS:C_ARS + 16], dts[:], ALU.mult, r=["psm", "dts"], w=["u16"])
                p.actf(rr[:], u16[:], AF.Exp, r=["u16"], w=["rr"])
                p.tt("dve", u16[:], psm[:, C_AIS:C_AIS + 16], dts[:], ALU.mult, r=["psm", "dts", "rr"], w=["u16"])
                p.ts("dve", k16[:], u16[:], 1.0 / TWO_PI, None, ALU.mult, r=["u16"], w=["k16"])
                p.stt("dve", ffp[:], u16[:], 1.0 / TWO_PI, k16[:], ALU.mult, ALU.subtract, r=["u16", "k16"], w=["ffp"])
                p.actf(off[:], ffp[:], AF.Sin, r=["ffp"], w=["off"], scale=SC2)
                p.actf(bo1[:], ffp[:], AF.Sin, r=["ffp"], w=["bo1"], scale=SC1)
                p.actf(bo1[:], bo1[:], AF.Square, r=["bo1"], w=["bo1"])
                p.ts("dve", bo1[:], bo1[:], -2.0, 1.0, ALU.mult, ALU.add, r=["bo1"], w=["bo1"])
                p.tt("dve", zsm[:, 1, 0, :], rr[:, sl], bo1[:, sl], ALU.mult, r=["rr", "bo1"], w=["zsm"])
                p.tt("dve", zsm[:, 1, 1, :], rr[:, sl], off[:, sl], ALU.mult, r=["rr", "off"], w=["zsm"])
                for k in range(1, 4):
                    a_r, a_i = zsm[:, k, 0, :], zsm[:, k, 1, :]
                    l_r, l_i = zsm[:, 1, 0, :], zsm[:, 1, 1, :]
                    p.tt("dve", bo2[:, 0:8], a_r, l_r, ALU.mult, r=["zsm"], w=["bo2"])
                    p.tt("dve", bo2[:, 8:16], a_i, l_i, ALU.mult, r=["zsm"], w=["bo2"])
                    p.tt("dve", zsm[:, k + 1, 0, :], bo2[:, 0:8], bo2[:, 8:16], ALU.subtract, r=["bo2"], w=["zsm"])
                    p.tt("dve", bo2[:, 0:8], a_r, l_i, ALU.mult, r=["zsm"], w=["bo2"])
                    p.tt("dve", bo2[:, 8:16], a_i, l_r, ALU.mult, r=["zsm"], w=["bo2"])
                    p.tt("dve", zsm[:, k + 1, 1, :], bo2[:, 0:8], bo2[:, 8:16], ALU.add, r=["bo2"], w=["zsm"])
                sgn = 1.0 if d == 0 else -1.0
                p.ts("dve", k16[:, sl], ffp[:, sl], sgn * TCH, None, ALU.mult, r=["ffp"], w=["k16"])
                p.stt("dve", ffT[:, sl], ffp[:, sl], sgn * TCH, k16[:, sl], ALU.mult, ALU.subtract, r=["ffp", "k16"], w=["ffT"])
                p.tt("dve", rrT[:, sl], rr[:, sl], rr[:, sl], ALU.mult, r=["rr"], w=["rrT"])
                p.tt("dve", rrT[:, sl], rrT[:, sl], rrT[:, sl], ALU.mult, r=["rrT"], w=["rrT"])


            p = p_real
            if "B" not in which:
                return
            p.dmas(stage[0][:, 0:1024], cpad[l, 0, :, d * 1024:(d + 1) * 1024], "stg0", w=["stage0", "stage0b"])
            p.dmas(stage[1][:, 0:1024], cpad[l, 1, :, d * 1024:(d + 1) * 1024], "stg1", w=["stage1", "stage1b"])
            c3 = lambda t: t[:, 0:1024].rearrange("p (a b) -> p a b", a=8)
            cre, cim = c3(stage[0]), c3(stage[1])
            t1 = yo2[:, 0:4, :].rearrange("p a b -> p (a b)").rearrange("p (a b) -> p a b", a=8)
            t2 = yo2[:, 4:8, :].rearrange("p a b -> p (a b)").rearrange("p (a b) -> p a b", a=8)
            RW = ["yo2", "stage0", "stage1", "zsm"]
            Ta0 = cGr[:].bitcast(BF16)
            Tb0 = cGi[:].bitcast(BF16)
            assert tuple(Ta0.shape) == (128, 8, 128), Ta0.shape
            p.acopy(Ta0, cre, r=["stage0"] + ALLC, w=GRALL)
            p.amul(Tb0, cim, -1.0, r=["stage1"] + ALLC, w=GIALL)

            x2 = lambda t, o: fl2(t)[:, o:o + 256].rearrange("p (a b) -> p a b", a=2)
            Xs = [x2(Hre, 0), x2(Hre, 256), x2(Him, 0), x2(Him, 256)]
            XK = ["Xs0", "Xs1", "Xs2", "Xs3"]
            p.memset("dve", fl2(Hre), 0.0, w=["Hre", "Xs0", "Xs1"])
            p.memset("dve", fl2(Him), 0.0, w=["Him", "Xs2", "Xs3"])
            def kg_transpose(ct, tau, gp):
                j = (3 - tau) if d == 0 else tau
                tb = proj_slot()
                tkey = "b%d" % tb
                if gp < 3:
                    rows, ncol, cbase, src = slice(32 * gp, 32 * gp + 32), 32, 32 * gp, WS
                else:
                    rows, ncol, cbase, src = slice(64, 128), 64, 64, WS3
                for ri in range(2):
                    p.mm(banks[tb][:, ri * 64:ri * 64 + ncol], src[rows, ct, j, ri, :], identb[rows, cbase:cbase + ncol],
                         True, True, r=["WS", "WS3", "identb"], w=[tkey])
                p.acopy(Xs[gp][:, :, cbase:cbase + ncol],
                        banks[tb][:, 0:128].rearrange("p (a b) -> p a b", a=2)[:, :, 0:ncol], r=[tkey], w=[XK[gp]])

            steps = [(ct, tau, gp) for ct in range(2) for tau in range(4) for gp in range(4)]
            kg_transpose(*steps[0])
            for si, (ct, tau, gp) in enumerate(steps):
                if si + 1 < len(steps):
                    kg_transpose(*steps[si + 1])
                kb_ = 2
                tile = ct * 4 + gp
                p.mm(banks[kb_][:, 0:128], Xs[gp][:, 0, :], Ta0[:, tile, :], gp == 0, False, r=[XK[gp], "cGr"], w=["b2"])
                p.mm(banks[kb_][:, 0:128], Xs[gp][:, 1, :], Tb0[:, tile, :], False, gp == 3, r=[XK[gp], "cGi"], w=["b2"])
                if gp == 3:
                    p.acopy(Kt[:, ct, tau, :], banks[kb_][:, 0:128], r=["b2"], w=["Kt"])
            for i in range(4):
                k = (i + 1) if d == 0 else (TCH - i)
                zr = zsm[:, k, 0, :].unsqueeze(2).to_broadcast([128, 8, 128])
                zi = zsm[:, k, 1, :].unsqueeze(2).to_broadcast([128, 8, 128])
                p.tt("dve", t1, cre, zr, ALU.mult, r=RW, w=["yo2"])
                p.tt("dve", t2, cim, zi, ALU.mult, r=RW, w=["yo2"])
                p.tt("dve", WC[:, :, i, 0, :], t1, t2, ALU.subtract, r=RW, w=["WC"])
                p.tt("dve", t1, cre, zi, ALU.mult, r=RW, w=["yo2"])
                p.tt("dve", t2, cim, zr, ALU.mult, r=RW, w=["yo2"])
                p.stt("dve", WC[:, :, i, 1, :], t1, -1.0, t2, ALU.mult, ALU.subtract, r=RW, w=["WC"])
            p.memset("dve", fl2(Hre), 0.0, w=["Hre", "Xs0", "Xs1"])
            p.memset("dve", fl2(Him), 0.0, w=["Him", "Xs2", "Xs3"])
            p.memset("dve", carc[:].rearrange("p a b -> p (a b)"), 0.0, w=["carc"])
            p.tt("dve", cbase[d], iota[:, 0:NCH].unsqueeze(1).to_broadcast([128, 8, NCH]),
                 ffT[:, sl].unsqueeze(2).to_broadcast([128, 8, NCH]), ALU.mult, r=["iota", "ffT", cbase_key[d]], w=[cbase_key[d]])

        proj_slot = pslot(2)

        def norm_sq(xs_t, xkey):
            p.actf(sqb[:].rearrange("p a b -> p (a b)"), xs_t[:].rearrange("p a b -> p (a b)"), AF.Square,
                   r=[xkey], w=["sqb"])

        def norm_a(xs_t, xkey, sq=True):
            if sq:
                norm_sq(xs_t, xkey)
            b = proj_slot()
            bk = "b%d" % b
            for kc in range(8):
                p.mm(banks[b][:, 0:W], onesb[:], sqb[:, kc, :], kc == 0, kc == 7, r=["sqb", "onesb"], w=[bk])
            p.actf(sd[:], banks[b][:, 0:W], AF.Ln, r=[bk], w=["sd"], scale=1.0 / D, bias=EPS)
            p.actf(rstd[:], sd[:], AF.Exp, r=["sd"], w=["rstd"], scale=-0.5)

        def norm_b(xs_t, xkey, eng="dve"):
            p.tt(eng, hT[:], xs_t[:], rstd[:].unsqueeze(1).to_broadcast([128, 8, W]), ALU.mult,
                 r=[xkey, "rstd"], w=["hT"])

        def norm_slab(xs_t, xkey):
            norm_a(xs_t, xkey)
            norm_b(xs_t, xkey)

        def proj_fm(wb, wkey, col0, nt=2):
            b = proj_slot()
            key = "b%d" % b
            for t in range(nt):
                for kc in range(8):
                    p.mm(banks[b][:, t * W:(t + 1) * W], wb[:, kc, col0 + t * 128:col0 + (t + 1) * 128], hT[:, kc, :],
                         kc == 0, kc == 7, r=["hT", wkey], w=[key])
            return bview(b, nt), key

        tab_slot = pslot(2)
        pp_slot = pslot(2)
        brbi_slot = pslot(2)

        def ssm_slab3(direction, s, su_t, sukey, after_ct, fillers=(), after_s=None):
            d = direction
            fillers = list(fillers)
            sl = slice(8 * d, 8 * d + 8)
            rev = (d == 1)
            nops = 30
            per_op = max(1, -(-len(fillers) // nops)) if fillers else 0
            fl2 = lambda t: t[:].rearrange("p a b -> p (a b)")

            def fill(n=1):
                for _ in range(n * per_op):
                    if fillers:
                        fillers.pop(0)()

            for tile in range(8):
                ct, gp = tile // 4, tile % 4
                if gp < 3:
                    rows, src = slice(32 * gp, 32 * gp + 32), WS
                else:
                    rows, src = slice(64, 128), WS3
                if gp not in DBG.get("gps", (0, 1, 2, 3)):
                    continue
                for ri in range(2):
                    for j in range(TCH):
                        p.mm(banks[2 + ri][:, tile * NCH:(tile + 1) * NCH], src[rows, ct, j, ri, :], su_t[rows, ct, j:W:TCH],
                             j == 0, j == TCH - 1, r=[sukey, "WS", "WS3"], w=["b%d" % (2 + ri), "rgser"],
                             force=(ri == 0 and j == 0))
            Sre = banks[2][:, 0:8 * NCH].rearrange("p (a b) -> p a b", a=8)
            Sim = banks[3][:, 0:8 * NCH].rearrange("p (a b) -> p a b", a=8)
            c0 = float(s * NCH)
            p.ts("dve", k16[:, sl], ffT[:, sl], c0, None, ALU.mult, r=["ffT"], w=["k16"])
            p.stt("dve", offT[:, sl], ffT[:, sl], c0, k16[:, sl], ALU.mult, ALU.subtract, r=["ffT", "k16"], w=["offT"])
            bc_t = lambda t: t[:, sl].unsqueeze(2).to_broadcast([128, 8, NCH])
            io_b = iota[:, 0:NCH].unsqueeze(1).to_broadcast([128, 8, NCH])
            bk_ = cbase_key[d]
            p.tt("dve", cK[:], cbase[d], bc_t(offT), ALU.add, r=[bk_, "offT"], w=["cK"])
            p.tt("dve", cA[:], cbase[d], bc_t(offT), ALU.add, r=[bk_, "offT", "cA"], w=["cA"])
            p.tt("dve", cA[:], cA[:], cK[:], ALU.subtract, r=["cA", "cK"], w=["cA"])
            p.actf(fl2(cSn), fl2(cA), AF.Sin, r=["cA"], w=["cSn"], scale=SC2)
            p.actf(fl2(cCs), fl2(cA), AF.Sin, r=["cA"], w=["cCs"], scale=SC1)
            p.actf(fl2(cCs), fl2(cCs), AF.Square, r=["cCs"], w=["cCs"])
            p.actf(fl2(cCs), fl2(cCs), AF.Identity, r=["cCs"], w=["cCs"], scale=-2.0, bias=1.0)
            if after_s is not None:
                after_s()
            fill(4)
            p.tt("dve", cVr[:], Sre, cCs[:], ALU.mult, r=["b2", "cCs"], w=["cVr"])
            p.tt("dve", cA[:], Sim, cSn[:], ALU.mult, r=["b3", "cSn", "cA"], w=["cA"])
            fill()
            p.tt("dve", cVi[:], Sim, cCs[:], ALU.mult, r=["b3", "cCs"], w=["cVi"])
            p.tt("dve", cGr[:], Sre, cSn[:], ALU.mult, r=["b2", "cSn"], w=GRALL)
            fill()
            p.tt("dve", cVr[:], cVr[:], cA[:], ALU.add, r=["cVr", "cA"], w=["cVr"])
            p.tt("dve", cVi[:], cVi[:], cGr[:], ALU.subtract, r=["cVi"] + GRALL, w=["cVi"])
            fill()
            fw = (lambda ap: ap[:, ::-1]) if rev else (lambda ap: ap)
            for tile in range(8):
                st = 8 * d + tile
                rb = rrT[:, st:st + 1].to_broadcast([128, NCH])
                p.scan(fw(cGr[:, tile, :]), rb, fw(cVr[:, tile, :]), carc[:, 0, tile:tile + 1], r=["cVr", "rrT", "carc"],
                       w=["cGr_%d" % tile])
                p.scan(fw(cGi[:, tile, :]), rb, fw(cVi[:, tile, :]), carc[:, 1, tile:tile + 1], r=["cVi", "rrT", "carc"],
                       w=["cGi_%d" % tile])
                if tile % 2 == 1:
                    fill()
            lastc = 0 if rev else NCH - 1
            GRK = ["cGr_%d" % t_ for t_ in range(8)]
            GIK = ["cGi_%d" % t_ for t_ in range(8)]
            p.acopy(carc[:, 0, :], cGr[:, :, lastc], r=GRK, w=["carc"])
            p.acopy(carc[:, 1, :], cGi[:, :, lastc], r=GIK, w=["carc"])
            hcols = slice(1, NCH + 1)
            if not rev:
                hdst, hsrc, hrhs = 0, NCH, slice(0, NCH)
            else:
                hdst, hsrc, hrhs = NCH + 1, 1, slice(2, NCH + 2)
            p.copy("dve", Hre[:, :, hdst], Hre[:, :, hsrc], r=["Hre"], w=["Hre"])
            p.copy("dve", Him[:, :, hdst], Him[:, :, hsrc], r=["Him"], w=["Him"])
            fill()
            p.tt("dve", cVr[:], cGr[:], cCs[:], ALU.mult, r=GRK + ["cCs", "cVr"], w=["cVr"])
            p.tt("dve", cVi[:], cGi[:], cSn[:], ALU.mult, r=GIK + ["cSn", "cVi"], w=["cVi"])
            fill()
            p.tt("dve", Hre[:, :, hcols], cVr[:], cVi[:], ALU.subtract, r=["cVr", "cVi", "Hre"], w=["Hre"])
            p.tt("dve", cVr[:], cGr[:], cSn[:], ALU.mult, r=GRK + ["cSn", "cVr", "Hre"], w=["cVr"])
            fill()
            p.tt("dve", cVi[:], cGi[:], cCs[:], ALU.mult, r=GIK + ["cCs", "cVi", "Hre"], w=["cVi"])
            fill()
            p.tt("dve", Him[:, :, hcols], cVr[:], cVi[:], ALU.add, r=["cVr", "cVi", "Him"], w=["Him"])
            while fillers:
                fillers.pop(0)()

            def part_b():
              for ct in range(2):
                  ykey = "b%d" % (4 + ct)
                  for i in range(TCH):
                      yo_ = banks[4 + ct][:, i:W:TCH]
                      js = list(range(0, i + 1)) if not rev else list(range(i, TCH))
                      nmm = len(js) + 8
                      n = 0
                      for j in js:
                          tau = abs(i - j)
                          p.mm(yo_, Kt[:, ct, tau, :], su_t[:, ct, j:W:TCH], n == 0, n == nmm - 1, r=[sukey, "Kt"], w=[ykey])
                          n += 1
                      for gp in range(4):
                          tile = ct * 4 + gp
                          p.mm(yo_, WC[:, tile, i, 0, :], Hre[:, tile, hrhs], n == 0, n == nmm - 1, r=["WC", "Hre"], w=[ykey])
                          n += 1
                          p.mm(yo_, WC[:, tile, i, 1, :], Him[:, tile, hrhs], n == 0, n == nmm - 1, r=["WC", "Him"], w=[ykey])
                          n += 1
                  after_ct(ct, banks[4 + ct][:, 0:W], ykey)
            return part_b

        def p1_pieces(s, src_v, srckeyf):
            c0, c1 = s * W, (s + 1) * W
            xb = s % 2
            xkey = "xs%d" % xb
            sus, suk = su_s[s % 2], "su_s%d" % (s % 2)
            pcs = []

            def f_norm():
                norm_b(xs[xb], xkey, eng=DBG.get("p1_norm_eng", "pool"))
            pcs.append(f_norm)

            def f_sq_next():
                if s + 1 < NS:
                    norm_sq(xs[(s + 1) % 2], "xs%d" % ((s + 1) % 2))
            pcs.append(f_sq_next)

            def f_su():
                ps, key = proj_fm(w1b, "w1b", 0)
                p.acopy(sus[:], ps, r=[key], w=[suk])
                p.dmas(susc[:, :, c0:c1], sus[:], "st_su%d" % (s % 2), r=[suk], w=["susc"])
            pcs.append(f_su)

            def f_pu():
                ps, key = proj_fm(w1b, "w1b", 256)
                p.acopy(pu_s[:], ps, r=[key], w=["pu_s"])
                p.dmas(pusc[:, :, 8 + c0:8 + c1], pu_s[:], "st_pu", r=["pu_s"], w=["pusc"])
            pcs.append(f_pu)

            def f_kt():
                ps, key = proj_fm(w1b, "w1b", 512)
                p.acopy(kt_s[:], ps, r=[key], w=["kt_s"])
                p.dmas(ktsc[:, :, c0:c1], kt_s[:], "st_kt", r=["kt_s"], w=["ktsc"])
            pcs.append(f_kt)

            def f_v():
                b = proj_slot()
                key = "b%d" % b
                for blk in range(2):
                    for kc in range(8):
                        p.mm(banks[b][:, blk * 128:(blk + 1) * 128], hT[:, kc, blk * 128:(blk + 1) * 128], w1b[:, kc, 768:896],
                             kc == 0, kc == 7, r=["hT", "w1b"], w=[key])
                p.acopy(v_s[:], banks[b][:, 0:256].rearrange("p (a c) -> p a c", a=2), r=[key], w=["v_s"])
                p.dmas(vsc[:, 2 * s:2 * s + 2, :], v_s[:], "st_v", r=["v_s"], w=["vsc"])
            pcs.append(f_v)

            def f_next_norm():
                if s + 1 < NS:
                    nb = (s + 1) % 2
                    norm_a(xs[nb], "xs%d" % nb, sq=False)
                p.actf(dmy[:, 0:1], zer[:, 0:1], AF.Sin, r=["zer"], w=["dmy"])
                if s + 2 < NS:
                    p.dmas(xs[xb][:], src_v[:, :, c1 + W:c1 + 2 * W], "xs%d" % xb, r=[srckeyf(s + 2)], w=["xs%d" % xb])
            pcs.append(f_next_norm)
            return pcs

        def interleave(pieces, thunks):
            thunks = list(thunks)
            per = -(-len(thunks) // max(len(pieces), 1))
            for pc in pieces:
                pc()
                for _ in range(per):
                    if thunks:
                        thunks.pop(0)()
            while thunks:
                thunks.pop(0)()

        def p1_prologue(l, src_v, srckeyf, thunks=()):
            def f0():
                p.dmas(xs[0][:], src_v[:, :, 0:W], "xs0", r=[srckeyf(0)], w=["xs0"])
                p.dmas(xs[1][:], src_v[:, :, W:2 * W], "xs1", r=[srckeyf(1)], w=["xs1"])
                norm_a(xs[0], "xs0")
            interleave([f0] + p1_pieces(0, src_v, srckeyf), thunks)

        def phase1(l, src_v, srckeyf, deferred=()):
            deferred = list(deferred)
            pend = None
            for s in range(DBG.get("p1", NS)):
                c0, c1 = s * W, (s + 1) * W
                nxt_p = p1_pieces(s + 1, src_v, srckeyf) if s + 1 < NS else []
                if nxt_p:
                    nxt_p.pop(0)()
                fill = []
                for _ in range(2):
                    if deferred:
                        fill.append(deferred.pop(0))
                fill = nxt_p + fill

                def after_f(ct, ps, key, c0=c0, c1=c1):
                    p.acopy(yf_s[ct][:], ps, r=[key], w=["yf_s%d" % ct])
                    p.dmas(yfsc[:, ct, c0:c1], yf_s[ct][:], "st_yf%d" % ct, r=["yf_s%d" % ct], w=["yfsc"])
                pend = ssm_slab3(0, s, su_s[s % 2], "su_s%d" % (s % 2), after_f, fillers=fill, after_s=pend)
            if pend is not None:
                pend()
            while deferred:
                deferred.pop(0)()

        s_slot = pslot(2)
        od_slot = pslot(2)
        od_bank = lambda: 0 + proj_slot()
        pt_slot = pslot(4)
        pe_slot = pslot(2)

        def attention_pairs(s, nb_lo):
            v3 = lambda ap: ap.rearrange("p (a b) -> p a b", a=2)
            pairs = [(nloc, jp, jj) for nloc in range(2) for jp in range(2) for jj in range(2)]
            state = {}

            def stage1(nloc, jp, jj):
                n = 2 * s + nloc
                kbs = [kb for kb in (n - 1, n, n + 1) if 0 <= kb < 32]
                rel0 = kbs[0] - (n - 1)
                nk = len(kbs)
                qc0 = nloc * 128
                j = jp * 2 + jj
                kv = j // 2
                pts = []
                for hh in range(2):
                    h = 2 * j + hh
                    ss = s_slot()
                    skey = "b%d" % (6 + ss)
                    s_ps = banks[6 + ss]
                    rows = slice(64 * hh, 64 * hh + 64)
                    for ki_, kb in enumerate(kbs):
                        kl = kb - nb_lo
                        p.mm(s_ps[:, ki_ * 128:(ki_ + 1) * 128], kt_l[rows, kv, kl * 128:(kl + 1) * 128],
                             qT[rows, j, qc0:qc0 + 128], True, True, r=["kt_l", "qT"], w=[skey])
                    pe_i = pe_slot()
                    pe_, pk = pexp[pe_i], "pexp%d" % pe_i
                    p.actf(pe_[:, 0:nk * 128], s_ps[:, 0:nk * 128], AF.Exp, r=[skey], w=[pk])
                    pi_ = pt_slot()
                    pt_, tk = pT[pi_], "pT%d" % pi_
                    p.tt("dve", pt_[:, 0:nk * 128], pe_[:, 0:nk * 128], Eb[:, h, rel0 * 128:(rel0 + nk) * 128], ALU.mult,
                         r=[pk, "Eb"], w=[tk])
                    pts.append((pt_, tk))
                state[(nloc, jp, jj)] = (pts, kbs)

            def stage2(nloc, jp, jj):
                pts, kbs = state.pop((nloc, jp, jj))
                nk = len(kbs)
                qc0 = nloc * 128
                j = jp * 2 + jj
                kv = j // 2
                if jj == 0:
                    state[("od", nloc, jp)] = proj_slot()
                ob = state[("od", nloc, jp)]
                okey = "b%d" % ob
                odv = banks[ob][:].rearrange("p (j t c) -> p j t c", j=2, t=2)
                for t in range(2):
                    for hh in range(2):
                        pt_, tk = pts[hh]
                        for ki_, kb in enumerate(kbs):
                            kl = kb - nb_lo
                            first = (hh == 0 and ki_ == 0)
                            last = (hh == 1 and ki_ == nk - 1)
                            lhs = VLR[:, kl, kv, hh, :] if t == 0 else onesLR[:, hh, :]
                            p.mm(odv[:, jj, t, :], lhs, pt_[:, ki_ * 128:(ki_ + 1) * 128], first, last,
                                 r=[tk, "VLR", "onesLR"], w=[okey])
                if jj == 1:
                    state.pop(("od", nloc, jp))
                    for jj_ in range(2):
                        p.actf(dn[:, jj_ * 128:(jj_ + 1) * 128], odv[:, jj_, 1, :], AF.Ln, r=[okey, "esink"], w=["dn"],
                               bias=esink[:, 2 * jp + jj_:2 * jp + jj_ + 1])
                    p.actf(dn[:], dn[:], AF.Exp, r=["dn"], w=["dn"], scale=-1.0)
                    p.tt("dve", v3(o1[:]), odv[:, :, 0, :], v3(dn[:]), ALU.mult, r=[okey, "dn"], w=["o1"])
                    p.tt("dve", mix[:, 4 + 2 * jp:6 + 2 * jp, qc0:qc0 + 128], v3(o1[:]), gate_a[:, 2 * jp:2 * jp + 2, qc0:qc0 + 128],
                         ALU.mult, r=["o1", "gate_a"], w=["mix"])

            pcs = [lambda: stage1(*pairs[0])]
            for k in range(1, len(pairs)):
                pcs.append(lambda k=k: (stage1(*pairs[k]), stage2(*pairs[k - 1])))
            pcs.append(lambda: stage2(*pairs[-1]))
            return pcs

        def pool_slab(s):
            first, last = (s == 0), (s == NS - 1)
            n = W + 16
            lo, hi = slice(0, 64), slice(64, 128)
            for ct in range(2):
                Xc = pu_l[:, ct, :]
                p.tt("pool", pA[:, 1:n], Xc[:, 0:n - 1], Xc[:, 1:n], ALU.add, r=["pu_l"], w=["pA"])
                if ct == 0:
                    p.tt("pool", pB[hi, 2:n - 1], pA[hi, 1:n - 2], pA[hi, 3:n], ALU.add, r=["pA"], w=["pB"])
                    srcs = ((lo, pA, "pA"), (hi, pB, "pB"))
                else:
                    p.tt("pool", pB[:, 2:n - 1], pA[:, 1:n - 2], pA[:, 3:n], ALU.add, r=["pA"], w=["pB"])
                    p.tt("pool", pC[:, 4:n - 3], pB[:, 2:n - 5], pB[:, 6:n - 1], ALU.add, r=["pB"], w=["pC"])
                    p.tt("pool", pD[hi, 8:n - 7], pC[hi, 4:n - 11], pC[hi, 12:n - 3], ALU.add, r=["pC"], w=["pD"])
                    srcs = ((lo, pC, "pC"), (hi, pD, "pD"))
                for (rows, buf, bk) in srcs:
                    p.ts("pool", pmean[rows, :], buf[rows, 8:8 + W], psm[rows, C_INVW + ct:C_INVW + ct + 1], None, ALU.mult,
                         r=[bk, "psm"], w=["pmean"])
                if first:
                    p.tt("pool", pmean[:, 0:8], pmean[:, 0:8], ratio[:, ct, 0, :], ALU.mult, r=["pmean", "ratio"], w=["pmean"])
                if last:
                    p.tt("pool", pmean[:, W - 8:W], pmean[:, W - 8:W], ratio[:, ct, 1, :], ALU.mult, r=["pmean", "ratio"], w=["pmean"])
                p.tt("pool", mixed[:, ct, :], pmean[:], Xc[:, 8:8 + W], ALU.subtract, r=["pmean", "pu_l"], w=["mixed"])
                b = proj_slot()
                key = "b%d" % b
                p.mm(banks[b][:, 0:W], plwb[:, ct, :], mixed[:, ct, :], True, True, r=["mixed", "plwb"], w=[key])
                p.stt("dve", mix[:, 2 + ct, :], banks[b][:, 0:W], psm[:, C_PSC + ct:C_PSC + ct + 1], gate_p[:, ct, :], ALU.mult, ALU.mult,
                      r=[key, "psm", "gate_p"], w=["mix"])

        def ssm_b_loads(s):
            c0, c1 = s * W, (s + 1) * W
            pb = s % 2
            p.dmas(su_l[pb][:], susc[:, :, c0:c1], "ld_su%d" % pb, r=["susc"], w=["su_l%d" % pb])
            p.dmas(yf_l[pb][:], yfsc[:, :, c0:c1], "ld_yf%d" % pb, r=["yfsc"], w=["yf_l%d" % pb])

        def ssm_b(s, fillers=(), after_s=None):
            pb = s % 2

            def after_b(ct, ps, key):
                p.tt("dve", ytmp[:], ps, yf_l[pb][:, ct, :], ALU.add, r=[key, "yf_l%d" % pb], w=["ytmp"])
                p.stt("dve", ytmp[:], su_l[pb][:, ct, :], psm[:, C_SD + ct:C_SD + ct + 1], ytmp[:], ALU.mult, ALU.add,
                      r=["su_l%d" % pb, "psm", "ytmp"], w=["ytmp"])
                p.actf(yg[pb][:, ct, :], ytmp[:], AF.Gelu_apprx_tanh, r=["ytmp"], w=["yg%d" % pb])
            return ssm_slab3(1, s, su_l[pb], "su_l%d" % pb, after_b, fillers=fillers, after_s=after_s)

        out_stores = []

        def flush_stores():
            while out_stores:
                out_stores.pop(0)()

        def main_loads(s):
            c0 = s * W
            nb_lo = 2 * s - 1
            p.dmas(pu_l[:], pusc[:, :, c0:c0 + W + 16], "ld_pu", r=["pusc", "pusc_pad"], w=["pu_l"])
            blo, bhi = max(nb_lo, 0), min(nb_lo + 4, 32)
            p.dmas(kt_l[:, :, (blo - nb_lo) * 128:(bhi - nb_lo) * 128], ktsc[:, :, blo * 128:bhi * 128], "ld_kt",
                   r=["ktsc"], w=["kt_l"])
            p.dmas(v_l[:, blo - nb_lo:bhi - nb_lo, :], vsc[:, blo:bhi, :], "ld_v", r=["vsc"], w=["v_l"])

        def main_pieces(s, src_v, srckeyf):
            c0, c1 = s * W, (s + 1) * W
            xb = s % 2
            xkey = "xs%d" % xb
            nb_lo = 2 * s - 1
            pb = s % 2
            pcs = []

            def f_loads():
                if s - 1 >= 0:
                    nb = (s - 1) % 2
                    p.dmas(xs[nb][:], src_v[:, :, c0 - W:c0], "xs%d" % nb, r=[srckeyf(s - 1)], w=["xs%d" % nb])
                for kv in range(2):
                    for hh in range(2):
                        p.copy("pool", VLR[:, :, kv, hh, 64 * hh:64 * hh + 64], v_l[:, :, 64 * kv:64 * kv + 64],
                               r=["v_l"], w=["VLR"])
                norm_b(xs[xb], xkey)
            pcs.append(f_loads)


            def f_gs():
                ps, key = proj_fm(w2b, "w2b", 0)
                p.actf(gate_s[:], ps, AF.Silu, r=[key], w=["gate_s"])
            pcs.append(f_gs)

            def f_gp():
                ps, key = proj_fm(w2b, "w2b", 256)
                p.actf(gate_p[:], ps, AF.Silu, r=[key], w=["gate_p"])
            pcs.append(f_gp)
            for jp in range(2):
                def f_q(jp=jp):
                    ps, key = proj_fm(w2b, "w2b", 512 + jp * 256)
                    p.amul(qT[:, 2 * jp:2 * jp + 2, :], ps, 0.125, r=[key], w=["qT"])
                pcs.append(f_q)
            for jp in range(2):
                def f_ga(jp=jp):
                    ps, key = proj_fm(w2b, "w2b", 1024 + jp * 256)
                    p.actf(gate_a[:, 2 * jp:2 * jp + 2, :], ps, AF.Silu, r=[key], w=["gate_a"])
                pcs.append(f_ga)
            pcs.append(lambda: pool_slab(s))
            pcs.extend(attention_pairs(s, nb_lo))
            pass
            for mt in range(2):
                def f_glu(mt=mt):
                    b = proj_slot()
                    bk = "b%d" % b
                    for (hx, col) in ((0, mt * 128), (1, 256 + mt * 128)):
                        for kc in range(2):
                            p.mm(banks[b][:, hx * W:(hx + 1) * W], glub[:, kc, col:col + 128], yg[pb][:, kc, :], kc == 0, kc == 1,
                                 r=["yg%d" % pb, "glub"], w=[bk])
                    p.actf(sig[:], banks[b][:, W:2 * W], AF.Sigmoid, r=[bk, "psm"], w=["sig"], bias=psm[:, C_GB + 2 + mt:C_GB + 3 + mt])
                    p.stt("dve", glt[:], banks[b][:, 0:W], psm[:, C_GB + mt:C_GB + mt + 1], sig[:], ALU.add, ALU.mult,
                          r=[bk, "psm", "sig"], w=["glt"])
                    p.tt("dve", mix[:, mt, :], glt[:], gate_s[:, mt, :], ALU.mult, r=["glt", "gate_s"], w=["mix"])
                pcs.append(f_glu)
            for mp in range(4):
                def f_op(mp=mp):
                    b = proj_slot()
                    key = "b%d" % b
                    for t in range(2):
                        mt = 2 * mp + t
                        for kt in range(8):
                            p.mm(banks[b][:, t * W:(t + 1) * W], wob[:, kt, mt * 128:(mt + 1) * 128], mix[:, kt, :], kt == 0, kt == 7,
                                 r=["mix", "wob"], w=[key])
                    for t in range(2):
                        mt = 2 * mp + t
                        p.amul(yo2[:, mt, :], banks[b][:, t * W:(t + 1) * W], psm[:, C_GPOST + mt:C_GPOST + mt + 1],
                               r=[key, "psm"], w=["yo2"])
                    p.actf(sqb[:, 2 * mp:2 * mp + 2, :], bview(b, 2), AF.Square, r=[key], w=["sqb"])
                pcs.append(f_op)

            def f_post():
                b = proj_slot()
                bk = "b%d" % b
                for mt in range(8):
                    p.mm(banks[b][:, 0:W], onesb[:], sqb[:, mt, :], mt == 0, mt == 7, r=["sqb", "onesb"], w=[bk])
                p.actf(sd[:], banks[b][:, 0:W], AF.Ln, r=[bk], w=["sd"], scale=1.0 / D, bias=EPS)
                p.actf(rstd[:], sd[:], AF.Exp, r=["sd"], w=["rstd"], scale=-0.5)
                p.tt("dve", yo2[:], yo2[:], rstd[:].unsqueeze(1).to_broadcast([128, 8, W]), ALU.mult, r=["yo2", "rstd"], w=["yo2"])
                p.tt("dve", xs[xb][:], xs[xb][:], yo2[:], ALU.add, r=["yo2", xkey], w=[xkey])
                out_stores.append(lambda: p.dmas(out_v[:, :, c0:c1], xs[xb][:], "st_o%d" % xb, r=[xkey], w=["out%d" % s]))
            pcs.append(f_post)

            def f_tail():
                if s - 1 >= 0:
                    main_loads(s - 1)
                    norm_a(xs[(s - 1) % 2], "xs%d" % ((s - 1) % 2), sq=True)
            pcs.append(f_tail)
            return pcs

        EARLY = list(range(17))

        def main_early(l, src_v, srckeyf, thunks=()):
            lastb = (NS - 1) % 2
            p.dmas(xs[lastb][:], src_v[:, :, (NS - 1) * W:NS * W], "xs%d" % lastb, r=[srckeyf(NS - 1)], w=["xs%d" % lastb])
            main_loads(NS - 1)
            ssm_b_loads(NS - 1)
            norm_a(xs[lastb], "xs%d" % lastb)
            pcs = main_pieces(NS - 1, src_v, srckeyf)
            interleave([pcs[i_] for i_ in EARLY], thunks)
            return [pc for i_, pc in enumerate(pcs) if i_ not in EARLY]

        def phase_main(l, src_v, srckeyf, rest_first, deferred=()):
            deferred = list(deferred)
            nmain = DBG.get("main", NS)
            if nmain == 0:
                return
            pend = ssm_b(NS - 1)
            for s in range(NS - 1, NS - 1 - nmain, -1):
                fill = rest_first if s == NS - 1 else main_pieces(s, src_v, srckeyf)
                if s - 1 >= 0:
                    ssm_b_loads(s - 1)
                flush_stores()
                if s != NS - 1:
                    fill.pop(0)()
                if deferred:
                    fill.insert(1, deferred.pop(0))
                if s - 1 >= 0:
                    pend = ssm_b(s - 1, fillers=fill, after_s=pend)
                else:
                    if pend is not None:
                        pend()
                        pend = None
                    for f_ in fill:
                        f_()
            flush_stores()
            while deferred:
                deferred.pop(0)()

        prep_gains(0)
        for f_ in prep_weights_p1(0):
            f_()
        for l in range(L):
            first_src = (l == 0 and not from_out_first)
            src_v = xT_v if first_src else out_v
            srckeyf = (lambda s: "xin") if first_src else (lambda s: "out%d" % s)
            if DBG.get("stop") == "const":
                break
            prep_small(l)
            rec = Rec()
            prep_tables(l, 0, rec=rec, which="A")
            p1_prologue(l, src_v, srckeyf, thunks=rec.thunks(p))
            prep_tables(l, 0, which="B")
            if DBG.get("stop") == "prep":
                break
            phase1(l, src_v, srckeyf, deferred=prep_weights_main(l))
            rec = Rec()
            prep_tables(l, 1, rec=rec, which="A")
            rest_first = main_early(l, src_v, srckeyf, thunks=rec.thunks(p))
            prep_tables(l, 1, which="B")
            nxt = []
            if l + 1 < L:
                nxt = [lambda l=l: prep_gains(l + 1)] + prep_weights_p1(l + 1)
            phase_main(l, src_v, srckeyf, rest_first, deferred=nxt)

        p.emit(nc, es)
    return nc


def _consts():
    sp = np.arange(128)[:, None, None]
    rel = np.arange(3)[None, :, None]
    qi = np.arange(128)[None, None, :]
    dist = np.abs(qi - (rel - 1) * 128 - sp).astype(np.float32)
    valid = (dist <= 128).astype(np.float32)
    iota = np.broadcast_to(np.arange(W, dtype=np.float32)[None, :], (128, W)).copy()
    ratio = np.zeros((128, 2, 2, 8), np.float32)
    for ct in range(2):
        for row in range(128):
            w = (2, 4, 8, 16)[2 * ct + row // 64]
            for side in range(2):
                for c in range(8):
                    t = c if side == 0 else SEQ - 8 + c
                    cnt = min(t + w // 2, SEQ) - max(t - w // 2, 0)
                    ratio[row, ct, side, c] = w / cnt
    return (dist.reshape(128, 384), valid.reshape(128, 384), iota, ratio.reshape(128, 32), np.eye(128, dtype=np.float32))


def _prep_layer_inputs(inp, ls):
    L = len(ls)
    f = lambda a: np.ascontiguousarray(a, dtype=np.float32)
    w_in = inp["w_in"][ls]
    su, sg, pu, pg, q, k, v, ag = np.split(w_in, [256, 512, 768, 1024, 1536, 1664, 1792], axis=-1)
    k0, k1 = k[..., :64], k[..., 64:]
    w1 = f(np.concatenate([su, pu, k0, k0, k1, k1, v], axis=-1))
    w2 = f(np.concatenate([sg, pg, q, ag], axis=-1))
    wo = f(inp["w_out"][ls])
    gluw = f(inp["ssm_glu_w"][ls])
    pw = inp["pool_w"][ls]
    poolw = np.zeros((L, 128, 2, 128), np.float32)
    for ct in range(2):
        for hg in range(2):
            poolw[:, hg * 64:(hg + 1) * 64, ct, hg * 64:(hg + 1) * 64] = pw[:, 2 * ct + hg]
    poolw = poolw.reshape(L, 128, 256)

    a_re, a_im, ldt = inp["ssm_a_re"][ls], inp["ssm_a_im"][ls], inp["ssm_log_dt"][ls]
    b_re, b_im = inp["ssm_b_re"][ls], inp["ssm_b_im"][ls]
    c_re, c_im = inp["ssm_c_re"][ls], inp["ssm_c_im"][ls]

    psm = np.zeros((L, 128, NSM), np.float32)
    psm[:, :, C_GPRE:C_GPRE + 8] = inp["pre_norm_g"][ls].reshape(L, 8, 128).transpose(0, 2, 1)
    psm[:, :, C_GPOST:C_GPOST + 8] = inp["post_norm_g"][ls].reshape(L, 8, 128).transpose(0, 2, 1)
    psm[:, :, C_SD:C_SD + 2] = inp["ssm_d"][ls].reshape(L, 2, 128).transpose(0, 2, 1)
    psm[:, :, C_GB:C_GB + 4] = inp["ssm_glu_b"][ls].reshape(L, 4, 128).transpose(0, 2, 1)
    psm[:, :, C_PSC:C_PSC + 2] = inp["pool_scale"][ls].reshape(L, 2, 128).transpose(0, 2, 1)
    sink = inp["attn_sink"][ls]
    for j in range(4):
        psm[:, 0:64, C_SINK + j] = sink[:, 2 * j][:, None]
        psm[:, 64:128, C_SINK + j] = sink[:, 2 * j + 1][:, None]
    ldt_r = ldt.reshape(L, 2, 2, 8)
    psm[:, :, C_LDTR:C_LDTR + 4] = np.repeat(ldt_r.transpose(0, 3, 1, 2).reshape(L, 8, 4), 16, axis=1)
    ldt_s = ldt.reshape(L, 2, 2, 4, 2)
    psm[:, :, C_LDTS:C_LDTS + 16] = np.repeat(ldt_s.transpose(0, 4, 1, 2, 3).reshape(L, 2, 16), 64, axis=1)
    ars = a_re.reshape(L, 2, 2, 4, 2, 64)
    psm[:, :, C_ARS:C_ARS + 16] = ars.transpose(0, 4, 5, 1, 2, 3).reshape(L, 128, 16)
    ais = a_im.reshape(L, 2, 2, 4, 2, 64)
    psm[:, :, C_AIS:C_AIS + 16] = ais.transpose(0, 4, 5, 1, 2, 3).reshape(L, 128, 16)
    mask = np.zeros((128, 4, 2), np.float32)
    for g8 in range(8):
        mask[g8 * 16:(g8 + 1) * 16, g8 // 2, g8 % 2] = 1.0
    psm[:, :, C_MASK:C_MASK + 8] = mask.reshape(128, 8)[None]
    for pp_ in range(128):
        psm[:, pp_, C_MASK2 + (pp_ // 16) % 2] = 1.0
    psm[:, 96:128, C_MASK3] = 1.0
    for ct in range(2):
        psm[:, 0:64, C_INVW + ct] = 1.0 / (2, 4, 8, 16)[2 * ct]
        psm[:, 64:128, C_INVW + ct] = 1.0 / (2, 4, 8, 16)[2 * ct + 1]

    pch = np.zeros((L, 128, 4, 4, 64), np.float32)
    br = b_re.reshape(L, 2, 2, 8, 64, 16)
    bi = b_im.reshape(L, 2, 2, 8, 64, 16)
    pch[:, :, 0] = br.transpose(0, 3, 5, 1, 2, 4).reshape(L, 128, 4, 64)
    pch[:, :, 1] = bi.transpose(0, 3, 5, 1, 2, 4).reshape(L, 128, 4, 64)
    ar = a_re.reshape(L, 2, 2, 8, 64)
    ai = a_im.reshape(L, 2, 2, 8, 64)
    pch[:, :, 2] = np.repeat(ar.transpose(0, 3, 1, 2, 4).reshape(L, 8, 4, 64), 16, axis=1)
    pch[:, :, 3] = np.repeat(ai.transpose(0, 3, 1, 2, 4).reshape(L, 8, 4, 64), 16, axis=1)
    pch = pch.reshape(L, 128, 1024)

    cpad = np.zeros((L, 2, 2, 64, 2, 2, 4, 8, 16), np.float32)
    cr = c_re.reshape(L, 2, 2, 4, 2, 16, 64)
    ci = c_im.reshape(L, 2, 2, 4, 2, 16, 64)
    for gp in range(4):
        for gg in range(2):
            cpad[:, 0, gg, :, :, :, gp, 2 * gp + gg, :] = cr[:, :, :, gp, gg].transpose(0, 4, 1, 2, 3)
            cpad[:, 1, gg, :, :, :, gp, 2 * gp + gg, :] = ci[:, :, :, gp, gg].transpose(0, 4, 1, 2, 3)
    cpad = cpad.reshape(L, 2, 128, 2048)
    return dict(w1=w1, w2=w2, wo=wo, gluw=gluw, poolw=f(poolw), psmall=psm, pchan=f(pch), cpad=f(cpad))


_NC_CACHE = {}


def _get_nc(L, from_out_first=False):
    key = (L, from_out_first)
    if key not in _NC_CACHE:
        _NC_CACHE[key] = build(L, from_out_first)
    return _NC_CACHE[key]


FUSED = True
DBG = {}


def kernel(**inputs):
    x = np.asarray(inputs["x"], dtype=np.float32)
    B = x.shape[0]
    inp = {k: np.asarray(v, dtype=np.float32) for k, v in inputs.items()}
    cd, cv, cio, crat, cid = _consts()
    xT = [np.ascontiguousarray(x[b].T) for b in range(B)]
    if FUSED:
        groups = [list(range(DEPTH))]
    else:
        groups = [[l] for l in range(DEPTH)]
    for ls in groups:
        nc = _get_nc(len(ls))
        lw = _prep_layer_inputs(inp, ls)
        in_maps = []
        for b in range(B):
            m = dict(lw)
            m.update(xT=xT[b], cdist=cd, cvalid=cv, ciota=cio, cratio=crat, cident=cid)
            in_maps.append(m)
        res = run_bass_kernel_spmd(nc, in_maps, core_ids=list(range(B)))
        xT = [np.asarray(res.results[b]["out"]) for b in range(B)]
    return np.stack([xT[b].T for b in range(B)], axis=0).astype(np.float32)
```
